# Optimizing a Trainium2 kernel written in Bass

```python
import jax, jax.numpy as jnp
from jax import lax
import numpy as np

D_MODEL = 1024
BATCH = 4
SEQ = 4096
DEPTH = 1

MLA_HEADS = 8
MLA_Q_RANK = 256
MLA_KV_RANK = 128
MLA_NOPE_DIM = 64
MLA_ROPE_DIM = 32
MLA_QK_DIM = MLA_NOPE_DIM + MLA_ROPE_DIM
MLA_V_DIM = 64
MLA_WIDTH = MLA_HEADS * MLA_V_DIM
DSA_HEADS = 8
DSA_KV_HEADS = 2
DSA_HEAD_DIM = 64
DSA_WIDTH = DSA_HEADS * DSA_HEAD_DIM
DSA_ROT_DIM = DSA_HEAD_DIM // 4
IDX_HEADS = 8
IDX_DIM = 32
IDX_ROT_DIM = IDX_DIM // 4
INDEX_TOPK_MAX = 256
ROPE_THETA = 500000.0
Q_BLOCK = 128
NORM_EPS = 1e-6

IN_SPLITS = (
    MLA_Q_RANK, MLA_KV_RANK, MLA_ROPE_DIM, MLA_WIDTH,
    DSA_HEADS * DSA_HEAD_DIM, DSA_KV_HEADS * DSA_HEAD_DIM,
    DSA_KV_HEADS * DSA_HEAD_DIM, DSA_WIDTH,
    IDX_HEADS * IDX_DIM, IDX_DIM, IDX_HEADS,
    D_MODEL, D_MODEL,
)
D_IN = sum(IN_SPLITS)

kernel_name = 'hybrid_mla_dsa_gated_block'


def rms_norm(x, gain):
    xf = x.astype(jnp.float32)
    y = xf * lax.rsqrt(jnp.mean(xf * xf, axis=-1, keepdims=True) + NORM_EPS)
    return (y * gain.astype(jnp.float32)).astype(x.dtype)


def rope_angles(positions, rot_dim):
    inv_freq = ROPE_THETA ** (-jnp.arange(0, rot_dim, 2, dtype=jnp.float32) / rot_dim)
    ang = positions.astype(jnp.float32)[..., None] * inv_freq
    return jnp.cos(ang), jnp.sin(ang)


def apply_rope(x, cos, sin):
    xf = x.astype(jnp.float32)
    half = xf.shape[-1] // 2
    x1, x2 = xf[..., :half], xf[..., half:]
    c, s = cos[:, :, None, :], sin[:, :, None, :]
    return jnp.concatenate([x1 * c - x2 * s, x2 * c + x1 * s], axis=-1).astype(x.dtype)


def partial_rope(x, cos, sin, rot_dim):
    return jnp.concatenate([apply_rope(x[..., :rot_dim], cos, sin), x[..., rot_dim:]], axis=-1)


def to_blocks(a):
    b, s = a.shape[0], a.shape[1]
    a = a.reshape((b, s // Q_BLOCK, Q_BLOCK) + a.shape[2:])
    return jnp.moveaxis(a, 1, 0)


def from_blocks(a):
    a = jnp.moveaxis(a, 0, 1)
    return a.reshape((a.shape[0], a.shape[1] * a.shape[2]) + a.shape[3:])


def mla_attention(q, k, v):
    s_len = q.shape[1]
    scale = MLA_QK_DIM ** -0.5
    key_pos = jnp.arange(s_len, dtype=jnp.int32)
    starts = jnp.arange(s_len // Q_BLOCK, dtype=jnp.int32) * Q_BLOCK

    def one_block(args):
        qb, start = args
        qpos = start + jnp.arange(Q_BLOCK, dtype=jnp.int32)
        sc = jnp.einsum('bqhd,bkhd->bhqk', qb, k).astype(jnp.float32) * scale
        causal = key_pos[None, :] <= qpos[:, None]
        sc = jnp.where(causal[None, None], sc, -jnp.inf)
        p = jax.nn.softmax(sc, axis=-1).astype(v.dtype)
        return jnp.einsum('bhqk,bkhd->bqhd', p, v)

    return from_blocks(lax.map(one_block, (to_blocks(q), starts)))


def dsa_attention(q, k, v, q_idx, k_idx, w_idx, k_top):
    bsz, s_len, n_heads, hd = q.shape
    n_kv = k.shape[2]
    grp = n_heads // n_kv
    scale = hd ** -0.5
    key_pos = jnp.arange(s_len, dtype=jnp.int32)
    starts = jnp.arange(s_len // Q_BLOCK, dtype=jnp.int32) * Q_BLOCK
    gather = jax.vmap(lambda kb, ib: kb[ib])

    def one_block(args):
        qb, qib, wb, start = args
        qpos = start + jnp.arange(Q_BLOCK, dtype=jnp.int32)
        rel = jax.nn.relu(jnp.einsum('bqhd,bsd->bqhs', qib, k_idx).astype(jnp.float32))
        score = jnp.einsum('bqhs,bqh->bqs', rel, wb.astype(jnp.float32))
        causal = key_pos[None, None, :] <= qpos[None, :, None]
        score = jnp.where(causal, score, -jnp.inf)
        _, sel = lax.top_k(score, k_top)
        valid = sel <= qpos[None, :, None]
        kg = gather(k, sel)
        vg = gather(v, sel)
        qg = qb.reshape(bsz, Q_BLOCK, n_kv, grp, hd)
        sc = jnp.einsum('bqkgd,bqjkd->bqkgj', qg, kg).astype(jnp.float32) * scale
        sc = jnp.where(valid[:, :, None, None, :], sc, -jnp.inf)
        p = jax.nn.softmax(sc, axis=-1).astype(v.dtype)
        o = jnp.einsum('bqkgj,bqjkd->bqkgd', p, vg)
        return o.reshape(bsz, Q_BLOCK, n_heads, hd)

    out = lax.map(one_block, (to_blocks(q), to_blocks(q_idx), to_blocks(w_idx), starts))
    return from_blocks(out)


def setup_inputs(seed: int = 0) -> dict:
    key = jax.random.key(seed)
    ks = jax.random.split(key, 20)
    f32 = jnp.float32

    def w(k, shape, fan_in):
        return jax.random.normal(k, shape, f32) * fan_in ** -0.5

    def g(k, n):
        return 1.0 + 0.01 * jax.random.normal(k, (DEPTH, n), f32)

    x = jax.random.normal(ks[0], (BATCH, SEQ, D_MODEL), f32)
    positions = jnp.broadcast_to(jnp.arange(SEQ, dtype=jnp.int32)[None, :], (BATCH, SEQ))
    return {
        'x': x,
        'positions': positions,
        'norm_gain': g(ks[1], D_MODEL),
        'w_in': w(ks[2], (DEPTH, D_MODEL, D_IN), D_MODEL),
        'b_merge': 0.01 * jax.random.normal(ks[3], (DEPTH, 2, D_MODEL), f32),
        'mla_q_norm': g(ks[4], MLA_Q_RANK),
        'mla_w_uq': w(ks[5], (DEPTH, MLA_Q_RANK, MLA_HEADS * MLA_QK_DIM), MLA_Q_RANK),
        'mla_kv_norm': g(ks[6], MLA_KV_RANK),
        'mla_w_ukv': w(ks[7], (DEPTH, MLA_KV_RANK, MLA_HEADS * (MLA_NOPE_DIM + MLA_V_DIM)), MLA_KV_RANK),
        'mla_q_gain': g(ks[8], MLA_QK_DIM),
        'mla_k_gain': g(ks[9], MLA_QK_DIM),
        'dsa_q_gain': g(ks[10], DSA_HEAD_DIM),
        'dsa_k_gain': g(ks[11], DSA_HEAD_DIM),
        'w_branch_mla': w(ks[12], (DEPTH, MLA_WIDTH, D_MODEL), MLA_WIDTH),
        'w_branch_dsa': w(ks[13], (DEPTH, DSA_WIDTH, D_MODEL), DSA_WIDTH),
        'w_out': w(ks[14], (DEPTH, D_MODEL, D_MODEL), D_MODEL),
    }


def reference(x, positions, norm_gain, w_in, b_merge, mla_q_norm, mla_w_uq, mla_kv_norm, mla_w_ukv,
              mla_q_gain, mla_k_gain, dsa_q_gain, dsa_k_gain, w_branch_mla, w_branch_dsa, w_out):
    bsz, s_len, _ = x.shape
    k_top = min(INDEX_TOPK_MAX, s_len // 4)
    offsets = [int(o) for o in np.cumsum(IN_SPLITS)[:-1]]
    cos_m, sin_m = rope_angles(positions, MLA_ROPE_DIM)
    cos_d, sin_d = rope_angles(positions, DSA_ROT_DIM)
    cos_i, sin_i = rope_angles(positions, IDX_ROT_DIM)

    for l in range(DEPTH):
        h = rms_norm(x, norm_gain[l])
        proj = h @ w_in[l]
        (c_q, c_kv, k_pe, gate_a, q_b, k_b, v_b, gate_b,
         q_i, k_i, w_i, m_a, m_b) = jnp.split(proj, offsets, axis=-1)

        q_a = (rms_norm(c_q, mla_q_norm[l]) @ mla_w_uq[l]).reshape(bsz, s_len, MLA_HEADS, MLA_QK_DIM)
        kv_a = (rms_norm(c_kv, mla_kv_norm[l]) @ mla_w_ukv[l]).reshape(
            bsz, s_len, MLA_HEADS, MLA_NOPE_DIM + MLA_V_DIM)
        k_nope, v_a = kv_a[..., :MLA_NOPE_DIM], kv_a[..., MLA_NOPE_DIM:]
        k_rope = jnp.broadcast_to(k_pe[:, :, None, :], (bsz, s_len, MLA_HEADS, MLA_ROPE_DIM))
        k_a = jnp.concatenate([k_nope, k_rope], axis=-1)
        q_a = rms_norm(q_a, mla_q_gain[l])
        k_a = rms_norm(k_a, mla_k_gain[l])
        q_a = jnp.concatenate([q_a[..., :MLA_NOPE_DIM], apply_rope(q_a[..., MLA_NOPE_DIM:], cos_m, sin_m)], axis=-1)
        k_a = jnp.concatenate([k_a[..., :MLA_NOPE_DIM], apply_rope(k_a[..., MLA_NOPE_DIM:], cos_m, sin_m)], axis=-1)
        o_a = mla_attention(q_a, k_a, v_a).reshape(bsz, s_len, MLA_WIDTH) * jax.nn.silu(gate_a)

        q_b = rms_norm(q_b.reshape(bsz, s_len, DSA_HEADS, DSA_HEAD_DIM), dsa_q_gain[l])
        k_b = rms_norm(k_b.reshape(bsz, s_len, DSA_KV_HEADS, DSA_HEAD_DIM), dsa_k_gain[l])
        v_b = v_b.reshape(bsz, s_len, DSA_KV_HEADS, DSA_HEAD_DIM)
        q_b = partial_rope(q_b, cos_d, sin_d, DSA_ROT_DIM)
        k_b = partial_rope(k_b, cos_d, sin_d, DSA_ROT_DIM)
        q_i = partial_rope(q_i.reshape(bsz, s_len, IDX_HEADS, IDX_DIM), cos_i, sin_i, IDX_ROT_DIM)
        k_i = partial_rope(k_i[:, :, None, :], cos_i, sin_i, IDX_ROT_DIM)[:, :, 0, :]
        w_i = w_i * (IDX_HEADS ** -0.5 * IDX_DIM ** -0.5)
        o_b = dsa_attention(q_b, k_b, v_b, q_i, k_i, w_i, k_top).reshape(bsz, s_len, DSA_WIDTH)
        o_b = o_b * jax.nn.silu(gate_b)

        merged = (jax.nn.sigmoid(m_a + b_merge[l, 0]) * (o_a @ w_branch_mla[l])
                  + jax.nn.sigmoid(m_b + b_merge[l, 1]) * (o_b @ w_branch_dsa[l]))
        x = x + merged @ w_out[l]
    return x
```

```python
import os
from contextlib import ExitStack

import numpy as np
import concourse.bass as bass
import concourse.mybir as mybir
from concourse.bass_utils import run_bass_kernel_spmd

F32 = mybir.dt.float32
BF16 = mybir.dt.bfloat16
I32 = mybir.dt.int32
ALU = mybir.AluOpType
AF = mybir.ActivationFunctionType
AX = mybir.AxisListType

THETA = 500000.0
EPS = 1e-6
BIG = 1.0e30
MAGIC = 12582912.0
TWO_PI = 6.283185307179586

CQ = (0, 256); CKV = (256, 384); KPE = (384, 416); GA = (416, 928); QB = (928, 1440)
KB = (1440, 1568); VB = (1568, 1696); GB = (1696, 2208); QI = (2208, 2464); KI = (2464, 2496)
WI = (2496, 2504); MA = (2504, 3528); MB = (3528, 4552)


class Op:
    __slots__ = ("eng", "fn", "deps", "odeps", "idx", "dma", "need", "ms", "dur", "region", "seq", "dval", "succ", "npred", "prio", "fin")

    def __init__(self, eng, fn, dma, dur, region, seq):
        self.eng = eng
        self.fn = fn
        self.deps = set()
        self.odeps = set()
        self.idx = 0
        self.dma = dma
        self.need = False
        self.ms = 0
        self.dur = dur
        self.region = region
        self.seq = seq
        self.dval = 0


class Prog:
    ENGS = ["pe", "act", "dve", "pool", "sp"]

    def __init__(self, nc, schedule=True):
        self.nc = nc
        self.ops = []
        self.lastw = {}
        self.readers = {}
        self.region = 0
        self.last_dma = {}
        self.schedule = schedule
        self.alias = {}

    def _expand(self, keys):
        out = []
        for k in keys:
            if isinstance(k, tuple) and k[0] == "psp":
                out.append(("ps", 2 * k[1]))
                out.append(("ps", 2 * k[1] + 1))
            else:
                out.append(k)
                for a_ in self.alias.get(k, ()):
                    out.append(a_)
        return out

    def op(self, eng, fn, reads=(), writes=(), dma=None, dur=0.3):
        reads = self._expand(reads)
        writes = self._expand(writes)
        o = Op(eng, fn, dma, dur, self.region, len(self.ops))
        deps = set()
        for k in reads:
            t = self.lastw.get(k)
            if t is not None:
                deps.add(t)
            if isinstance(k, tuple) and k[0] == "ps":
                for r in self.readers.get(k, ()):
                    deps.add(r)
        for k in writes:
            t = self.lastw.get(k)
            if t is not None:
                deps.add(t)
            for r in self.readers.get(k, ()):
                deps.add(r)
        for d in deps:
            if d.region != o.region:
                continue
            if d.eng == "pe" and eng == "pe" and d.dma is None:
                o.odeps.add(d)
            else:
                o.deps.add(d)
        if dma is not None:
            p_ = self.last_dma.get((eng, dma))
            if p_ is not None and p_.region == o.region:
                o.odeps.add(p_)
            self.last_dma[(eng, dma)] = o
        self.ops.append(o)
        for k in reads:
            self.readers.setdefault(k, []).append(o)
        for k in writes:
            self.lastw[k] = o
            self.readers[k] = []
        return o

    def barrier(self):
        self.region += 1

    def _schedule_region(self, ops):
        if not self.schedule or len(ops) < 3:
            return list(ops)
        LAT_X, LAT_S = 0.25, 0.1
        for o in ops:
            o.succ = []
            o.npred = 0
        for o in ops:
            for d in list(o.deps) + list(o.odeps):
                d.succ.append(o)
                o.npred += 1
        for o in reversed(ops):
            m = 0.0
            for s_ in o.succ:
                if s_.prio > m:
                    m = s_.prio
            o.prio = m + o.dur + LAT_X
        free = {e: 0.0 for e in self.ENGS}
        ready = [o for o in ops if o.npred == 0]
        out = []
        rt = {}
        for o in ready:
            rt[o] = 0.0
        while ready:
            best = None
            bs = None
            for o in ready:
                st = rt[o]
                if free[o.eng] > st:
                    st = free[o.eng]
                key = (st, -o.prio, o.seq)
                if bs is None or key < bs:
                    bs = key
                    best = o
            ready.remove(best)
            st = bs[0]
            extra = 0.0
            if best.dma is not None:
                extra = 2.0
                fin_issue = st + 0.1
                free[best.eng] = fin_issue
                best.fin = fin_issue + best.dur + extra
            else:
                best.fin = st + best.dur
                free[best.eng] = best.fin
            out.append(best)
            for s_ in best.succ:
                s_.npred -= 1
                lat = LAT_S if (s_.eng == best.eng and best in s_.odeps) else LAT_X
                t_ = (best.fin + lat) if best not in s_.odeps else (st + 0.01)
                if t_ > rt.get(s_, 0.0):
                    rt[s_] = t_
                if s_.npred == 0:
                    ready.append(s_)
        assert len(out) == len(ops), (len(out), len(ops))
        return out

    def emit(self, stack):
        nc = self.nc
        nreg = self.region + 1
        regions = [[] for _ in range(nreg)]
        for o in self.ops:
            regions[o.region].append(o)
        self.q = {e: [] for e in self.ENGS}
        prev_last = {}
        prev_dmas = []
        for r in range(nreg):
            order = self._schedule_region(regions[r])
            first = {}
            for o in order:
                if o.eng not in first:
                    first[o.eng] = o
            if r > 0:
                for e, o in first.items():
                    for e2, l2 in prev_last.items():
                        if e2 == e == "pe" and l2.dma is None:
                            continue
                        o.deps.add(l2)
                    for d in prev_dmas:
                        o.deps.add(d)
            last = {}
            for o in order:
                last[o.eng] = o
                self.q[o.eng].append(o)
            for e, l2 in prev_last.items():
                if e not in last:
                    last[e] = l2
            prev_last = last
            prev_dmas = prev_dmas + [o for o in order if o.dma is not None]
        sems = {}
        for e in self.ENGS:
            sems[("e", e)] = stack.enter_context(nc.semaphore("s_" + e))
        dnames = sorted({o.dma for o in self.ops if o.dma is not None}, key=str)
        for d in dnames:
            sems[("d", d)] = stack.enter_context(nc.semaphore("d_" + str(d)))
        dcnt = {}
        for e in self.ENGS:
            for i, o in enumerate(self.q[e]):
                o.idx = i
                if o.dma is not None:
                    dcnt[o.dma] = dcnt.get(o.dma, 0) + 16
                    o.dval = dcnt[o.dma]
        for o in self.ops:
            for d in o.deps:
                if d.dma is None and d.fn is not None:
                    d.need = True
        for e in self.ENGS:
            c = 0
            for o in self.q[e]:
                if o.need:
                    c += 1
                o.ms = c
        block = stack.enter_context(nc.Block())
        engobj = {"pe": "tensor", "act": "scalar", "dve": "vector", "pool": "gpsimd", "sp": "sync"}

        def run(e):
            def body(eng):
                waited = {}
                for o in self.q[e]:
                    ws = {}
                    for d in o.deps:
                        if d.dma is None:
                            if d.eng == e and d.idx > o.idx:
                                raise RuntimeError("same-engine dependency scheduled out of order")
                            key = ("e", d.eng)
                            val = d.ms
                        else:
                            key = ("d", d.dma)
                            val = d.dval
                        if val > ws.get(key, 0):
                            ws[key] = val
                    for key, val in ws.items():
                        if val > waited.get(key, 0):
                            eng.wait_ge(sems[key], val)
                            waited[key] = val
                    if o.fn is None:
                        continue
                    ins = o.fn(eng)
                    if o.dma is not None:
                        ins.then_inc(sems[("d", o.dma)], 16)
                    elif o.need:
                        ins.then_inc(sems[("e", e)], 1)

            getattr(block, engobj[e])(body)

        for e in self.ENGS:
            if self.q[e]:
                run(e)


class Bump:
    def __init__(self, base, limit):
        self.o = base
        self.limit = limit

    def __call__(self, words):
        o = self.o
        self.o += (int(words) + 15) // 16 * 16
        assert self.o <= self.limit, (self.o, self.limit)
        return o


def col_lo(m):
    if m <= 1:
        return 0
    return 128 * ((m - 1 + 1) // 2)


def build_program(stop_after=99, taps=(), P1C=8, P1S=99, SCHED=True):
    nc = bass.Bass("TRN2", target_bir_lowering=False)

    def din(name, shape, dt=F32):
        return nc.dram_tensor(name, shape, dt, kind="ExternalInput").ap()

    xa = din("xa", [4096, 1024]); xo = din("xo", [2048, 1024])
    posa_d = din("posa", [128, 32], I32); poso_d = din("poso", [128, 16], I32)
    qrel_d = din("qrel", [128, 512]); qrel2_d = din("qrel2", [128, 1])
    ng_d = din("ng", [128, 8]); w_in = din("w_in", [1024, 4552]); bm_d = din("bm", [128, 16])
    gqn_d = din("gqn", [128, 2]); w_uq = din("w_uq", [256, 768]); gkvn_d = din("gkvn", [128, 1])
    w_ukv = din("w_ukv", [128, 1024])
    gq_d = din("gq", [128, 96]); gk_d = din("gk", [128, 96]); gqd_d = din("gqd", [128, 64]); gkd_d = din("gkd", [128, 64])
    wba_d = din("wba", [512, 1024]); wbd_d = din("wbd", [512, 1024]); wo_d = din("wo", [1024, 1024])
    y = nc.dram_tensor("y", [2048, 1024], F32, kind="ExternalOutput").ap()
    tap_out = {}
    w_in_v = w_in.rearrange("(kc p) n -> p kc n", p=128)

    with ExitStack() as st:
        NW = 52480
        A = st.enter_context(nc.sbuf_tensor("A", [128, NW], F32))
        PSP = [st.enter_context(nc.psum_tensor("psp%d" % i, [128, 1024], F32)) for i in range(4)]
        PS = [PSP[i // 2][:, (i % 2) * 512:(i % 2 + 1) * 512] for i in range(8)]
        p = Prog(nc, schedule=SCHED)

        def vw(off, words, dt=F32):
            a = A[:, off:off + int(words)]
            return a if dt == F32 else a.bitcast(dt)

        def _fsz(ap):
            n = 1
            for d_ in ap.shape[1:]:
                n *= int(d_)
            return n

        def E(eng, meth, reads, writes, *a, **kw):
            ap = kw.get("out", kw.get("in_", a[0] if a else None))
            n = _fsz(ap) if ap is not None else 64
            if meth in ("max", "match_replace"):
                n = _fsz(kw.get("in_", kw.get("in_values")))
            if eng == "act":
                dur = 0.22 + n / 1400.0
            elif eng == "pool":
                dur = 0.35 + n / 480.0
            else:
                dur = 0.14 + n / 960.0
            return p.op(eng, lambda e: getattr(e, meth)(*a, **kw), reads, writes, dur=dur)

        def DMA(eng, out, in_, sem, reads, writes, **kw):
            n = _fsz(out)
            return p.op(eng, lambda e: e.dma_start(out=out, in_=in_, **kw), reads, writes, dma=sem, dur=1.0 + n * 4 / 1500.0)

        def MM(out, lhsT, rhs, reads, writes, start=True, stop=True):
            n = _fsz(rhs)
            return p.op("pe", lambda e: e.matmul(out, lhsT=lhsT, rhs=rhs, start=start, stop=stop), reads, writes,
                        dur=0.06 + n / 1500.0)

        ntap = [0]

        def tap(name, ap, shape, key, dt=F32):
            if name not in taps:
                return
            t = nc.dram_tensor("tap_" + name, list(shape), dt, kind="ExternalOutput").ap()
            tap_out[name] = t
            ntap[0] += 1
            DMA("sp", t, ap, "tap%d" % ntap[0], [key] if not isinstance(key, list) else key, [])

        bank_rr = [0]

        def psk(i):
            return ("ps", i)

        P = Bump(0, NW)
        o_ident = P(64); ident = vw(o_ident, 64, BF16)
        o_onesf = P(64); ones_f = vw(o_onesf, 64)
        o_onesb = P(1); ones_b = vw(o_onesb, 1, BF16)
        o_ng = P(8); ng = vw(o_ng, 8)
        o_gqn = P(2); gqn = vw(o_gqn, 2)
        o_gkvn = P(1); gkvn = vw(o_gkvn, 1)
        o_bm = P(16); bm = vw(o_bm, 16)
        o_gq = P(96); gq = vw(o_gq, 96)
        o_gk = P(96); gk = vw(o_gk, 96)
        o_gqd = P(64); gqd = vw(o_gqd, 64)
        o_gkd = P(64); gkd = vw(o_gkd, 64)
        o_invf = P(28); invf = vw(o_invf, 28)
        o_qrel = P(512); qrel = vw(o_qrel, 512)
        o_qrel2 = P(1); qrel2 = vw(o_qrel2, 1)
        o_cbias = P(256); cbias = vw(o_cbias, 256)
        o_rkv = P(32); rkv = vw(o_rkv, 32)
        o_small = P(256)
        o_OTa = P(4096); OTa = vw(o_OTa, 4096, BF16).rearrange("p (c t) -> p c t", c=4)
        o_OTb = P(4096); OTb = vw(o_OTb, 4096, BF16).rearrange("p (c t) -> p c t", c=4)
        o_dead4 = P.o
        o_cosA = P(896); cosA = vw(o_cosA, 896).rearrange("p (t f) -> p t f", f=28)
        o_sinA = P(896); sinA = vw(o_sinA, 896).rearrange("p (t f) -> p t f", f=28)
        o_cosO = P(448); cosO = vw(o_cosO, 448).rearrange("p (t f) -> p t f", f=28)
        o_sinO = P(448); sinO = vw(o_sinO, 448).rearrange("p (t f) -> p t f", f=28)
        o_ckvT = P(2048); ckvT = vw(o_ckvT, 2048, BF16)
        o_kpe = P(1024); kpe = vw(o_kpe, 1024).rearrange("p (t f) -> p t f", f=32)
        o_kss = P(32); kss = vw(o_kss, 32)
        o_dead4_end = P.o
        ARENA = P.o
        assert ARENA % 16 == 0

        ss4 = vw(o_small, 4); rs4 = vw(o_small + 4, 4)
        ssn = vw(o_small + 8, 32); rsn = vw(o_small + 40, 32)
        sgn = vw(o_small + 72, 32)
        rq4 = vw(o_small + 104, 4)
        m8 = vw(o_small + 112, 8)
        thr = vw(o_small + 120, 1)
        ssk = vw(o_small + 124, 4)
        LO = vw(o_small + 150, 2); HI = vw(o_small + 152, 2); TC = vw(o_small + 154, 2); FB = vw(o_small + 156, 2)
        ta = vw(o_small + 158, 1); tf = vw(o_small + 159, 1); td = vw(o_small + 160, 1); te = vw(o_small + 161, 1)
        negbig = vw(o_small + 162, 1)
        ss4b = vw(o_small + 224, 4); rs4b = vw(o_small + 228, 4); sskb = vw(o_small + 232, 4)
        g2 = vw(o_small + 164, 2, I32); ng2 = vw(o_small + 166, 2, I32)
        u8 = vw(o_small + 168, 48); nvv = vw(o_small + 236, 1)
        io8 = vw(o_small + 238, 8); oh8 = vw(o_small + 246, 8); io8i = vw(o_small + 216, 8, I32)

        tmpc = Bump(ARENA, NW)
        o_t0 = tmpc(1024); o_t1 = tmpc(1024); o_t2 = tmpc(1024); o_t3 = tmpc(1024)
        idi = vw(o_t0, 128, I32); idf = vw(o_t1, 128)
        E("pool", "iota", [], ["idi"], idi, pattern=[[1, 128]], base=0, channel_multiplier=-1)
        E("dve", "tensor_copy", ["idi"], ["idf"], out=idf, in_=idi)
        E("dve", "tensor_scalar", ["idf"], ["ident"], out=ident, in0=idf, scalar1=0.0, scalar2=None, op0=ALU.is_equal)
        E("dve", "memset", [], ["ones_f"], ones_f, 1.0)
        E("dve", "memset", [], ["ones_b"], ones_b, 1.0)
        fr = []
        for rot in (32, 16, 8):
            for j in range(rot // 2):
                fr.append(float(np.float32(THETA) ** np.float32(-(2.0 * j) / rot)))
        for j, f in enumerate(fr):
            E("dve" if j % 2 == 0 else "pool", "memset", [], [("invf", j)], invf[:, j:j + 1], f)
        INVF = [("invf", j) for j in range(len(fr))]
        for nm, dst, src in (("ng", ng, ng_d), ("gqn", gqn, gqn_d), ("gkvn", gkvn, gkvn_d), ("bm", bm, bm_d),
                             ("gq", gq, gq_d), ("gk", gk, gk_d), ("gqd", gqd, gqd_d), ("gkd", gkd, gkd_d),
                             ("qrel", qrel, qrel_d), ("qrel2", qrel2, qrel2_d)):
            DMA("sp", dst, src, "c_" + nm, [], [nm])
        E("pool", "iota", [], ["io8i"], io8i, pattern=[[1, 8]], base=0, channel_multiplier=0)
        E("dve", "tensor_copy", ["io8i"], ["io8"], out=io8, in_=io8i)
        E("dve", "memset", [], ["negbig"], negbig, -0.5 * BIG)
        kii = vw(o_t0 + 128, 256, I32)
        kio = vw(o_t1 + 128, 256)
        E("pool", "iota", [], ["kii"], kii, pattern=[[1, 256]], base=0, channel_multiplier=0)
        E("dve", "tensor_copy", ["kii"], ["kio"], out=kio, in_=kii)
        E("dve", "tensor_scalar", ["kio", "qrel2"], ["cbias"], out=cbias, in0=kio, scalar1=qrel2[:, 0:1], scalar2=-BIG,
          op0=ALU.is_gt, op1=ALU.mult)

        o_t4 = tmpc(1024); o_t5 = tmpc(1024)
        o_rsc = {"A": (o_t2, o_t3, o_t4, o_t5), "O": tuple(tmpc(1024) for _ in range(4))}

        def rope_tables(pos_d, ntile, cosT, sinT, nm):
            q2, q3, q4, q5 = o_rsc[nm]
            pi_ = vw(q2, ntile, I32)
            pf = vw(q2 + 64, ntile)
            ang = vw(q3, ntile * 28).rearrange("p (t f) -> p t f", f=28)
            uu = vw(q4, ntile * 28).rearrange("p (t f) -> p t f", f=28)
            kk = vw(q5, ntile * 28).rearrange("p (t f) -> p t f", f=28)
            DMA("sp", pi_, pos_d, "c_pos" + nm, [], ["pi" + nm])
            E("dve", "tensor_copy", ["pi" + nm], ["pf" + nm], out=pf, in_=pi_)
            E("dve", "tensor_tensor", ["pf" + nm] + INVF, ["ang" + nm], out=ang,
              in0=pf.unsqueeze(2).to_broadcast([128, ntile, 28]),
              in1=invf.unsqueeze(1).to_broadcast([128, ntile, 28]), op=ALU.mult)
            E("dve", "tensor_scalar", ["ang" + nm], ["ang" + nm], out=ang, in0=ang, scalar1=1.0 / TWO_PI, scalar2=None,
              op0=ALU.mult)
            for dst, shift in ((sinT, 0.0), (cosT, 0.25)):
                E("dve", "tensor_scalar", ["ang" + nm], ["uu" + nm], out=uu, in0=ang, scalar1=shift, scalar2=None, op0=ALU.add)
                E("dve", "tensor_scalar", ["uu" + nm], ["rk" + nm], out=kk, in0=uu, scalar1=MAGIC, scalar2=MAGIC,
                  op0=ALU.add, op1=ALU.subtract)
                E("dve", "tensor_tensor", ["uu" + nm, "rk" + nm], ["rk" + nm], out=kk, in0=uu, in1=kk, op=ALU.subtract)
                E("act", "activation", ["rk" + nm], ["tab" + nm], out=dst, in_=kk, func=AF.Sin, scale=TWO_PI * (1.0 - 1e-6))

        rope_tables(posa_d, 32, cosA, sinA, "A")
        rope_tables(poso_d, 16, cosO, sinO, "O")
        ROPE_A = ["tabA"]
        ROPE_O = ["tabO"]

        def load_x(src_rows, xs, xskey, semname, alias_keys=()):
            DMA("sp", xs, src_rows.rearrange("(t p) d -> p t d", p=128), semname, [], [xskey] + list(alias_keys))

        def make_hT(src_rows, xs, xskey, hb, hT, junk, semname, tb=(0, 1, 2, 3), hkey="hT", alias_keys=(), do_load=True,
                    sfx="", st4=None):
            ss4_, rs4_ = (ss4, rs4) if st4 is None else st4
            kj, kr = "junk" + sfx, "rs4" + sfx
            if do_load:
                load_x(src_rows, xs, xskey, semname, alias_keys)
            for t in range(4):
                E("act", "activation", [xskey], [kj, ("ss4" + sfx, t)], out=junk, in_=xs[:, t, :], func=AF.Square,
                  accum_out=ss4_[:, t:t + 1])
            E("act", "activation", [("ss4" + sfx, t) for t in range(4)], [kr], out=rs4_, in_=ss4_, func=AF.Ln,
              scale=1.0 / 1024, bias=EPS)
            E("act", "activation", [kr], [kr], out=rs4_, in_=rs4_, func=AF.Exp, scale=-0.5)
            for t in range(4):
                if t % 2 == 0:
                    E("dve", "tensor_scalar", [xskey, kr], [("hb" + sfx, t)], out=hb[:, t, :], in0=xs[:, t, :],
                      scalar1=rs4_[:, t:t + 1], scalar2=None, op0=ALU.mult)
                else:
                    E("pool", "tensor_scalar", [xskey, kr], [("hb" + sfx, t)], out=hb[:, t, :], in0=xs[:, t, :],
                      scalar1=rs4_[:, t:t + 1], scalar2=0.0, op0=ALU.mult, op1=ALU.add)
            for kc in range(8):
                b = tb[kc % len(tb)]
                for t in range(4):
                    MM(PS[b][:, t * 128:(t + 1) * 128], hb[:, t, kc * 128:(kc + 1) * 128], ident,
                       [("hb" + sfx, t), "ident"], [psk(b)])
                if kc % 2 == 0:
                    E("act", "activation", [psk(b), "ng"], [(hkey, kc)], out=hT[:, kc, :], in_=PS[b][:], func=AF.Identity,
                      scale=ng[:, kc:kc + 1])
                else:
                    E("dve", "tensor_scalar", [psk(b), "ng"], [(hkey, kc)], out=hT[:, kc, :], in0=PS[b][:],
                      scalar1=ng[:, kc:kc + 1], scalar2=None, op0=ALU.mult)

        HT_KEYS = [("hT", kc) for kc in range(8)]

        def head_prep(W3, W4, n, D, Wk, sq, ssv, rsv, gain, gkey, rope, cosv, sinv, tkey, T, H, rt, sfx=""):
            ksq, kss, krs = "sq" + sfx, "ssn" + sfx, "rsn" + sfx
            if W3 is not None:
                E("act", "activation", [Wk], [ksq], out=sq, in_=W3, func=AF.Square)
                E("dve", "tensor_reduce", [ksq], [kss], out=ssv, in_=sq, axis=AX.X, op=ALU.add)
                E("act", "activation", [kss], [krs], out=rsv, in_=ssv, func=AF.Ln, scale=1.0 / D, bias=EPS)
                E("act", "activation", [krs], [krs], out=rsv, in_=rsv, func=AF.Exp, scale=-0.5)
                E("dve", "tensor_tensor", [Wk, krs], [Wk], out=W3, in0=W3, in1=rsv.unsqueeze(2).to_broadcast([128, n, D]),
                  op=ALU.mult)
                E("dve", "tensor_tensor", [Wk, gkey], [Wk], out=W3, in0=W3, in1=gain.unsqueeze(1).to_broadcast([128, n, D]),
                  op=ALU.mult)
            if rope is not None:
                ro, r = rope
                r2 = r // 2
                x1 = W4[:, :, :, ro:ro + r2]
                x2 = W4[:, :, :, ro + r2:ro + r]
                c = cosv.unsqueeze(2).to_broadcast([128, T, H, r2])
                s = sinv.unsqueeze(2).to_broadcast([128, T, H, r2])
                t1, t2, t3, t4 = [rt[k][:, 0:T * H * r2].rearrange("p (t h d) -> p t h d", t=T, h=H) for k in range(4)]
                rk = [("rt" + sfx, k) for k in range(4)]
                E("dve", "tensor_tensor", [Wk] + tkey, [rk[0]], out=t1, in0=x1, in1=c, op=ALU.mult)
                E("dve", "tensor_tensor", [Wk] + tkey, [rk[1]], out=t2, in0=x2, in1=s, op=ALU.mult)
                E("dve", "tensor_tensor", [Wk] + tkey, [rk[2]], out=t3, in0=x2, in1=c, op=ALU.mult)
                E("dve", "tensor_tensor", [Wk] + tkey, [rk[3]], out=t4, in0=x1, in1=s, op=ALU.mult)
                E("dve", "tensor_tensor", [rk[0], rk[1]], [Wk], out=x1, in0=t1, in1=t2, op=ALU.subtract)
                E("dve", "tensor_tensor", [rk[2], rk[3]], [Wk], out=x2, in0=t3, in1=t4, op=ALU.add)

        def finalize(bank, dest, dkey, rd, bcs, bcbank):
            E("act", "activation", [psk(bank)], ["rd"], out=rd[64:65, :], in_=PS[bank][64:65, :], func=AF.Ln)
            MM(PS[bcbank][0:64, :], ones_f[64:65, 0:64], rd[64:65, :], ["ones_f", "rd"], [psk(bcbank)])
            E("act", "activation", [psk(bcbank)], ["bcs"], out=bcs[0:64, :], in_=PS[bcbank][0:64, :], func=AF.Exp, scale=-1.0)
            E("dve", "tensor_tensor", [psk(bank), "bcs"], [dkey], out=dest, in0=PS[bank][0:64, :], in1=bcs[0:64, :],
              op=ALU.mult)

        def load_w(dst, src, sem, key):
            DMA("pool", dst, src, sem, [], [key])

        p.barrier()
        ar = Bump(ARENA, NW)
        o_KTb = ar(4096)
        KTz = [vw(o_KTb, 2048, BF16), vw(o_KTb + 2048, 2048, BF16)]
        KTb = KTz[0]
        o_Vb = ar(2080); Vb = vw(o_Vb, 2080, BF16).rearrange("p (t c) -> p t c", c=130)
        Vb4 = vw(o_Vb, 2080, BF16).rearrange("p (t h c) -> p t h c", h=2, c=65)
        o_KTi = ar(2048); KTi = vw(o_KTi, 2048, BF16)
        DSAK_END = ar.o
        o_WK = ar(1792); WK = vw(o_WK, 1792, BF16).rearrange("p (k n) -> p k n", k=8)
        P1 = []
        for par_ in range(2):
            d_ = {}
            d_["xs"] = vw(ar(4096), 4096).rearrange("p (t d) -> p t d", t=4)
            d_["hb"] = vw(ar(2048), 2048, BF16).rearrange("p (t d) -> p t d", t=4)
            d_["hT"] = vw(ar(2048), 2048, BF16).rearrange("p (k n) -> p k n", k=8)
            d_["junk"] = vw(ar(512), 512, BF16)
            o_ = ar(1280); d_["Wt"] = vw(o_, 1280).rearrange("p (t n) -> p t n", t=4)
            o_ = ar(512); d_["o_kbw"] = o_
            d_["kbw3"] = vw(o_, 512).rearrange("p (n d) -> p n d", d=64)
            d_["kbw4"] = vw(o_, 512).rearrange("p (t h d) -> p t h d", t=4, h=2)
            d_["sq3"] = vw(ar(512), 512).rearrange("p (n d) -> p n d", d=64)
            d_["rt"] = [vw(ar(64), 64) for _ in range(4)]
            d_["Kb"] = vw(ar(256), 256, BF16).rearrange("p (t n) -> p t n", t=4)
            o_ = ar(192); d_["Ks"] = vw(o_, 192, BF16).rearrange("p (t n) -> p t n", t=4)
            d_["Ks5"] = vw(o_, 192, BF16).rearrange("p (t a d) -> p t a d", t=4, a=3)
            d_["kiw"] = vw(ar(128), 128).rearrange("p (t n) -> p t n", t=4)
            d_["sqb"] = vw(ar(256), 256, BF16)
            d_["sqp"] = vw(ar(128), 128).rearrange("p (t f) -> p t f", f=32)
            P1.append(d_)

        cols = [(CKV, 0), (KPE, 128), (KB, 160), (VB, 288), (KI, 416)]
        for i, ((c0, c1), o0) in enumerate(cols):
            load_w(WK[:, :, o0:o0 + (c1 - c0)], w_in_v[:, :, c0:c1], "wk%d" % i, ("WK", i))
        WKK = [("WK", i) for i in range(5)]
        tap("WK", vw(o_WK, 1792, BF16), [128, 3584], WKK, BF16)
        E("pool", "memset", [], ["Vb"], Vb[:, :, 64:65], 1.0)
        E("pool", "memset", [], ["Vb"], Vb[:, :, 129:130], 1.0)
        E("pool", "memset", [], ["KTz"], KTz[0][64:128, :], 0.0)
        E("pool", "memset", [], ["KTz"], KTz[1][0:64, :], 0.0)

        def p1_gen(par):
            d = P1[par]
            sx = "p%d" % par
            bA = 4 + 2 * par
            bB = 5 + 2 * par
            tb = (0, 1) if par == 0 else (2, 3)
            st4 = (ss4, rs4) if par == 0 else (ss4b, rs4b)
            ssk_ = ssk if par == 0 else sskb
            HK = "hT" + sx
            hT = d["hT"]; Wt = d["Wt"]; sqb = d["sqb"]; Kb = d["Kb"]; Ks = d["Ks"]; Ks5 = d["Ks5"]; kiw = d["kiw"]
            for c in range(par, P1C, 2):
                make_hT(xa[c * 512:(c + 1) * 512, :], d["xs"], "xs" + sx, d["hb"], hT, d["junk"], "xs%d" % par, tb=tb, hkey=HK, sfx=sx, st4=st4)
                yield 12.0
                for kc in range(8):
                    MM(PS[bA][:], WK[:, kc, 0:128], hT[:, kc, :], WKK + [(HK, kc)], [psk(bA)], start=(kc == 0), stop=(kc == 7))
                E("dve", "tensor_scalar", [psk(bA), "gkvn"], [("ckvT", c)], out=ckvT[:, c * 512:(c + 1) * 512], in0=PS[bA][:],
                  scalar1=gkvn[:, 0:1], scalar2=None, op0=ALU.mult)
                E("act", "activation", [psk(bA)], ["sqb" + sx], out=sqb, in_=PS[bA][:], func=AF.Square)
                for t in range(4):
                    MM(PS[bB][:, 2 * t:2 * t + 2], sqb[:, t * 128:(t + 1) * 128], ones_b[:, 0:2], ["sqb" + sx, "ones_b"], [psk(bB)])
                E("act", "activation", [psk(bB)], ["ssk" + sx], out=ssk_, in_=PS[bB][:, 0:8].rearrange("p (t two) -> p t two", two=2)[:, :, 0],
                  func=AF.Ln, scale=1.0 / 128, bias=EPS)
                E("act", "activation", ["ssk" + sx], [("rkv", c)], out=rkv[:, c * 4:(c + 1) * 4], in_=ssk_, func=AF.Exp, scale=-0.5)
                yield 5.0
                for t in range(4):
                    for kc in range(8):
                        MM(PS[bB][:, 0:320], hT[:, kc, t * 128:(t + 1) * 128], WK[:, kc, 128:448], WKK + [(HK, kc)], [psk(bB)],
                           start=(kc == 0), stop=(kc == 7))
                    E("act", "activation", [psk(bB)], [("Wt" + sx, t)], out=Wt[:, t, :], in_=PS[bB][:, 0:320], func=AF.Copy)
                    yield 2.5
                WtK = [("Wt" + sx, t) for t in range(4)]
                kpc = kpe[:, c * 4:(c + 1) * 4, :]
                E("pool", "tensor_copy", WtK, [("kpe", c)], out=kpc, in_=Wt[:, :, 0:32])
                E("act", "activation", [("kpe", c)], ["sqp" + sx], out=d["sqp"], in_=kpc, func=AF.Square)
                E("dve", "tensor_reduce", ["sqp" + sx], [("kss", c)], out=kss[:, c * 4:(c + 1) * 4], in_=d["sqp"], axis=AX.X, op=ALU.add)
                E("dve", "tensor_tensor", [("kpe", c), "gk", "sqp" + sx], [("kpe", c)], out=kpc, in0=kpc,
                  in1=gk[:, 64:96].unsqueeze(1).to_broadcast([128, 4, 32]), op=ALU.mult)
                head_prep(None, kpc.unsqueeze(2), 4, 32, ("kpe", c), None, None, None, None, None, (0, 32),
                          cosA[:, c * 4:(c + 1) * 4, 0:16], sinA[:, c * 4:(c + 1) * 4, 0:16], ROPE_A, 4, 1, d["rt"], sfx=sx)
                E("pool", "tensor_copy", WtK, ["Vb"], out=Vb4[:, c * 4:(c + 1) * 4, :, 0:64],
                  in_=Wt[:, :, 160:288].rearrange("p t (h d) -> p t h d", h=2))
                E("pool", "tensor_copy", WtK, ["kbw" + sx], out=d["kbw4"], in_=Wt[:, :, 32:160].rearrange("p t (h d) -> p t h d", h=2))
                head_prep(d["kbw3"], d["kbw4"], 8, 64, "kbw" + sx, d["sq3"], ssn[:, par * 8:par * 8 + 8], rsn[:, par * 8:par * 8 + 8], gkd, "gkd", (0, 16),
                          cosA[:, c * 4:(c + 1) * 4, 16:24], sinA[:, c * 4:(c + 1) * 4, 16:24], ROPE_A, 4, 2, d["rt"], sfx=sx)
                yield 14.0
                E("act", "activation", ["kbw" + sx], ["Kb" + sx], out=Kb, in_=vw(d["o_kbw"], 512).rearrange("p (t n) -> p t n", t=4), func=AF.Copy)
                for t in range(4):
                    MM(PS[bA][:, t * 128:(t + 1) * 128], Kb[:, t, :], ident, ["Kb" + sx, "ident"], [psk(bA)])
                E("dve", "tensor_copy", [psk(bA), "KTz"], [("KTb", c)], out=KTz[0][0:64, c * 512:(c + 1) * 512], in_=PS[bA][0:64, :])
                E("act", "activation", [psk(bA), "KTz"], [("KTb", c)], out=KTz[1][64:128, c * 512:(c + 1) * 512], in_=PS[bA][64:128, :], func=AF.Copy)
                E("pool", "tensor_copy", WtK, ["kiw" + sx], out=kiw, in_=Wt[:, :, 288:320])
                head_prep(None, kiw.unsqueeze(2), 4, 32, "kiw" + sx, None, None, None, None, None, (0, 8),
                          cosA[:, c * 4:(c + 1) * 4, 24:28], sinA[:, c * 4:(c + 1) * 4, 24:28], ROPE_A, 4, 1, d["rt"], sfx=sx)
                E("dve", "tensor_copy", ["kiw" + sx], ["Ks" + sx], out=Ks5[:, :, 0, :], in_=kiw)
                E("dve", "tensor_tensor", ["kiw" + sx, "Ks" + sx], ["Ks" + sx], out=Ks5[:, :, 1, :], in0=kiw, in1=Ks5[:, :, 0, :], op=ALU.subtract)
                E("pool", "tensor_copy", ["Ks" + sx], ["Ks" + sx], out=Ks5[:, :, 2, :], in_=Ks5[:, :, 0, :])
                for t in range(4):
                    MM(PS[bB][0:96, t * 128:(t + 1) * 128], Ks[:, t, :], ident, ["Ks" + sx, "ident"], [psk(bB)])
                E("act", "activation", [psk(bB)], [("KTi", c)], out=KTi[0:96, c * 512:(c + 1) * 512], in_=PS[bB][0:96, :], func=AF.Copy)
                yield 9.0

        def _ileave(ga, gb):
            ta = tb_ = 0.0
            ea = eb = False
            while not (ea and eb):
                if not ea and (eb or ta <= tb_):
                    try:
                        ta += next(ga) or 1.0
                    except StopIteration:
                        ea = True
                else:
                    try:
                        tb_ += next(gb) or 1.0
                    except StopIteration:
                        eb = True

        def _stagger(g, t0):
            yield t0
            yield from g

        _ileave(p1_gen(0), _stagger(p1_gen(1), 25.0))

        KTbK = [("KTb", c) for c in range(8)]
        KTiK = [("KTi", c) for c in range(8)]
        tap("KTb", KTz[0], [128, 4096], KTbK, BF16)
        tap("KTb1", KTz[1], [128, 4096], KTbK, BF16)
        tap("KTi", KTi[0:96, :], [96, 4096], KTiK, BF16)
        tap("Vb", vw(o_Vb, 2080, BF16), [128, 4160], ["Vb"], BF16)
        tap("ckvT", ckvT, [128, 4096], [("ckvT", c) for c in range(8)], BF16)
        tap("rkv", rkv, [128, 32], [("rkv", c) for c in range(8)])
        tap("kpe", vw(o_kpe, 1024), [128, 1024], [("kpe", c) for c in range(8)])
        tap("cosA", vw(o_cosA, 896), [128, 896], ["tabA"])
        tap("sinA", vw(o_sinA, 896), [128, 896], ["tabA"])
        p.barrier()

        if stop_after >= 2:
            U8 = mybir.dt.uint8
            ar = Bump(DSAK_END, NW)
            o_WQb = ar(3104); WQb = vw(o_WQb, 3104, BF16).rearrange("p (k n) -> p k n", k=8)
            o_mT = ar(7168)
            ORDER2 = [0, 1, 3, 2]
            BUF2 = {jj: pos % 2 for pos, jj in enumerate(ORDER2)}
            mTp = [vw(o_mT, 4096, U8).rearrange("p (k q) -> p k q", q=512),
                   vw(o_mT + 4096, 3072, U8).rearrange("p (k q) -> p k q", q=512)]
            o_QTb = ar(1024)
            QTb2 = [vw(o_QTb, 1024, BF16).rearrange("p (a q) -> p a q", a=4),
                    vw(o_cosA, 1024, BF16).rearrange("p (a q) -> p a q", a=4)]
            o_hT = ar(2048); hT = vw(o_hT, 2048, BF16).rearrange("p (k n) -> p k n", k=8)
            o_QTi = ar(2048); QTi = vw(o_QTi, 2048, BF16).rearrange("p (h q) -> p h q", h=8)
            R0 = ar.o
            rs_ = Bump(R0, NW)
            o_hb = rs_(2048); hb = vw(o_hb, 2048, BF16).rearrange("p (t d) -> p t d", t=4)
            o_xs2 = rs_(4096); xs2 = vw(o_xs2, 4096).rearrange("p (t d) -> p t d", t=4)
            sqq3 = vw(o_xs2, 2048).rearrange("p (n d) -> p n d", d=64)
            Qs = vw(o_xs2 + 2048, 1536, BF16).rearrange("p (t n) -> p t n", t=4)
            o_Wqb = rs_(2048); Wqb3 = vw(o_Wqb, 2048).rearrange("p (n d) -> p n d", d=64)
            Wqb4 = vw(o_Wqb, 2048).rearrange("p (t h d) -> p t h d", t=4, h=8)
            Wqbt = vw(o_Wqb, 2048).rearrange("p (t n) -> p t n", t=4)
            o_Wqi = rs_(1056); Wqi = vw(o_Wqi, 1056).rearrange("p (t n) -> p t n", t=4)
            o_Qb = rs_(1024); Qb = vw(o_Qb, 1024, BF16).rearrange("p (t n) -> p t n", t=4)
            Qb5 = vw(o_Qb, 1024, BF16).rearrange("p (t a g d) -> p t a g d", t=4, a=4, g=2)
            rtq = [vw(o_xs2 + 256 * k_, 256) for k_ in range(4)]
            o_wsc = rs_(32); wsc = vw(o_wsc, 32)
            junk = vw(o_Qb, 512, BF16)
            assert rs_.o <= NW - 2560
            rm_ = Bump(R0, NW)
            o_S = rm_(4096)
            Sv2 = [vw(o_S, 4096), vw(o_OTa, 4096)]
            o_wrk = rm_(4096); wrk = vw(o_wrk, 4096)
            o_M = rm_(2048); Mv = vw(o_M, 2048, BF16)
            assert rm_.o <= NW - 2560
            ra_ = Bump(NW - 2560, NW)
            PT = [vw(ra_(256), 256, BF16) for _ in range(4)]
            o_rd = ra_(512); rd = vw(o_rd, 512)
            o_bcs = ra_(512); bcs = vw(o_bcs, 512)
            o_osb = ra_(512); osb = vw(o_osb, 512)
            S0 = ("S", 0)
            p.alias = {("hb", 0): [S0], ("hb", 1): [S0], ("hb", 2): [S0], ("hb", 3): [S0], "xs2": [S0, "wrk"],
                       "Wqb": ["wrk"], "Wqi": ["M"], "Qb": ["M", "junk"], "junk": ["M", "Qb"], "sq": [S0], "Qs": ["wrk"],
                       ("rt", 0): [S0], ("rt", 1): [S0], ("rt", 2): [S0], ("rt", 3): [S0]}
            assert o_hb + 2048 <= o_S + 4096 and o_xs2 + 4096 <= o_wrk + 4096 and o_Wqb >= o_wrk and o_Wqb + 2048 <= o_wrk + 4096
            assert o_Wqi >= o_M and o_Qb + 1024 <= NW - 2560

            for i, ((c0, c1), o0) in enumerate([(QB, 0), (QI, 512), (WI, 768)]):
                load_w(WQb[:, :, o0:o0 + (c1 - c0)], w_in_v[:, :, c0:c1], "wqb%d" % i, ("WQb", i))
            WQK = [("WQb", i) for i in range(3)]

            def start2(j):
                QTb = QTb2[BUF2[j]]
                make_hT(xo[j * 512:(j + 1) * 512, :], xs2, "xs2", hb, hT, junk, "xs2")
                for t in range(4):
                    for kc in range(8):
                        MM(PS[4][:], hT[:, kc, t * 128:(t + 1) * 128], WQb[:, kc, 0:512], WQK + [("hT", kc)], [psk(4)],
                           start=(kc == 0), stop=(kc == 7))
                    for kc in range(8):
                        MM(PS[5][:, 0:264], hT[:, kc, t * 128:(t + 1) * 128], WQb[:, kc, 512:776], WQK + [("hT", kc)], [psk(5)],
                           start=(kc == 0), stop=(kc == 7))
                    E("act", "activation", [psk(4)], ["Wqb"], out=Wqbt[:, t, :], in_=PS[4][:], func=AF.Copy)
                    E("dve", "tensor_copy", [psk(5)], ["Wqi"], out=Wqi[:, t, :], in_=PS[5][:, 0:264])
                head_prep(Wqb3, Wqb4, 32, 64, "Wqb", sqq3, ssn, rsn, gqd, "gqd", (0, 16),
                          cosO[:, j * 4:(j + 1) * 4, 16:24], sinO[:, j * 4:(j + 1) * 4, 16:24], ROPE_O, 4, 8, rtq)
                E("act", "activation", ["Wqb"], ["Qb"], out=Qb5[:, :, :, 0, :], in_=Wqb4[:, :, 0:4, :], func=AF.Copy)
                E("pool", "tensor_copy", ["Wqb"], ["Qb"], out=Qb5[:, :, :, 1, :], in_=Wqb4[:, :, 4:8, :])
                for a in range(4):
                    b = a % 2
                    for t in range(4):
                        MM(PS[b][:, t * 128:(t + 1) * 128], Qb[:, t, a * 128:(a + 1) * 128], ident, ["Qb", "ident"], [psk(b)])
                    if a % 2 == 0:
                        E("act", "activation", [psk(b)], [("QTb", BUF2[j], a)], out=QTb[:, a, :], in_=PS[b][:], func=AF.Copy)
                    else:
                        E("dve", "tensor_copy", [psk(b)], [("QTb", BUF2[j], a)], out=QTb[:, a, :], in_=PS[b][:])
                sg3 = sgn.rearrange("p (t h) -> p t h", t=4)
                ws3 = wsc.rearrange("p (t h) -> p t h", t=4)
                E("act", "activation", ["Wqi"], ["sgn"], out=sg3, in_=Wqi[:, :, 256:264], func=AF.Sign)
                E("dve", "scalar_tensor_tensor", ["Wqi", "sgn"], ["wsc"], out=ws3, in0=Wqi[:, :, 256:264], scalar=1.0 / 16.0, in1=sg3,
                  op0=ALU.mult, op1=ALU.mult)
                Wqi4 = Wqi[:, :, 0:256].rearrange("p t (h d) -> p t h d", h=8)
                head_prep(None, Wqi4, 32, 32, "Wqi", None, None, None, None, None, (0, 8),
                          cosO[:, j * 4:(j + 1) * 4, 24:28], sinO[:, j * 4:(j + 1) * 4, 24:28], ROPE_O, 4, 8, rtq)
                E("dve", "tensor_tensor", ["Wqi", "wsc"], ["Wqi"], out=Wqi4, in0=Wqi4,
                  in1=ws3.unsqueeze(3).to_broadcast([128, 4, 8, 32]), op=ALU.mult)
                Qs6 = vw(o_xs2 + 2048, 1536, BF16).rearrange("p (t h a d) -> p t h a d", t=4, h=8, a=3)
                E("dve", "tensor_copy", ["Wqi", "sq"], ["Qs"], out=Qs6[:, :, :, 0, :], in_=Wqi4)
                E("dve", "tensor_tensor", ["Wqi", "Qs"], ["Qs"], out=Qs6[:, :, :, 2, :], in0=Wqi4, in1=Qs6[:, :, :, 0, :],
                  op=ALU.subtract)
                E("pool", "tensor_copy", ["Qs"], ["Qs"], out=Qs6[:, :, :, 1, :], in_=Qs6[:, :, :, 0, :])
                for hh in range(8):
                    b = hh % 2
                    for t in range(4):
                        MM(PS[b][0:96, t * 128:(t + 1) * 128], Qs[:, t, hh * 96:(hh + 1) * 96], ident, ["Qs", "ident"], [psk(b)])
                    if hh % 2 == 0:
                        E("act", "activation", [psk(b)], [("QTi", hh)], out=QTi[0:96, hh, :], in_=PS[b][0:96, :], func=AF.Copy)
                    else:
                        E("dve", "tensor_copy", [psk(b)], [("QTi", hh)], out=QTi[0:96, hh, :], in_=PS[b][0:96, :])

            PAIRS = [1, 2]
            ntr = [0]
            npair = [0]
            ntile = [0]

            def idx_gen(j):
                maskT = mTp[BUF2[j]]
                pend = []
                for i in range(4):
                    nkt = 8 * j + 2 * i + 2
                    n = nkt * 128
                    sbi = ntile[0] % 2
                    ntile[0] += 1
                    Sv = Sv2[sbi]
                    SK = ("S", sbi)
                    ks = 0
                    while ks < n:
                        wd = min(512, n - ks)
                        for hp in range(4):
                            pp = PAIRS[npair[0] % len(PAIRS)]
                            npair[0] += 1
                            pkey = ("psp", pp)
                            for e_ in range(2):
                                hh = 2 * hp + e_
                                MM(PSP[pp][:, e_ * 512:e_ * 512 + wd], QTi[0:96, hh, i * 128:(i + 1) * 128], KTi[0:96, ks:ks + wd],
                                   [("QTi", hh)] + KTiK, [pkey])
                            pv_ = PSP[pp][:, :].rearrange("p (e c) -> p e c", e=2)[:, :, 0:wd]
                            E("act", "activation", [pkey], [pkey], out=pv_, in_=pv_, func=AF.Relu)
                            for e_ in range(2):
                                hh = 2 * hp + e_
                                sc = sgn[:, i * 8 + hh:i * 8 + hh + 1]
                                rr = PSP[pp][:, e_ * 512:e_ * 512 + wd]
                                if hh == 0:
                                    E("act", "activation", [pkey, "sgn"], [SK], out=Sv[:, ks:ks + wd], in_=rr, func=AF.Identity, scale=sc)
                                else:
                                    E("dve", "scalar_tensor_tensor", [pkey, "sgn", SK], [SK], out=Sv[:, ks:ks + wd],
                                      in0=rr, scalar=sc, in1=Sv[:, ks:ks + wd], op0=ALU.mult, op1=ALU.add)
                            yield 1.7
                        ks += wd
                        while pend:
                            pend.pop(0)()
                    E("dve", "tensor_tensor", [SK, "cbias"], [SK], out=Sv[:, n - 256:n], in0=Sv[:, n - 256:n], in1=cbias, op=ALU.add)
                    if n <= 256:
                        E("dve", "tensor_scalar", [SK], ["M"], out=Mv[:, 0:n], in0=Sv[:, 0:n], scalar1=-0.5 * BIG, scalar2=None,
                          op0=ALU.is_ge)
                    elif n <= 512:
                        for r in range(32):
                            src = Sv if r == 0 else wrk
                            E("dve", "max", [SK if r == 0 else "wrk"], ["m8"], out=m8, in_=src[:, 0:n])
                            if r < 31:
                                E("dve", "match_replace", ["m8", SK if r == 0 else "wrk"], ["wrk"], out=wrk[:, 0:n], in_to_replace=m8,
                                  in_values=src[:, 0:n], imm_value=-BIG)
                            if r % 4 == 3:
                                yield 8 * (n / 960.0 + 0.3)
                        E("dve", "tensor_scalar", ["m8"], ["thr"], out=thr, in0=m8[:, 7:8], scalar1=-0.5 * BIG, scalar2=None, op0=ALU.max)
                        E("dve", "tensor_scalar", [SK, "thr"], ["M"], out=Mv[:, 0:n], in0=Sv[:, 0:n], scalar1=thr[:, 0:1], scalar2=None,
                          op0=ALU.is_ge)
                    else:
                        m_ = n // 16
                        sub = wrk[:, 0:m_]
                        tmp2 = wrk[:, 256:256 + m_]
                        E("dve", "tensor_copy", [SK], ["wrk"], out=sub,
                          in_=Sv[:, 0:n].rearrange("p (a s) -> p a s", s=16)[:, :, 0])
                        E("dve", "tensor_scalar", ["wrk"], ["wrk2"], out=tmp2, in0=sub, scalar1=-0.5 * BIG, scalar2=2.0 * BIG,
                          op0=ALU.is_lt, op1=ALU.mult)
                        E("dve", "tensor_tensor", ["wrk", "wrk2"], ["wrk2"], out=tmp2, in0=tmp2, in1=sub, op=ALU.add)
                        E("dve", "tensor_reduce", ["wrk2"], ["fb"], out=FB[:, 0:1], in_=tmp2, axis=AX.X, op=ALU.min)
                        E("dve", "tensor_scalar", ["qrel2"], ["fb"], out=FB[:, 1:2], in0=qrel2, scalar1=float((8 * j + 2 * i) * 128 + 1),
                          scalar2=None, op0=ALU.add)
                        E("dve", "tensor_copy", ["fb"], ["nvv"], out=nvv, in_=FB[:, 1:2])
                        for r in range(6):
                            E("dve", "max", ["wrk"], ["u8"], out=u8[:, r * 8:(r + 1) * 8], in_=sub)
                            if r < 5:
                                E("dve", "match_replace", ["u8", "wrk"], ["wrk"], out=sub, in_to_replace=u8[:, r * 8:(r + 1) * 8],
                                  in_values=sub, imm_value=-BIG)
                        E("dve", "tensor_tensor", ["u8", "fb"], ["LO"], out=LO[:, 0:1], in0=u8[:, 31:32], in1=FB[:, 0:1], op=ALU.max)
                        E("dve", "tensor_tensor", ["u8", "fb"], ["fb"], out=FB[:, 0:1], in0=u8[:, 47:48], in1=FB[:, 0:1], op=ALU.max)
                        E("dve", "tensor_scalar", ["fb"], ["fb"], out=FB[:, 1:2], in0=FB[:, 1:2], scalar1=768.0, scalar2=None, op0=ALU.min)
                        E("dve", "tensor_scalar", [SK, "LO"], ["M", "LO"], out=Mv[:, 0:n], in0=Sv[:, 0:n], scalar1=LO[:, 0:1], scalar2=None,
                          op0=ALU.is_ge, op1=ALU.add, accum_out=LO[:, 1:2])
                        E("dve", "tensor_copy", ["u8"], ["HI"], out=HI[:, 0:1], in_=u8[:, 3:4])
                        E("dve", "memset", [], ["HI"], HI[:, 1:2], 64.0)
                        E("dve", "tensor_scalar", ["LO"], ["g2"], out=g2, in0=LO[:, 1:2].to_broadcast([128, 2]), scalar1=256.0, scalar2=None,
                          op0=ALU.is_lt)
                        E("dve", "copy_predicated", ["g2", "LO", "HI"], ["HI"], out=HI, mask=g2, data=LO)
                        E("dve", "copy_predicated", ["g2", "fb", "LO", "HI"], ["LO"], out=LO, mask=g2, data=FB)
                        yield n / 960.0 + 6.0
                        for it in range(7):
                            E("dve", "tensor_tensor", ["LO", "HI"], ["ta"], out=ta, in0=LO[:, 1:2], in1=HI[:, 1:2], op=ALU.subtract)
                            E("dve", "reciprocal", ["ta"], ["ta"], out=ta, in_=ta)
                            E("dve", "scalar_tensor_tensor", ["LO", "ta"], ["tf"], out=tf, in0=LO[:, 1:2], scalar=-259.5, in1=ta,
                              op0=ALU.add, op1=ALU.mult)
                            E("dve", "tensor_scalar", ["tf"], ["tf"], out=tf, in0=tf, scalar1=0.03, scalar2=0.97, op0=ALU.max, op1=ALU.min)
                            E("dve", "tensor_tensor", ["LO", "HI"], ["td"], out=td, in0=HI[:, 0:1], in1=LO[:, 0:1], op=ALU.subtract)
                            E("dve", "scalar_tensor_tensor", ["td", "tf", "LO"], ["TC"], out=TC[:, 0:1], in0=td, scalar=tf[:, 0:1], in1=LO[:, 0:1],
                              op0=ALU.mult, op1=ALU.add)
                            E("dve", "tensor_scalar", [SK, "TC"], ["M", "TC"], out=Mv[:, 0:n], in0=Sv[:, 0:n], scalar1=TC[:, 0:1], scalar2=None,
                              op0=ALU.is_ge, op1=ALU.add, accum_out=TC[:, 1:2])
                            E("dve", "tensor_scalar", ["TC"], ["g2"], out=g2, in0=TC[:, 1:2].to_broadcast([128, 2]), scalar1=256.0,
                              scalar2=None, op0=ALU.is_ge)
                            E("dve", "tensor_scalar", ["TC"], ["ng2"], out=ng2, in0=TC[:, 1:2].to_broadcast([128, 2]), scalar1=256.0,
                              scalar2=None, op0=ALU.is_lt)
                            E("dve", "copy_predicated", ["g2", "TC", "LO"], ["LO"], out=LO, mask=g2, data=TC)
                            E("dve", "copy_predicated", ["ng2", "TC", "HI"], ["HI"], out=HI, mask=ng2, data=TC)
                            yield n / 960.0 + 2.2
                        E("dve", "tensor_scalar", [SK, "LO"], ["M"], out=Mv[:, 0:n], in0=Sv[:, 0:n], scalar1=LO[:, 0:1], scalar2=-BIG,
                          op0=ALU.is_lt, op1=ALU.mult)
                        nh_ = (n // 2) // 64 * 64
                        E("pool", "tensor_tensor", ["M", SK], [("wrkh", 1)], out=wrk[:, nh_:n], in0=Mv[:, nh_:n], in1=Sv[:, nh_:n], op=ALU.subtract)
                        E("dve", "tensor_tensor", ["M", SK], [("wrkh", 0)], out=wrk[:, 0:nh_], in0=Mv[:, 0:nh_], in1=Sv[:, 0:nh_], op=ALU.subtract)
                        yield 2.0 * n / 960.0 + 1.0
                        E("dve", "max", ["wrk", ("wrkh", 0), ("wrkh", 1)], ["m8", "wrk"], out=m8, in_=wrk[:, 0:n])
                        E("dve", "tensor_scalar", ["LO"], ["te"], out=te, in0=LO[:, 1:2], scalar1=-256.0, scalar2=7.0, op0=ALU.add, op1=ALU.min)
                        E("dve", "tensor_scalar", ["te", "io8"], ["oh8"], out=oh8, in0=io8, scalar1=te[:, 0:1], scalar2=None, op0=ALU.is_equal)
                        E("dve", "tensor_tensor", ["oh8", "m8"], ["oh8"], out=oh8, in0=oh8, in1=m8, op=ALU.mult)
                        E("dve", "tensor_reduce", ["oh8"], ["thr"], out=thr, in_=oh8, axis=AX.X, op=ALU.add)
                        E("dve", "tensor_scalar", ["thr"], ["thr"], out=thr, in0=thr, scalar1=-1.0, scalar2=None, op0=ALU.mult)
                        E("dve", "tensor_tensor", ["LO", "HI"], ["ta"], out=ta, in0=LO[:, 1:2], in1=HI[:, 1:2], op=ALU.add)
                        E("dve", "tensor_scalar", ["ta", "g2"], ["g2"], out=g2[:, 0:1], in0=ta, scalar1=519.0, scalar2=None, op0=ALU.is_gt)
                        E("dve", "copy_predicated", ["g2", "thr", "HI"], ["thr"], out=thr, mask=g2[:, 0:1], data=HI[:, 0:1])
                        E("dve", "tensor_scalar", ["nvv", "g2"], ["g2"], out=g2[:, 0:1], in0=nvv, scalar1=256.5, scalar2=None, op0=ALU.is_lt)
                        E("dve", "copy_predicated", ["g2", "thr", "negbig"], ["thr"], out=thr, mask=g2[:, 0:1], data=negbig)
                        E("dve", "tensor_scalar", [SK, "thr"], ["M"], out=Mv[:, 0:n], in0=Sv[:, 0:n], scalar1=thr[:, 0:1], scalar2=None,
                          op0=ALU.is_ge)
                    yield 2.0 * n / 960.0 + 2.0

                    def mtrans(i=i, nkt=nkt):
                        g0 = 0
                        while g0 < nkt:
                            cnt = min(4, nkt - g0)
                            tbk = 0
                            ntr[0] += 1
                            for a in range(cnt):
                                MM(PS[tbk][:, a * 128:(a + 1) * 128], Mv[:, (g0 + a) * 128:(g0 + a + 1) * 128], ident, ["M", "ident"], [psk(tbk)])
                            E("act", "activation", [psk(tbk)], [("mT", BUF2[j], i)], out=maskT[:, g0:g0 + cnt, i * 128:(i + 1) * 128],
                              in_=PS[tbk][:, 0:cnt * 128].rearrange("p (a q) -> p a q", a=cnt), func=AF.Copy)
                            g0 += cnt

                    pend.append(mtrans)
                for f_ in pend:
                    f_()
                yield 1.0

            def attn_gen(j, sb, ob, bcb, mengs=("pool",)):
                maskT = mTp[BUF2[j]]
                QTb = QTb2[BUF2[j]]
                MTK = [("mT", BUF2[j], i) for i in range(4)]
                nsb = len(sb)
                la = nsb - 1
                for g in range(8):
                    r0 = (g // 4) * 64
                    a = g % 4
                    kvh = g // 4
                    steps = [(kt, col_lo(max(kt - 8 * j, 0))) for kt in range(8 * j + 8)]

                    def Sstep(s):
                        kt, cl = steps[s]
                        bk = sb[s % nsb]
                        MM(PS[bk][:, cl:512], KTz[kvh][:, kt * 128:(kt + 1) * 128], QTb[:, a, cl:512],
                           KTbK + [("QTb", BUF2[j], a)], [psk(bk)])

                    for s0 in range(min(la, len(steps))):
                        Sstep(s0)
                    for s, (kt, cl) in enumerate(steps):
                        if s + la < len(steps):
                            Sstep(s + la)
                        bk = sb[s % nsb]
                        pt = PT[s % 4]
                        pk = ("PT", s % 4)
                        E("act", "activation", [psk(bk)], [pk], out=pt[:, cl:512], in_=PS[bk][:, cl:512], func=AF.Exp, scale=0.125)
                        E(mengs[s % len(mengs)], "tensor_tensor", [pk] + MTK, [pk], out=pt[:, cl:512], in0=pt[:, cl:512],
                          in1=maskT[:, kt, cl:512], op=ALU.mult)

                        def PVstep(s2):
                            kt2, cl2 = steps[s2]
                            MM(PS[ob][0:65, cl2:512], Vb[:, kt2, kvh * 65:(kvh + 1) * 65], PT[s2 % 4][:, cl2:512], [("PT", s2 % 4), "Vb"],
                               [psk(ob)], start=(s2 == 0), stop=(s2 == len(steps) - 1))

                        if s >= 2:
                            PVstep(s - 2)
                        yield 1.0
                    for s2 in range(max(0, len(steps) - 2), len(steps)):
                        PVstep(s2)
                    E("act", "activation", [psk(ob)], ["rd"], out=rd[64:65, :], in_=PS[ob][64:65, :], func=AF.Ln)
                    E("act", "activation", [psk(ob)], ["osb"], out=osb[0:64, :], in_=PS[ob][0:64, :], func=AF.Copy)
                    MM(PS[bcb][0:64, :], ones_f[64:65, 0:64], rd[64:65, :], ["ones_f", "rd"], [psk(bcb)])
                    E("act", "activation", [psk(bcb)], ["bcs"], out=bcs[0:64, :], in_=PS[bcb][0:64, :], func=AF.Exp, scale=-1.0)
                    E("pool", "tensor_tensor", ["osb", "bcs"], [("OTb", j, g)],
                      out=OTb[(g % 2) * 64:(g % 2) * 64 + 64, g // 2, j * 512:(j + 1) * 512], in0=osb[0:64, :], in1=bcs[0:64, :], op=ALU.mult)
                    yield 3.0

            def run_gen(gen):
                for _ in gen:
                    pass

            def interleave(ga, gb):
                ta = tb = 0.0
                ea = eb = False
                while not (ea and eb):
                    if not ea and (eb or ta <= tb):
                        try:
                            ta += next(ga) or 1.0
                        except StopIteration:
                            ea = True
                    else:
                        try:
                            tb += next(gb) or 1.0
                        except StopIteration:
                            eb = True

            def n_idx(j):
                tot = 0
                for i in range(4):
                    n = (8 * j + 2 * i + 2) * 128
                    tot += 4 * ((n + 511) // 512)
                    tot += 0 if n <= 256 else (8 if n <= 512 else 9)
                    tot += 2
                return tot

            start2(ORDER2[0])
            run_gen(idx_gen(ORDER2[0]))
            for pos in range(4):
                j = ORDER2[pos]
                if pos + 1 < 4:
                    jn = ORDER2[pos + 1]
                    start2(jn)
                    interleave(attn_gen(j, [6, 1], 7, 6), idx_gen(jn))
                else:
                    run_gen(attn_gen(j, [0, 1, 2], 6, 7, mengs=("dve", "pool")))
            p.barrier()
            p.alias = {}
            tap("OTb", vw(o_OTb, 4096, BF16), [128, 8192], [("OTb", j, g) for j in range(4) for g in range(8)], BF16)

        if stop_after >= 3:
            ar = Bump(ARENA, NW)
            o_WQa = ar(1024); WQa = vw(o_WQa, 1024, BF16).rearrange("p (k n) -> p k n", k=8)
            o_wuq = ar(768); wuq = vw(o_wuq, 768, BF16).rearrange("p (k n) -> p k n", k=2)
            o_wukv = ar(512); wukv = vw(o_wukv, 512, BF16)
            o_cm = ar(2048); cmask = vw(o_cm, 2048, BF16).rearrange("p (m q) -> p m q", m=8)
            o_KTa = ar(8192); KTa = vw(o_KTa, 8192, BF16).rearrange("p (h s) -> p h s", h=4)
            o_Va = ar(4160); Va = vw(o_Va, 4160, BF16).rearrange("p (t c) -> p t c", c=260)
            Va4 = vw(o_Va, 4160, BF16).rearrange("p (t h c) -> p t h c", h=4, c=65)
            o_hT = ar(2048); hT = vw(o_hT, 2048, BF16).rearrange("p (k n) -> p k n", k=8)
            o_QTa = ar(1024); QTa = vw(o_QTa, 1024, BF16).rearrange("p (h q) -> p h q", h=4)
            o_junk = ar(512); junk = vw(o_junk, 512, BF16)
            o_cqT = ar(512); cqT = vw(o_cqT, 512, BF16).rearrange("p (f q) -> p f q", f=2)
            o_sqb2 = ar(512); sqb2 = vw(o_sqb2, 512, BF16).rearrange("p (f q) -> p f q", f=2)
            PT = [vw(ar(256), 256, BF16) for _ in range(4)]
            o_rd = ar(512); rd = vw(o_rd, 512)
            o_bcs = ar(512); bcs = vw(o_bcs, 512)
            R0 = ar.o
            rs_ = Bump(R0, NW)
            o_hb = rs_(2048); hb = vw(o_hb, 2048, BF16).rearrange("p (t d) -> p t d", t=4)
            o_xs3 = rs_(4096); xs3 = vw(o_xs3, 4096).rearrange("p (t d) -> p t d", t=4)
            sqa3 = vw(o_xs3, 1536).rearrange("p (n d) -> p n d", d=96)
            Qa = vw(o_xs3 + 1536, 768, BF16).rearrange("p (t n) -> p t n", t=4)
            o_Wqa = rs_(1536); Wqa3 = vw(o_Wqa, 1536).rearrange("p (n d) -> p n d", d=96)
            Wqa4 = vw(o_Wqa, 1536).rearrange("p (t h d) -> p t h d", t=4, h=4)
            Wqat = vw(o_Wqa, 1536).rearrange("p (t n) -> p t n", t=4)
            rta = [vw(o_xs3 + 256 * k_, 256) for k_ in range(4)]
            o_rta1 = rs_(1024)
            rk_ = Bump(R0, NW)
            KSET = []
            KSQ = []
            for par_ in range(2):
                o_w_ = rk_(1536); o_s_ = rk_(1536); o_k_ = rk_(768)
                KSQ.append(o_s_)
                KSET.append((vw(o_w_, 1536).rearrange("p (n d) -> p n d", d=96),
                             vw(o_w_, 1536).rearrange("p (t h d) -> p t h d", t=4, h=4),
                             vw(o_s_, 1536).rearrange("p (n d) -> p n d", d=96),
                             vw(o_k_, 768, BF16).rearrange("p (t n) -> p t n", t=4),
                             [vw(o_s_ + 256 * k_, 256) for k_ in range(4)],
                             o_w_))

            load_w(WQa, w_in_v[:, :, CQ[0]:CQ[1]], "wqa", "WQa")
            load_w(wuq, w_uq.rearrange("(k p) n -> p k n", p=128), "wuq", "wuq")
            load_w(wukv, w_ukv, "wukv", "wukv")
            kr = vw(o_small + 130, 8)
            kri = vw(o_small + 140, 8, I32)
            E("pool", "iota", [], ["kri"], kri, pattern=[[128, 8]], base=0, channel_multiplier=1)
            E("dve", "tensor_copy", ["kri"], ["kr"], out=kr, in_=kri)
            for m in range(8):
                E("dve", "tensor_scalar", ["qrel", "kr"], ["cmask"], out=cmask[:, m, :], in0=qrel, scalar1=kr[:, m:m + 1], scalar2=None,
                  op0=ALU.is_ge)

            QTa2 = [QTa, vw(o_rta1, 1024, BF16).rearrange("p (h q) -> p h q", h=4)]

            def kexp_gen(hg, par):
                W3_, W4_, sq_, Kb_, rt_, o_w = KSET[par]
                bA = (4, 5) if par == 0 else (6, 7)
                bT = (0, 1) if par == 0 else (2, 3)
                sfx = "k%d" % par
                Wkey = "Wk%d" % par
                for c in range(par, 8, 2):
                    for t in range(4):
                        T_ = 4 * c + t
                        b = bA[t % 2]
                        MM(PS[b][:], ckvT[:, T_ * 128:(T_ + 1) * 128], wukv[:, hg * 512:(hg + 1) * 512], [("ckvT", c), "wukv"], [psk(b)])
                        pv = PS[b][:].rearrange("p (h d) -> p h d", h=4)
                        E("act", "activation", [psk(b), ("rkv", c)], [Wkey], out=W4_[:, t, :, 0:64], in_=pv[:, :, 0:64], func=AF.Identity,
                          scale=rkv[:, T_:T_ + 1])
                        E("dve", "tensor_scalar", [psk(b), ("rkv", c)], ["Va"], out=Va4[:, T_, :, 0:64], in0=pv[:, :, 64:128],
                          scalar1=rkv[:, T_:T_ + 1], scalar2=None, op0=ALU.mult)
                    E("pool", "tensor_copy", [("kpe", c), Wkey], [Wkey], out=W4_[:, :, :, 64:96],
                      in_=kpe[:, c * 4:(c + 1) * 4, :].unsqueeze(2).to_broadcast([128, 4, 4, 32]))
                    yield 5.0
                    ssv_ = ssn[:, par * 16:par * 16 + 16]
                    rsv_ = rsn[:, par * 16:par * 16 + 16]
                    sq64 = vw(KSQ[par], 1024).rearrange("p (n d) -> p n d", d=64)
                    E("act", "activation", [Wkey], ["sq" + sfx], out=sq64, in_=W3_[:, :, 0:64], func=AF.Square)
                    E("dve", "tensor_reduce", ["sq" + sfx], ["ssn" + sfx], out=ssv_, in_=sq64, axis=AX.X, op=ALU.add)
                    E("dve", "tensor_tensor", ["ssn" + sfx, ("kss", c)], ["ssn" + sfx], out=ssv_.rearrange("p (t h) -> p t h", t=4),
                      in0=ssv_.rearrange("p (t h) -> p t h", t=4), in1=kss[:, c * 4:(c + 1) * 4].unsqueeze(2).to_broadcast([128, 4, 4]),
                      op=ALU.add)
                    E("act", "activation", ["ssn" + sfx], ["rsn" + sfx], out=rsv_, in_=ssv_, func=AF.Ln, scale=1.0 / 96, bias=EPS)
                    E("act", "activation", ["rsn" + sfx], ["rsn" + sfx], out=rsv_, in_=rsv_, func=AF.Exp, scale=-0.5)
                    E("dve", "tensor_tensor", [Wkey, "rsn" + sfx], [Wkey], out=W3_, in0=W3_, in1=rsv_.unsqueeze(2).to_broadcast([128, 16, 96]),
                      op=ALU.mult)
                    E("dve", "tensor_tensor", [Wkey, "gk"], [Wkey], out=W3_[:, :, 0:64], in0=W3_[:, :, 0:64],
                      in1=gk[:, 0:64].unsqueeze(1).to_broadcast([128, 16, 64]), op=ALU.mult)
                    yield 10.0
                    E("act", "activation", [Wkey], ["Kb16" + sfx], out=Kb_, in_=vw(o_w, 1536).rearrange("p (t n) -> p t n", t=4), func=AF.Copy)
                    for hl in range(4):
                        b = bT[hl % 2]
                        for t in range(4):
                            MM(PS[b][0:96, t * 128:(t + 1) * 128], Kb_[:, t, hl * 96:(hl + 1) * 96], ident, ["Kb16" + sfx, "ident"], [psk(b)])
                        if hl % 2 == 0:
                            E("act", "activation", [psk(b)], [("KTa", c)], out=KTa[0:96, hl, c * 512:(c + 1) * 512], in_=PS[b][0:96, :],
                              func=AF.Copy)
                        else:
                            E("dve", "tensor_copy", [psk(b)], [("KTa", c)], out=KTa[0:96, hl, c * 512:(c + 1) * 512], in_=PS[b][0:96, :])
                    yield 5.0

            XS3_ALIAS = ["sq", "Qa"] + [("rt", k) for k in range(4)]

            def start3_gen(hg, j):
                QTb_ = QTa2[j % 2]
                make_hT(xo[j * 512:(j + 1) * 512, :], xs3, "xs3", hb, hT, junk, "xs3", tb=(4, 5), alias_keys=XS3_ALIAS, do_load=False)
                yield 14.0
                for f in range(2):
                    for kc in range(8):
                        MM(PS[6 + f][:], WQa[:, kc, f * 128:(f + 1) * 128], hT[:, kc, :], ["WQa", ("hT", kc)], [psk(6 + f)],
                           start=(kc == 0), stop=(kc == 7))
                    E("dve", "tensor_scalar", [psk(6 + f), "gqn"], [("cqT", f)], out=cqT[:, f, :], in0=PS[6 + f][:],
                      scalar1=gqn[:, f:f + 1], scalar2=None, op0=ALU.mult)
                    E("act", "activation", [psk(6 + f)], [("sqb2", f)], out=sqb2[:, f, :], in_=PS[6 + f][:], func=AF.Square)
                yield 6.0
                for t in range(4):
                    for f in range(2):
                        MM(PS[6][:, 2 * t:2 * t + 2], sqb2[:, f, t * 128:(t + 1) * 128], ones_b[:, 0:2], [("sqb2", f), "ones_b"], [psk(6)],
                           start=(f == 0), stop=(f == 1))
                E("act", "activation", [psk(6)], ["rq4"], out=rq4, in_=PS[6][:, 0:8].rearrange("p (t two) -> p t two", two=2)[:, :, 0],
                  func=AF.Ln, scale=1.0 / 256, bias=EPS)
                E("act", "activation", ["rq4"], ["rq4"], out=rq4, in_=rq4, func=AF.Exp, scale=-0.5)
                for t in range(4):
                    b = 6 + (t % 2)
                    for f in range(2):
                        MM(PS[b][:, 0:384], cqT[:, f, t * 128:(t + 1) * 128], wuq[:, f, hg * 384:(hg + 1) * 384],
                           [("cqT", f), "wuq"], [psk(b)], start=(f == 0), stop=(f == 1))
                    E("act", "activation", [psk(b), "rq4"], ["Wqa"], out=Wqat[:, t, :], in_=PS[b][:, 0:384], func=AF.Identity,
                      scale=rq4[:, t:t + 1])
                yield 6.0
                head_prep(Wqa3, Wqa4, 16, 96, "Wqa", sqa3, ssn[:, 0:16], rsn[:, 0:16], gq, "gq", (64, 32),
                          cosO[:, j * 4:(j + 1) * 4, 0:16], sinO[:, j * 4:(j + 1) * 4, 0:16], ROPE_O, 4, 4, rta)
                yield 16.0
                E("act", "activation", ["Wqa", "sq"], ["Qa"], out=Qa, in_=Wqat, func=AF.Copy)
                for hl in range(4):
                    b = 4 + hl % 2
                    for t in range(4):
                        MM(PS[b][0:96, t * 128:(t + 1) * 128], Qa[:, t, hl * 96:(hl + 1) * 96], ident, ["Qa", "ident"], [psk(b)])
                    if hl % 2 == 0:
                        E("act", "activation", [psk(b)], [("QTa", j % 2, hl)], out=QTb_[0:96, hl, :], in_=PS[b][0:96, :], func=AF.Copy)
                    else:
                        E("dve", "tensor_copy", [psk(b)], [("QTa", j % 2, hl)], out=QTb_[0:96, hl, :], in_=PS[b][0:96, :])
                if j + 1 < 4:
                    load_x(xo[(j + 1) * 512:(j + 2) * 512, :], xs3, "xs3", "xs3", XS3_ALIAS)
                yield 6.0

            def attn3_gen(hg, j, sb):
                QTb_ = QTa2[j % 2]
                nsb = len(sb)
                la = nsb - 1
                for hl in range(4):
                    g = hg * 4 + hl
                    steps = [(kt, col_lo(max(kt - 8 * j, 0))) for kt in range(8 * j + 8)]
                    ob = 2 if nsb == 2 else 4

                    def Sstep(s):
                        kt, cl = steps[s]
                        bk = sb[s % nsb]
                        MM(PS[bk][:, cl:512], KTa[0:96, hl, kt * 128:(kt + 1) * 128], QTb_[0:96, hl, cl:512],
                           KTaK + [("QTa", j % 2, hl)], [psk(bk)])

                    def PVstep(s2):
                        kt2, cl2 = steps[s2]
                        MM(PS[ob][0:65, cl2:512], Va[:, kt2, hl * 65:(hl + 1) * 65], PT[s2 % 4][:, cl2:512], [("PT", s2 % 4), "Va"],
                           [psk(ob)], start=(s2 == 0), stop=(s2 == len(steps) - 1))

                    for s0 in range(min(la, len(steps))):
                        Sstep(s0)
                    for s, (kt, cl) in enumerate(steps):
                        if s + la < len(steps):
                            Sstep(s + la)
                        bk = sb[s % nsb]
                        pt = PT[s % 4]
                        pk = ("PT", s % 4)
                        E("act", "activation", [psk(bk)], [pk], out=pt[:, cl:512], in_=PS[bk][:, cl:512], func=AF.Exp,
                          scale=float(96.0 ** -0.5))
                        m = kt - 8 * j
                        if m >= 0:
                            E("pool", "tensor_tensor", [pk, "cmask"], [pk], out=pt[:, cl:512],
                              in0=pt[:, cl:512], in1=cmask[:, m, cl:512], op=ALU.mult)
                        if s >= 2:
                            PVstep(s - 2)
                        yield 0.9
                    for s2 in range(max(0, len(steps) - 2), len(steps)):
                        PVstep(s2)
                    finalize(ob, OTa[(g % 2) * 64:(g % 2) * 64 + 64, g // 2, j * 512:(j + 1) * 512], ("OTa", j, g), rd, bcs, 3 if nsb == 2 else 6)
                    yield 3.0

            def run_gen3(gen):
                for _ in gen:
                    pass

            def interleave3(ga, gb):
                ta = tb = 0.0
                ea = eb = False
                while not (ea and eb):
                    if not ea and (eb or ta <= tb):
                        try:
                            ta += next(ga) or 1.0
                        except StopIteration:
                            ea = True
                    else:
                        try:
                            tb += next(gb) or 1.0
                        except StopIteration:
                            eb = True

            WGp = vw(o_cosA, 4096, BF16).rearrange("p (k n) -> p k n", k=8)
            assert o_ckvT == o_cosA + 2688 and o_kpe >= o_cosA + 4096
            KTaK = [("KTa", c) for c in range(8)]
            for hg in range(2):
                interleave3(kexp_gen(hg, 0), kexp_gen(hg, 1))
                if hg == 0:
                    E("pool", "memset", [], ["Va"], Va4[:, :, :, 64:65], 1.0)
                    tap("KTa", vw(o_KTa, 8192, BF16)[0:96, 0:4096], [96, 4096], KTaK, BF16)
                    tap("Va", vw(o_Va, 4160, BF16), [128, 8320], ["Va"], BF16)
                p.barrier()
                load_x(xo[0:512, :], xs3, "xs3", "xs3", XS3_ALIAS)
                run_gen3(start3_gen(hg, 0))
                for j in range(4):
                    if j + 1 < 4:
                        interleave3(attn3_gen(hg, j, [0, 1]), start3_gen(hg, j + 1))
                    else:
                        run_gen3(attn3_gen(hg, j, [0, 1, 2]))
                if hg == 1 and stop_after >= 4:
                    dead_keys = ["tabO"] + [("ckvT", c_) for c_ in range(8)]
                    DMA("pool", WGp[:, :, 0:512], w_in_v[:, :, GA[0]:GA[1]], "wg0", [], [("WG", 0)] + dead_keys)
                    DMA("pool", WGp[:, :, 512:1024], w_in_v[:, :, GB[0]:GB[1]], "wg1", [], [("WG", 1)] + dead_keys)
                p.barrier()
            tap("OTa", vw(o_OTa, 4096, BF16), [128, 8192], [("OTa", j, g) for j in range(4) for g in range(8)], BF16)

        if stop_after >= 4:
            ar = Bump(ARENA, NW)
            WG = vw(o_cosA, 4096, BF16).rearrange("p (k n) -> p k n", k=8)
            o_WM = ar(8192); WM = vw(o_WM, 8192, BF16).rearrange("p (k n) -> p k n", k=8)
            o_WBA = ar(2048); WBA = vw(o_WBA, 2048, BF16).rearrange("p (k n) -> p k n", k=4)
            o_WBD = ar(2048); WBD = vw(o_WBD, 2048, BF16).rearrange("p (k n) -> p k n", k=4)
            o_WO = ar(4096); WO = vw(o_WO, 4096, BF16).rearrange("p (k n) -> p k n", k=8)
            o_xs4 = ar(4096); xs4 = vw(o_xs4, 4096).rearrange("p (t d) -> p t d", t=4)
            o_hb = ar(2048); hb = vw(o_hb, 2048, BF16).rearrange("p (t d) -> p t d", t=4)
            o_junk = ar(512); junk = vw(o_junk, 512, BF16)
            sgt = [vw(ar(256), 256, BF16) for _ in range(2)]
            o_og = ar(2048); og = vw(o_og, 2048, BF16).rearrange("p (c q) -> p c q", c=8)
            o_sig = ar(512); sig = vw(o_sig, 512, BF16).rearrange("p (a q) -> p a q", a=2)
            tmpf = [vw(ar(512), 512) for _ in range(2)]
            yo = [vw(ar(1024), 1024) for _ in range(2)]
            o_hT = ar(2048); hT = vw(o_hT, 2048, BF16).rearrange("p (k n) -> p k n", k=8)
            o_mg = ar(2048); mg = vw(o_mg, 2048, BF16).rearrange("p (f q) -> p f q", f=8)

            load_w(WM[:, :, 0:1024], w_in_v[:, :, MA[0]:MA[1]], "wm0", ("WM", 0))
            load_w(WM[:, :, 1024:2048], w_in_v[:, :, MB[0]:MB[1]], "wm1", ("WM", 1))
            load_w(WBA, wba_d.rearrange("(k p) n -> p k n", p=128), "wba", "WBA")
            load_w(WBD, wbd_d.rearrange("(k p) n -> p k n", p=128), "wbd", "WBD")
            load_w(WO, wo_d.rearrange("(k p) n -> p k n", p=128), "wo", "WO")
            nst = 0
            for j in range(4):
                make_hT(xo[j * 512:(j + 1) * 512, :], xs4, "xs4", hb, hT, junk, "xs4")
                idx = 0
                for br in range(2):
                    OT = OTa if br == 0 else OTb
                    otk = [("OTa" if br == 0 else "OTb", j, g) for g in range(8)]
                    for f in range(4):
                        b = 4 + (idx % 2)
                        for kc in range(8):
                            MM(PS[b][:], WG[:, kc, br * 512 + f * 128:br * 512 + (f + 1) * 128], hT[:, kc, :],
                               [("WG", br), ("hT", kc)], [psk(b)], start=(kc == 0), stop=(kc == 7))
                        E("act", "activation", [psk(b)], [("sg", idx % 2)], out=sgt[idx % 2], in_=PS[b][:], func=AF.Silu)
                        E("dve" if idx % 2 == 0 else "pool", "tensor_tensor", [("sg", idx % 2)] + otk, [("og", br * 4 + f)],
                          out=og[:, br * 4 + f, :], in0=OT[:, f, j * 512:(j + 1) * 512], in1=sgt[idx % 2], op=ALU.mult)
                        idx += 1
                for f in range(8):
                    for a in range(2):
                        for kc in range(8):
                            MM(PS[4 + a][:], WM[:, kc, a * 1024 + f * 128:a * 1024 + (f + 1) * 128], hT[:, kc, :],
                               [("WM", a), ("hT", kc)], [psk(4 + a)], start=(kc == 0), stop=(kc == 7))
                        E("act", "activation", [psk(4 + a), "bm"], [("sig", a)], out=sig[:, a, :], in_=PS[4 + a][:], func=AF.Sigmoid,
                          bias=bm[:, a * 8 + f:a * 8 + f + 1])
                    for c4 in range(4):
                        MM(PS[6][:], WBA[:, c4, f * 128:(f + 1) * 128], og[:, c4, :], ["WBA", ("og", c4)], [psk(6)],
                           start=(c4 == 0), stop=(c4 == 3))
                    for c4 in range(4):
                        MM(PS[7][:], WBD[:, c4, f * 128:(f + 1) * 128], og[:, 4 + c4, :], ["WBD", ("og", 4 + c4)], [psk(7)],
                           start=(c4 == 0), stop=(c4 == 3))
                    E("dve", "tensor_tensor", [psk(6), ("sig", 0)], [("tmpf", 0)], out=tmpf[0], in0=PS[6][:], in1=sig[:, 0, :], op=ALU.mult)
                    E("dve", "tensor_tensor", [psk(7), ("sig", 1)], [("tmpf", 1)], out=tmpf[1], in0=PS[7][:], in1=sig[:, 1, :], op=ALU.mult)
                    E("pool", "tensor_tensor", [("tmpf", 0), ("tmpf", 1)], [("mg", f)], out=mg[:, f, :], in0=tmpf[0], in1=tmpf[1], op=ALU.add)
                for t in range(4):
                    sl = nst % 2
                    for hf in range(2):
                        b = hf
                        for f in range(8):
                            MM(PS[b][:], mg[:, f, t * 128:(t + 1) * 128], WO[:, f, hf * 512:(hf + 1) * 512], [("mg", f), "WO"], [psk(b)],
                               start=(f == 0), stop=(f == 7))
                        E("dve", "tensor_tensor", [psk(b), "xs4"], [("yo", sl)], out=yo[sl][:, hf * 512:(hf + 1) * 512], in0=PS[b][:],
                          in1=xs4[:, t, hf * 512:(hf + 1) * 512], op=ALU.add)
                    u = 4 * j + t
                    DMA("sp", y[u * 128:(u + 1) * 128, :], yo[sl], "st%d" % sl, [("yo", sl)], [])
                    nst += 1

        p.barrier()
        p.op("sp", None)
        p.emit(st)
    return nc, tap_out


def _prep_inputs(inputs):
    f32 = np.float32
    x = np.ascontiguousarray(inputs["x"], dtype=f32)
    pos = np.ascontiguousarray(inputs["positions"]).astype(np.int32)

    def pl(v, k):
        return np.ascontiguousarray(np.asarray(v, dtype=f32).reshape(k, 128).T)

    def rep(v):
        v = np.asarray(v, dtype=f32).reshape(1, -1)
        return np.ascontiguousarray(np.repeat(v, 128, axis=0))

    shared = {
        "ng": pl(inputs["norm_gain"][0], 8),
        "w_in": np.ascontiguousarray(inputs["w_in"][0], dtype=f32),
        "bm": np.ascontiguousarray(np.concatenate([pl(inputs["b_merge"][0, 0], 8), pl(inputs["b_merge"][0, 1], 8)], axis=1)),
        "gqn": pl(inputs["mla_q_norm"][0], 2),
        "w_uq": np.ascontiguousarray(inputs["mla_w_uq"][0], dtype=f32),
        "gkvn": pl(inputs["mla_kv_norm"][0], 1),
        "w_ukv": np.ascontiguousarray(inputs["mla_w_ukv"][0], dtype=f32),
        "gq": rep(inputs["mla_q_gain"][0]), "gk": rep(inputs["mla_k_gain"][0]),
        "gqd": rep(inputs["dsa_q_gain"][0]), "gkd": rep(inputs["dsa_k_gain"][0]),
        "wba": np.ascontiguousarray(inputs["w_branch_mla"][0], dtype=f32),
        "wbd": np.ascontiguousarray(inputs["w_branch_dsa"][0], dtype=f32),
        "wo": np.ascontiguousarray(inputs["w_out"][0], dtype=f32),
    }
    in_maps = []
    own_tiles = []
    for core in range(8):
        b, h = core // 2, core % 2
        tiles = [8 * j + 2 * i + h for j in range(4) for i in range(4)]
        own_tiles.append(tiles)
        xb = x[b]
        xo = np.ascontiguousarray(np.concatenate([xb[t * 128:(t + 1) * 128] for t in tiles], axis=0))
        pb = pos[b]
        posa = np.ascontiguousarray(pb.reshape(32, 128).T)
        poso = np.ascontiguousarray(np.stack([pb[t * 128:(t + 1) * 128] for t in tiles], axis=1))
        qrel = np.concatenate([(2 * i + h) * 128 + np.arange(128) for i in range(4)]).astype(f32)
        m = dict(shared)
        m.update({
            "xa": np.ascontiguousarray(xb), "xo": xo, "posa": posa.astype(np.int32), "poso": poso.astype(np.int32),
            "qrel": rep(qrel), "qrel2": (h * 128 + np.arange(128, dtype=f32)).reshape(128, 1).astype(f32),
        })
        in_maps.append(m)
    return in_maps, own_tiles


_CACHE = {}


def kernel(**inputs):
    in_maps, own_tiles = _prep_inputs(inputs)
    if "nc" not in _CACHE:
        _CACHE["nc"] = build_program()[0]
    nc = _CACHE["nc"]
    res = run_bass_kernel_spmd(nc, in_maps, core_ids=list(range(8)))
    out = np.empty((4, 4096, 1024), dtype=np.float32)
    for core in range(8):
        b = core // 2
        yv = np.asarray(res.results[core]["y"], dtype=np.float32)
        for u, t in enumerate(own_tiles[core]):
            out[b, t * 128:(t + 1) * 128, :] = yv[u * 128:(u + 1) * 128, :]
    return out
```

```python
import os
from contextlib import ExitStack

import numpy as np
import concourse.bass as bass
import concourse.mybir as mybir
from concourse.bass_utils import run_bass_kernel_spmd

F32 = mybir.dt.float32
BF16 = mybir.dt.bfloat16
I32 = mybir.dt.int32
ALU = mybir.AluOpType
AF = mybir.ActivationFunctionType
AX = mybir.AxisListType

THETA = 500000.0
EPS = 1e-6
BIG = 1.0e30
MAGIC = 12582912.0
TWO_PI = 6.283185307179586

CQ = (0, 256); CKV = (256, 384); KPE = (384, 416); GA = (416, 928); QB = (928, 1440)
KB = (1440, 1568); VB = (1568, 1696); GB = (1696, 2208); QI = (2208, 2464); KI = (2464, 2496)
WI = (2496, 2504); MA = (2504, 3528); MB = (3528, 4552)


class Op:
    __slots__ = ("eng", "fn", "deps", "odeps", "idx", "dma", "need", "ms", "dur", "region", "seq", "dval", "succ", "npred", "prio", "fin")

    def __init__(self, eng, fn, dma, dur, region, seq):
        self.eng = eng
        self.fn = fn
        self.deps = set()
        self.odeps = set()
        self.idx = 0
        self.dma = dma
        self.need = False
        self.ms = 0
        self.dur = dur
        self.region = region
        self.seq = seq
        self.dval = 0


class Prog:
    ENGS = ["pe", "act", "dve", "pool", "sp"]

    def __init__(self, nc, schedule=True):
        self.nc = nc
        self.ops = []
        self.lastw = {}
        self.readers = {}
        self.region = 0
        self.last_dma = {}
        self.schedule = schedule
        self.alias = {}

    def _expand(self, keys):
        out = []
        for k in keys:
            if isinstance(k, tuple) and k[0] == "psp":
                out.append(("ps", 2 * k[1]))
                out.append(("ps", 2 * k[1] + 1))
            else:
                out.append(k)
                for a_ in self.alias.get(k, ()):
                    out.append(a_)
        return out

    def op(self, eng, fn, reads=(), writes=(), dma=None, dur=0.3):
        reads = self._expand(reads)
        writes = self._expand(writes)
        o = Op(eng, fn, dma, dur, self.region, len(self.ops))
        deps = set()
        for k in reads:
            t = self.lastw.get(k)
            if t is not None:
                deps.add(t)
            if isinstance(k, tuple) and k[0] == "ps":
                for r in self.readers.get(k, ()):
                    deps.add(r)
        for k in writes:
            t = self.lastw.get(k)
            if t is not None:
                deps.add(t)
            for r in self.readers.get(k, ()):
                deps.add(r)
        for d in deps:
            if d.region != o.region:
                continue
            if d.eng == "pe" and eng == "pe" and d.dma is None:
                o.odeps.add(d)
            else:
                o.deps.add(d)
        if dma is not None:
            p_ = self.last_dma.get((eng, dma))
            if p_ is not None and p_.region == o.region:
                o.odeps.add(p_)
            self.last_dma[(eng, dma)] = o
        self.ops.append(o)
        for k in reads:
            self.readers.setdefault(k, []).append(o)
        for k in writes:
            self.lastw[k] = o
            self.readers[k] = []
        return o

    def barrier(self):
        self.region += 1

    def _schedule_region(self, ops):
        if not self.schedule or len(ops) < 3:
            return list(ops)
        LAT_X, LAT_S = 0.35, 0.12
        for o in ops:
            o.succ = []
            o.npred = 0
        for o in ops:
            for d in list(o.deps) + list(o.odeps):
                d.succ.append(o)
                o.npred += 1
        for o in reversed(ops):
            m = 0.0
            for s_ in o.succ:
                if s_.prio > m:
                    m = s_.prio
            o.prio = m + o.dur + LAT_X
        free = {e: 0.0 for e in self.ENGS}
        ready = [o for o in ops if o.npred == 0]
        out = []
        rt = {}
        for o in ready:
            rt[o] = 0.0
        while ready:
            best = None
            bs = None
            for o in ready:
                st = rt[o]
                if free[o.eng] > st:
                    st = free[o.eng]
                key = (st, -o.prio, o.seq)
                if bs is None or key < bs:
                    bs = key
                    best = o
            ready.remove(best)
            st = bs[0]
            extra = 0.0
            if best.dma is not None:
                extra = 2.0
                fin_issue = st + 0.1
                free[best.eng] = fin_issue
                best.fin = fin_issue + best.dur + extra
            else:
                best.fin = st + best.dur
                free[best.eng] = best.fin
            out.append(best)
            for s_ in best.succ:
                s_.npred -= 1
                lat = LAT_S if (s_.eng == best.eng and best in s_.odeps) else LAT_X
                t_ = (best.fin + lat) if best not in s_.odeps else (st + 0.01)
                if t_ > rt.get(s_, 0.0):
                    rt[s_] = t_
                if s_.npred == 0:
                    ready.append(s_)
        assert len(out) == len(ops), (len(out), len(ops))
        return out

    def emit(self, stack):
        nc = self.nc
        nreg = self.region + 1
        regions = [[] for _ in range(nreg)]
        for o in self.ops:
            regions[o.region].append(o)
        self.q = {e: [] for e in self.ENGS}
        prev_last = {}
        prev_dmas = []
        for r in range(nreg):
            order = self._schedule_region(regions[r])
            first = {}
            for o in order:
                if o.eng not in first:
                    first[o.eng] = o
            if r > 0:
                for e, o in first.items():
                    for e2, l2 in prev_last.items():
                        if e2 == e == "pe" and l2.dma is None:
                            continue
                        o.deps.add(l2)
                    for d in prev_dmas:
                        o.deps.add(d)
            last = {}
            for o in order:
                last[o.eng] = o
                self.q[o.eng].append(o)
            for e, l2 in prev_last.items():
                if e not in last:
                    last[e] = l2
            prev_last = last
            prev_dmas = prev_dmas + [o for o in order if o.dma is not None]
        sems = {}
        for e in self.ENGS:
            sems[("e", e)] = stack.enter_context(nc.semaphore("s_" + e))
        dnames = sorted({o.dma for o in self.ops if o.dma is not None}, key=str)
        for d in dnames:
            sems[("d", d)] = stack.enter_context(nc.semaphore("d_" + str(d)))
        dcnt = {}
        for e in self.ENGS:
            for i, o in enumerate(self.q[e]):
                o.idx = i
                if o.dma is not None:
                    dcnt[o.dma] = dcnt.get(o.dma, 0) + 16
                    o.dval = dcnt[o.dma]
        for o in self.ops:
            for d in o.deps:
                if d.dma is None and d.fn is not None:
                    d.need = True
        for e in self.ENGS:
            c = 0
            for o in self.q[e]:
                if o.need:
                    c += 1
                o.ms = c
        block = stack.enter_context(nc.Block())
        engobj = {"pe": "tensor", "act": "scalar", "dve": "vector", "pool": "gpsimd", "sp": "sync"}

        def run(e):
            def body(eng):
                waited = {}
                for o in self.q[e]:
                    ws = {}
                    for d in o.deps:
                        if d.dma is None:
                            if d.eng == e and d.idx > o.idx:
                                raise RuntimeError("same-engine dependency scheduled out of order")
                            key = ("e", d.eng)
                            val = d.ms
                        else:
                            key = ("d", d.dma)
                            val = d.dval
                        if val > ws.get(key, 0):
                            ws[key] = val
                    for key, val in ws.items():
                        if val > waited.get(key, 0):
                            eng.wait_ge(sems[key], val)
                            waited[key] = val
                    if o.fn is None:
                        continue
                    ins = o.fn(eng)
                    if o.dma is not None:
                        ins.then_inc(sems[("d", o.dma)], 16)
                    elif o.need:
                        ins.then_inc(sems[("e", e)], 1)

            getattr(block, engobj[e])(body)

        for e in self.ENGS:
            if self.q[e]:
                run(e)


class Bump:
    def __init__(self, base, limit):
        self.o = base
        self.limit = limit

    def __call__(self, words):
        o = self.o
        self.o += (int(words) + 15) // 16 * 16
        assert self.o <= self.limit, (self.o, self.limit)
        return o


def col_lo(m):
    if m <= 1:
        return 0
    return 128 * ((m - 1 + 1) // 2)


def build_program(stop_after=99, taps=(), P1C=8, P1S=99, SCHED=True):
    nc = bass.Bass("TRN2", target_bir_lowering=False)

    def din(name, shape, dt=F32):
        return nc.dram_tensor(name, shape, dt, kind="ExternalInput").ap()

    xa = din("xa", [4096, 1024]); xo = din("xo", [2048, 1024])
    posa_d = din("posa", [128, 32], I32); poso_d = din("poso", [128, 16], I32)
    qrel_d = din("qrel", [128, 512]); qrel2_d = din("qrel2", [128, 1])
    ng_d = din("ng", [128, 8]); w_in = din("w_in", [1024, 4552]); bm_d = din("bm", [128, 16])
    gqn_d = din("gqn", [128, 2]); w_uq = din("w_uq", [256, 768]); gkvn_d = din("gkvn", [128, 1])
    w_ukv = din("w_ukv", [128, 1024])
    gq_d = din("gq", [128, 96]); gk_d = din("gk", [128, 96]); gqd_d = din("gqd", [128, 64]); gkd_d = din("gkd", [128, 64])
    wba_d = din("wba", [512, 1024]); wbd_d = din("wbd", [512, 1024]); wo_d = din("wo", [1024, 1024])
    y = nc.dram_tensor("y", [2048, 1024], F32, kind="ExternalOutput").ap()
    tap_out = {}
    w_in_v = w_in.rearrange("(kc p) n -> p kc n", p=128)

    with ExitStack() as st:
        NW = 52480
        A = st.enter_context(nc.sbuf_tensor("A", [128, NW], F32))
        PSP = [st.enter_context(nc.psum_tensor("psp%d" % i, [128, 1024], F32)) for i in range(4)]
        PS = [PSP[i // 2][:, (i % 2) * 512:(i % 2 + 1) * 512] for i in range(8)]
        p = Prog(nc, schedule=SCHED)

        def vw(off, words, dt=F32):
            a = A[:, off:off + int(words)]
            return a if dt == F32 else a.bitcast(dt)

        def _fsz(ap):
            n = 1
            for d_ in ap.shape[1:]:
                n *= int(d_)
            return n

        def E(eng, meth, reads, writes, *a, **kw):
            ap = kw.get("out", kw.get("in_", a[0] if a else None))
            n = _fsz(ap) if ap is not None else 64
            if meth in ("max", "match_replace"):
                n = _fsz(kw.get("in_", kw.get("in_values")))
            if eng == "act":
                dur = 0.22 + n / 1400.0
            elif eng == "pool":
                dur = 0.35 + n / 480.0
            else:
                dur = 0.14 + n / 960.0
            return p.op(eng, lambda e: getattr(e, meth)(*a, **kw), reads, writes, dur=dur)

        def DMA(eng, out, in_, sem, reads, writes, **kw):
            n = _fsz(out)
            return p.op(eng, lambda e: e.dma_start(out=out, in_=in_, **kw), reads, writes, dma=sem, dur=1.0 + n * 4 / 1500.0)

        def MM(out, lhsT, rhs, reads, writes, start=True, stop=True):
            n = _fsz(rhs)
            return p.op("pe", lambda e: e.matmul(out, lhsT=lhsT, rhs=rhs, start=start, stop=stop), reads, writes,
                        dur=0.06 + n / 1500.0)

        ntap = [0]

        def tap(name, ap, shape, key, dt=F32):
            if name not in taps:
                return
            t = nc.dram_tensor("tap_" + name, list(shape), dt, kind="ExternalOutput").ap()
            tap_out[name] = t
            ntap[0] += 1
            DMA("sp", t, ap, "tap%d" % ntap[0], [key] if not isinstance(key, list) else key, [])

        bank_rr = [0]

        def psk(i):
            return ("ps", i)

        P = Bump(0, NW)
        o_ident = P(64); ident = vw(o_ident, 64, BF16)
        o_onesf = P(64); ones_f = vw(o_onesf, 64)
        o_onesb = P(1); ones_b = vw(o_onesb, 1, BF16)
        o_ng = P(8); ng = vw(o_ng, 8)
        o_gqn = P(2); gqn = vw(o_gqn, 2)
        o_gkvn = P(1); gkvn = vw(o_gkvn, 1)
        o_bm = P(16); bm = vw(o_bm, 16)
        o_gq = P(96); gq = vw(o_gq, 96)
        o_gk = P(96); gk = vw(o_gk, 96)
        o_gqd = P(64); gqd = vw(o_gqd, 64)
        o_gkd = P(64); gkd = vw(o_gkd, 64)
        o_invf = P(28); invf = vw(o_invf, 28)
        o_qrel = P(512); qrel = vw(o_qrel, 512)
        o_qrel2 = P(1); qrel2 = vw(o_qrel2, 1)
        o_cbias = P(256); cbias = vw(o_cbias, 256)
        o_rkv = P(32); rkv = vw(o_rkv, 32)
        o_small = P(256)
        o_OTa = P(4096); OTa = vw(o_OTa, 4096, BF16).rearrange("p (c t) -> p c t", c=4)
        o_OTb = P(4096); OTb = vw(o_OTb, 4096, BF16).rearrange("p (c t) -> p c t", c=4)
        o_dead4 = P.o
        o_cosA = P(896); cosA = vw(o_cosA, 896).rearrange("p (t f) -> p t f", f=28)
        o_sinA = P(896); sinA = vw(o_sinA, 896).rearrange("p (t f) -> p t f", f=28)
        o_cosO = P(448); cosO = vw(o_cosO, 448).rearrange("p (t f) -> p t f", f=28)
        o_sinO = P(448); sinO = vw(o_sinO, 448).rearrange("p (t f) -> p t f", f=28)
        o_ckvT = P(2048); ckvT = vw(o_ckvT, 2048, BF16)
        o_kpe = P(1024); kpe = vw(o_kpe, 1024).rearrange("p (t f) -> p t f", f=32)
        o_kss = P(32); kss = vw(o_kss, 32)
        o_dead4_end = P.o
        ARENA = P.o
        assert ARENA % 16 == 0

        ss4 = vw(o_small, 4); rs4 = vw(o_small + 4, 4)
        ssn = vw(o_small + 8, 32); rsn = vw(o_small + 40, 32)
        sgn = vw(o_small + 72, 32)
        rq4 = vw(o_small + 104, 4)
        m8 = vw(o_small + 112, 8)
        thr = vw(o_small + 120, 1)
        ssk = vw(o_small + 124, 4)
        LO = vw(o_small + 150, 2); HI = vw(o_small + 152, 2); TC = vw(o_small + 154, 2); FB = vw(o_small + 156, 2)
        ta = vw(o_small + 158, 1); tf = vw(o_small + 159, 1); td = vw(o_small + 160, 1); te = vw(o_small + 161, 1)
        negbig = vw(o_small + 162, 1)
        ss4b = vw(o_small + 224, 4); rs4b = vw(o_small + 228, 4); sskb = vw(o_small + 232, 4)
        g2 = vw(o_small + 164, 2, I32); ng2 = vw(o_small + 166, 2, I32)
        u8 = vw(o_small + 168, 48); nvv = vw(o_small + 236, 1)
        io8 = vw(o_small + 238, 8); oh8 = vw(o_small + 246, 8); io8i = vw(o_small + 216, 8, I32)

        tmpc = Bump(ARENA, NW)
        o_t0 = tmpc(1024); o_t1 = tmpc(1024); o_t2 = tmpc(1024); o_t3 = tmpc(1024)
        idi = vw(o_t0, 128, I32); idf = vw(o_t1, 128)
        E("pool", "iota", [], ["idi"], idi, pattern=[[1, 128]], base=0, channel_multiplier=-1)
        E("dve", "tensor_copy", ["idi"], ["idf"], out=idf, in_=idi)
        E("dve", "tensor_scalar", ["idf"], ["ident"], out=ident, in0=idf, scalar1=0.0, scalar2=None, op0=ALU.is_equal)
        E("dve", "memset", [], ["ones_f"], ones_f, 1.0)
        E("dve", "memset", [], ["ones_b"], ones_b, 1.0)
        fr = []
        for rot in (32, 16, 8):
            for j in range(rot // 2):
                fr.append(float(np.float32(THETA) ** np.float32(-(2.0 * j) / rot)))
        for j, f in enumerate(fr):
            E("dve" if j % 2 == 0 else "pool", "memset", [], [("invf", j)], invf[:, j:j + 1], f)
        INVF = [("invf", j) for j in range(len(fr))]
        for nm, dst, src in (("ng", ng, ng_d), ("gqn", gqn, gqn_d), ("gkvn", gkvn, gkvn_d), ("bm", bm, bm_d),
                             ("gq", gq, gq_d), ("gk", gk, gk_d), ("gqd", gqd, gqd_d), ("gkd", gkd, gkd_d),
                             ("qrel", qrel, qrel_d), ("qrel2", qrel2, qrel2_d)):
            DMA("sp", dst, src, "c_" + nm, [], [nm])
        E("pool", "iota", [], ["io8i"], io8i, pattern=[[1, 8]], base=0, channel_multiplier=0)
        E("dve", "tensor_copy", ["io8i"], ["io8"], out=io8, in_=io8i)
        E("dve", "memset", [], ["negbig"], negbig, -0.5 * BIG)
        kii = vw(o_t0 + 128, 256, I32)
        kio = vw(o_t1 + 128, 256)
        E("pool", "iota", [], ["kii"], kii, pattern=[[1, 256]], base=0, channel_multiplier=0)
        E("dve", "tensor_copy", ["kii"], ["kio"], out=kio, in_=kii)
        E("dve", "tensor_scalar", ["kio", "qrel2"], ["cbias"], out=cbias, in0=kio, scalar1=qrel2[:, 0:1], scalar2=-BIG,
          op0=ALU.is_gt, op1=ALU.mult)

        o_t4 = tmpc(1024); o_t5 = tmpc(1024)
        o_rsc = {"A": (o_t2, o_t3, o_t4, o_t5), "O": tuple(tmpc(1024) for _ in range(4))}

        def rope_tables(pos_d, ntile, cosT, sinT, nm):
            q2, q3, q4, q5 = o_rsc[nm]
            pi_ = vw(q2, ntile, I32)
            pf = vw(q2 + 64, ntile)
            ang = vw(q3, ntile * 28).rearrange("p (t f) -> p t f", f=28)
            uu = vw(q4, ntile * 28).rearrange("p (t f) -> p t f", f=28)
            kk = vw(q5, ntile * 28).rearrange("p (t f) -> p t f", f=28)
            DMA("sp", pi_, pos_d, "c_pos" + nm, [], ["pi" + nm])
            E("dve", "tensor_copy", ["pi" + nm], ["pf" + nm], out=pf, in_=pi_)
            E("dve", "tensor_tensor", ["pf" + nm] + INVF, ["ang" + nm], out=ang,
              in0=pf.unsqueeze(2).to_broadcast([128, ntile, 28]),
              in1=invf.unsqueeze(1).to_broadcast([128, ntile, 28]), op=ALU.mult)
            E("dve", "tensor_scalar", ["ang" + nm], ["ang" + nm], out=ang, in0=ang, scalar1=1.0 / TWO_PI, scalar2=None,
              op0=ALU.mult)
            for dst, shift in ((sinT, 0.0), (cosT, 0.25)):
                E("dve", "tensor_scalar", ["ang" + nm], ["uu" + nm], out=uu, in0=ang, scalar1=shift, scalar2=None, op0=ALU.add)
                E("dve", "tensor_scalar", ["uu" + nm], ["rk" + nm], out=kk, in0=uu, scalar1=MAGIC, scalar2=MAGIC,
                  op0=ALU.add, op1=ALU.subtract)
                E("dve", "tensor_tensor", ["uu" + nm, "rk" + nm], ["rk" + nm], out=kk, in0=uu, in1=kk, op=ALU.subtract)
                E("act", "activation", ["rk" + nm], ["tab" + nm], out=dst, in_=kk, func=AF.Sin, scale=TWO_PI * (1.0 - 1e-6))

        rope_tables(posa_d, 32, cosA, sinA, "A")
        rope_tables(poso_d, 16, cosO, sinO, "O")
        ROPE_A = ["tabA"]
        ROPE_O = ["tabO"]

        def load_x(src_rows, xs, xskey, semname, alias_keys=()):
            DMA("sp", xs, src_rows.rearrange("(t p) d -> p t d", p=128), semname, [], [xskey] + list(alias_keys))

        def make_hT(src_rows, xs, xskey, hb, hT, junk, semname, tb=(0, 1, 2, 3), hkey="hT", alias_keys=(), do_load=True,
                    sfx="", st4=None):
            ss4_, rs4_ = (ss4, rs4) if st4 is None else st4
            kj, kr = "junk" + sfx, "rs4" + sfx
            if do_load:
                load_x(src_rows, xs, xskey, semname, alias_keys)
            for t in range(4):
                E("act", "activation", [xskey], [kj, ("ss4" + sfx, t)], out=junk, in_=xs[:, t, :], func=AF.Square,
                  accum_out=ss4_[:, t:t + 1])
            E("act", "activation", [("ss4" + sfx, t) for t in range(4)], [kr], out=rs4_, in_=ss4_, func=AF.Ln,
              scale=1.0 / 1024, bias=EPS)
            E("act", "activation", [kr], [kr], out=rs4_, in_=rs4_, func=AF.Exp, scale=-0.5)
            for t in range(4):
                if t % 2 == 0:
                    E("dve", "tensor_scalar", [xskey, kr], [("hb" + sfx, t)], out=hb[:, t, :], in0=xs[:, t, :],
                      scalar1=rs4_[:, t:t + 1], scalar2=None, op0=ALU.mult)
                else:
                    E("pool", "tensor_scalar", [xskey, kr], [("hb" + sfx, t)], out=hb[:, t, :], in0=xs[:, t, :],
                      scalar1=rs4_[:, t:t + 1], scalar2=0.0, op0=ALU.mult, op1=ALU.add)
            for kc in range(8):
                b = tb[kc % len(tb)]
                for t in range(4):
                    MM(PS[b][:, t * 128:(t + 1) * 128], hb[:, t, kc * 128:(kc + 1) * 128], ident,
                       [("hb" + sfx, t), "ident"], [psk(b)])
                if kc % 2 == 0:
                    E("act", "activation", [psk(b), "ng"], [(hkey, kc)], out=hT[:, kc, :], in_=PS[b][:], func=AF.Identity,
                      scale=ng[:, kc:kc + 1])
                else:
                    E("dve", "tensor_scalar", [psk(b), "ng"], [(hkey, kc)], out=hT[:, kc, :], in0=PS[b][:],
                      scalar1=ng[:, kc:kc + 1], scalar2=None, op0=ALU.mult)

        HT_KEYS = [("hT", kc) for kc in range(8)]

        def head_prep(W3, W4, n, D, Wk, sq, ssv, rsv, gain, gkey, rope, cosv, sinv, tkey, T, H, rt, sfx=""):
            ksq, kss, krs = "sq" + sfx, "ssn" + sfx, "rsn" + sfx
            if W3 is not None:
                E("act", "activation", [Wk], [ksq], out=sq, in_=W3, func=AF.Square)
                E("dve", "tensor_reduce", [ksq], [kss], out=ssv, in_=sq, axis=AX.X, op=ALU.add)
                E("act", "activation", [kss], [krs], out=rsv, in_=ssv, func=AF.Ln, scale=1.0 / D, bias=EPS)
                E("act", "activation", [krs], [krs], out=rsv, in_=rsv, func=AF.Exp, scale=-0.5)
                E("dve", "tensor_tensor", [Wk, krs], [Wk], out=W3, in0=W3, in1=rsv.unsqueeze(2).to_broadcast([128, n, D]),
                  op=ALU.mult)
                E("dve", "tensor_tensor", [Wk, gkey], [Wk], out=W3, in0=W3, in1=gain.unsqueeze(1).to_broadcast([128, n, D]),
                  op=ALU.mult)
            if rope is not None:
                ro, r = rope
                r2 = r // 2
                x1 = W4[:, :, :, ro:ro + r2]
                x2 = W4[:, :, :, ro + r2:ro + r]
                c = cosv.unsqueeze(2).to_broadcast([128, T, H, r2])
                s = sinv.unsqueeze(2).to_broadcast([128, T, H, r2])
                t1, t2, t3, t4 = [rt[k][:, 0:T * H * r2].rearrange("p (t h d) -> p t h d", t=T, h=H) for k in range(4)]
                rk = [("rt" + sfx, k) for k in range(4)]
                E("dve", "tensor_tensor", [Wk] + tkey, [rk[0]], out=t1, in0=x1, in1=c, op=ALU.mult)
                E("dve", "tensor_tensor", [Wk] + tkey, [rk[1]], out=t2, in0=x2, in1=s, op=ALU.mult)
                E("dve", "tensor_tensor", [Wk] + tkey, [rk[2]], out=t3, in0=x2, in1=c, op=ALU.mult)
                E("dve", "tensor_tensor", [Wk] + tkey, [rk[3]], out=t4, in0=x1, in1=s, op=ALU.mult)
                E("dve", "tensor_tensor", [rk[0], rk[1]], [Wk], out=x1, in0=t1, in1=t2, op=ALU.subtract)
                E("dve", "tensor_tensor", [rk[2], rk[3]], [Wk], out=x2, in0=t3, in1=t4, op=ALU.add)

        def finalize(bank, dest, dkey, rd, bcs, bcbank):
            E("act", "activation", [psk(bank)], ["rd"], out=rd[64:65, :], in_=PS[bank][64:65, :], func=AF.Ln)
            MM(PS[bcbank][0:64, :], ones_f[64:65, 0:64], rd[64:65, :], ["ones_f", "rd"], [psk(bcbank)])
            E("act", "activation", [psk(bcbank)], ["bcs"], out=bcs[0:64, :], in_=PS[bcbank][0:64, :], func=AF.Exp, scale=-1.0)
            E("dve", "tensor_tensor", [psk(bank), "bcs"], [dkey], out=dest, in0=PS[bank][0:64, :], in1=bcs[0:64, :],
              op=ALU.mult)

        def load_w(dst, src, sem, key):
            DMA("pool", dst, src, sem, [], [key])

        p.barrier()
        ar = Bump(ARENA, NW)
        o_KTb = ar(4096)
        KTz = [vw(o_KTb, 2048, BF16), vw(o_KTb + 2048, 2048, BF16)]
        KTb = KTz[0]
        o_Vb = ar(2080); Vb = vw(o_Vb, 2080, BF16).rearrange("p (t c) -> p t c", c=130)
        Vb4 = vw(o_Vb, 2080, BF16).rearrange("p (t h c) -> p t h c", h=2, c=65)
        o_KTi = ar(2048); KTi = vw(o_KTi, 2048, BF16)
        DSAK_END = ar.o
        o_WK = ar(1792); WK = vw(o_WK, 1792, BF16).rearrange("p (k n) -> p k n", k=8)
        P1 = []
        for par_ in range(2):
            d_ = {}
            d_["xs"] = vw(ar(4096), 4096).rearrange("p (t d) -> p t d", t=4)
            d_["hb"] = vw(ar(2048), 2048, BF16).rearrange("p (t d) -> p t d", t=4)
            d_["hT"] = vw(ar(2048), 2048, BF16).rearrange("p (k n) -> p k n", k=8)
            d_["junk"] = vw(ar(512), 512, BF16)
            o_ = ar(1280); d_["Wt"] = vw(o_, 1280).rearrange("p (t n) -> p t n", t=4)
            o_ = ar(512); d_["o_kbw"] = o_
            d_["kbw3"] = vw(o_, 512).rearrange("p (n d) -> p n d", d=64)
            d_["kbw4"] = vw(o_, 512).rearrange("p (t h d) -> p t h d", t=4, h=2)
            d_["sq3"] = vw(ar(512), 512).rearrange("p (n d) -> p n d", d=64)
            d_["rt"] = [vw(ar(64), 64) for _ in range(4)]
            d_["Kb"] = vw(ar(256), 256, BF16).rearrange("p (t n) -> p t n", t=4)
            o_ = ar(192); d_["Ks"] = vw(o_, 192, BF16).rearrange("p (t n) -> p t n", t=4)
            d_["Ks5"] = vw(o_, 192, BF16).rearrange("p (t a d) -> p t a d", t=4, a=3)
            d_["kiw"] = vw(ar(128), 128).rearrange("p (t n) -> p t n", t=4)
            d_["sqb"] = vw(ar(256), 256, BF16)
            d_["sqp"] = vw(ar(128), 128).rearrange("p (t f) -> p t f", f=32)
            P1.append(d_)

        cols = [(CKV, 0), (KPE, 128), (KB, 160), (VB, 288), (KI, 416)]
        for i, ((c0, c1), o0) in enumerate(cols):
            load_w(WK[:, :, o0:o0 + (c1 - c0)], w_in_v[:, :, c0:c1], "wk%d" % i, ("WK", i))
        WKK = [("WK", i) for i in range(5)]
        tap("WK", vw(o_WK, 1792, BF16), [128, 3584], WKK, BF16)
        E("pool", "memset", [], ["Vb"], Vb[:, :, 64:65], 1.0)
        E("pool", "memset", [], ["Vb"], Vb[:, :, 129:130], 1.0)
        E("pool", "memset", [], ["KTz"], KTz[0][64:128, :], 0.0)
        E("pool", "memset", [], ["KTz"], KTz[1][0:64, :], 0.0)

        def p1_gen(par):
            d = P1[par]
            sx = "p%d" % par
            bA = 4 + 2 * par
            bB = 5 + 2 * par
            tb = (0, 1) if par == 0 else (2, 3)
            st4 = (ss4, rs4) if par == 0 else (ss4b, rs4b)
            ssk_ = ssk if par == 0 else sskb
            HK = "hT" + sx
            hT = d["hT"]; Wt = d["Wt"]; sqb = d["sqb"]; Kb = d["Kb"]; Ks = d["Ks"]; Ks5 = d["Ks5"]; kiw = d["kiw"]
            for c in range(par, P1C, 2):
                make_hT(xa[c * 512:(c + 1) * 512, :], d["xs"], "xs" + sx, d["hb"], hT, d["junk"], "xs%d" % par, tb=tb, hkey=HK, sfx=sx, st4=st4)
                yield 12.0
                for kc in range(8):
                    MM(PS[bA][:], WK[:, kc, 0:128], hT[:, kc, :], WKK + [(HK, kc)], [psk(bA)], start=(kc == 0), stop=(kc == 7))
                E("dve", "tensor_scalar", [psk(bA), "gkvn"], [("ckvT", c)], out=ckvT[:, c * 512:(c + 1) * 512], in0=PS[bA][:],
                  scalar1=gkvn[:, 0:1], scalar2=None, op0=ALU.mult)
                E("act", "activation", [psk(bA)], ["sqb" + sx], out=sqb, in_=PS[bA][:], func=AF.Square)
                for t in range(4):
                    MM(PS[bB][:, 2 * t:2 * t + 2], sqb[:, t * 128:(t + 1) * 128], ones_b[:, 0:2], ["sqb" + sx, "ones_b"], [psk(bB)])
                E("act", "activation", [psk(bB)], ["ssk" + sx], out=ssk_, in_=PS[bB][:, 0:8].rearrange("p (t two) -> p t two", two=2)[:, :, 0],
                  func=AF.Ln, scale=1.0 / 128, bias=EPS)
                E("act", "activation", ["ssk" + sx], [("rkv", c)], out=rkv[:, c * 4:(c + 1) * 4], in_=ssk_, func=AF.Exp, scale=-0.5)
                yield 5.0
                for t in range(4):
                    for kc in range(8):
                        MM(PS[bB][:, 0:320], hT[:, kc, t * 128:(t + 1) * 128], WK[:, kc, 128:448], WKK + [(HK, kc)], [psk(bB)],
                           start=(kc == 0), stop=(kc == 7))
                    E("act", "activation", [psk(bB)], [("Wt" + sx, t)], out=Wt[:, t, :], in_=PS[bB][:, 0:320], func=AF.Copy)
                    yield 2.5
                WtK = [("Wt" + sx, t) for t in range(4)]
                kpc = kpe[:, c * 4:(c + 1) * 4, :]
                E("pool", "tensor_copy", WtK, [("kpe", c)], out=kpc, in_=Wt[:, :, 0:32])
                E("act", "activation", [("kpe", c)], ["sqp" + sx], out=d["sqp"], in_=kpc, func=AF.Square)
                E("dve", "tensor_reduce", ["sqp" + sx], [("kss", c)], out=kss[:, c * 4:(c + 1) * 4], in_=d["sqp"], axis=AX.X, op=ALU.add)
                E("dve", "tensor_tensor", [("kpe", c), "gk", "sqp" + sx], [("kpe", c)], out=kpc, in0=kpc,
                  in1=gk[:, 64:96].unsqueeze(1).to_broadcast([128, 4, 32]), op=ALU.mult)
                head_prep(None, kpc.unsqueeze(2), 4, 32, ("kpe", c), None, None, None, None, None, (0, 32),
                          cosA[:, c * 4:(c + 1) * 4, 0:16], sinA[:, c * 4:(c + 1) * 4, 0:16], ROPE_A, 4, 1, d["rt"], sfx=sx)
                E("pool", "tensor_copy", WtK, ["Vb"], out=Vb4[:, c * 4:(c + 1) * 4, :, 0:64],
                  in_=Wt[:, :, 160:288].rearrange("p t (h d) -> p t h d", h=2))
                E("pool", "tensor_copy", WtK, ["kbw" + sx], out=d["kbw4"], in_=Wt[:, :, 32:160].rearrange("p t (h d) -> p t h d", h=2))
                head_prep(d["kbw3"], d["kbw4"], 8, 64, "kbw" + sx, d["sq3"], ssn[:, par * 8:par * 8 + 8], rsn[:, par * 8:par * 8 + 8], gkd, "gkd", (0, 16),
                          cosA[:, c * 4:(c + 1) * 4, 16:24], sinA[:, c * 4:(c + 1) * 4, 16:24], ROPE_A, 4, 2, d["rt"], sfx=sx)
                yield 14.0
                E("act", "activation", ["kbw" + sx], ["Kb" + sx], out=Kb, in_=vw(d["o_kbw"], 512).rearrange("p (t n) -> p t n", t=4), func=AF.Copy)
                for t in range(4):
                    MM(PS[bA][:, t * 128:(t + 1) * 128], Kb[:, t, :], ident, ["Kb" + sx, "ident"], [psk(bA)])
                E("dve", "tensor_copy", [psk(bA), "KTz"], [("KTb", c)], out=KTz[0][0:64, c * 512:(c + 1) * 512], in_=PS[bA][0:64, :])
                E("act", "activation", [psk(bA), "KTz"], [("KTb", c)], out=KTz[1][64:128, c * 512:(c + 1) * 512], in_=PS[bA][64:128, :], func=AF.Copy)
                E("pool", "tensor_copy", WtK, ["kiw" + sx], out=kiw, in_=Wt[:, :, 288:320])
                head_prep(None, kiw.unsqueeze(2), 4, 32, "kiw" + sx, None, None, None, None, None, (0, 8),
                          cosA[:, c * 4:(c + 1) * 4, 24:28], sinA[:, c * 4:(c + 1) * 4, 24:28], ROPE_A, 4, 1, d["rt"], sfx=sx)
                E("dve", "tensor_copy", ["kiw" + sx], ["Ks" + sx], out=Ks5[:, :, 0, :], in_=kiw)
                E("dve", "tensor_tensor", ["kiw" + sx, "Ks" + sx], ["Ks" + sx], out=Ks5[:, :, 1, :], in0=kiw, in1=Ks5[:, :, 0, :], op=ALU.subtract)
                E("pool", "tensor_copy", ["Ks" + sx], ["Ks" + sx], out=Ks5[:, :, 2, :], in_=Ks5[:, :, 0, :])
                for t in range(4):
                    MM(PS[bB][0:96, t * 128:(t + 1) * 128], Ks[:, t, :], ident, ["Ks" + sx, "ident"], [psk(bB)])
                E("act", "activation", [psk(bB)], [("KTi", c)], out=KTi[0:96, c * 512:(c + 1) * 512], in_=PS[bB][0:96, :], func=AF.Copy)
                yield 9.0

        def _ileave(ga, gb):
            ta = tb_ = 0.0
            ea = eb = False
            while not (ea and eb):
                if not ea and (eb or ta <= tb_):
                    try:
                        ta += next(ga) or 1.0
                    except StopIteration:
                        ea = True
                else:
                    try:
                        tb_ += next(gb) or 1.0
                    except StopIteration:
                        eb = True

        def _stagger(g, t0):
            yield t0
            yield from g

        _ileave(p1_gen(0), _stagger(p1_gen(1), 25.0))

        KTbK = [("KTb", c) for c in range(8)]
        KTiK = [("KTi", c) for c in range(8)]
        tap("KTb", KTz[0], [128, 4096], KTbK, BF16)
        tap("KTb1", KTz[1], [128, 4096], KTbK, BF16)
        tap("KTi", KTi[0:96, :], [96, 4096], KTiK, BF16)
        tap("Vb", vw(o_Vb, 2080, BF16), [128, 4160], ["Vb"], BF16)
        tap("ckvT", ckvT, [128, 4096], [("ckvT", c) for c in range(8)], BF16)
        tap("rkv", rkv, [128, 32], [("rkv", c) for c in range(8)])
        tap("kpe", vw(o_kpe, 1024), [128, 1024], [("kpe", c) for c in range(8)])
        tap("cosA", vw(o_cosA, 896), [128, 896], ["tabA"])
        tap("sinA", vw(o_sinA, 896), [128, 896], ["tabA"])
        p.barrier()

        if stop_after >= 2:
            U8 = mybir.dt.uint8
            ar = Bump(DSAK_END, NW)
            o_WQb = ar(3104); WQb = vw(o_WQb, 3104, BF16).rearrange("p (k n) -> p k n", k=8)
            o_mT = ar(7168)
            ORDER2 = [0, 1, 3, 2]
            BUF2 = {jj: pos % 2 for pos, jj in enumerate(ORDER2)}
            mTp = [vw(o_mT, 4096, U8).rearrange("p (k q) -> p k q", q=512),
                   vw(o_mT + 4096, 3072, U8).rearrange("p (k q) -> p k q", q=512)]
            o_QTb = ar(1024)
            QTb2 = [vw(o_QTb, 1024, BF16).rearrange("p (a q) -> p a q", a=4),
                    vw(o_cosA, 1024, BF16).rearrange("p (a q) -> p a q", a=4)]
            o_hT = ar(2048); hT = vw(o_hT, 2048, BF16).rearrange("p (k n) -> p k n", k=8)
            o_QTi = ar(2048); QTi = vw(o_QTi, 2048, BF16).rearrange("p (h q) -> p h q", h=8)
            R0 = ar.o
            rs_ = Bump(R0, NW)
            o_hb = rs_(2048); hb = vw(o_hb, 2048, BF16).rearrange("p (t d) -> p t d", t=4)
            o_xs2 = rs_(4096); xs2 = vw(o_xs2, 4096).rearrange("p (t d) -> p t d", t=4)
            sqq3 = vw(o_xs2, 2048).rearrange("p (n d) -> p n d", d=64)
            Qs = vw(o_xs2 + 2048, 1536, BF16).rearrange("p (t n) -> p t n", t=4)
            o_Wqb = rs_(2048); Wqb3 = vw(o_Wqb, 2048).rearrange("p (n d) -> p n d", d=64)
            Wqb4 = vw(o_Wqb, 2048).rearrange("p (t h d) -> p t h d", t=4, h=8)
            Wqbt = vw(o_Wqb, 2048).rearrange("p (t n) -> p t n", t=4)
            o_Wqi = rs_(1056); Wqi = vw(o_Wqi, 1056).rearrange("p (t n) -> p t n", t=4)
            o_Qb = rs_(1024); Qb = vw(o_Qb, 1024, BF16).rearrange("p (t n) -> p t n", t=4)
            Qb5 = vw(o_Qb, 1024, BF16).rearrange("p (t a g d) -> p t a g d", t=4, a=4, g=2)
            rtq = [vw(o_xs2 + 256 * k_, 256) for k_ in range(4)]
            o_wsc = rs_(32); wsc = vw(o_wsc, 32)
            junk = vw(o_Qb, 512, BF16)
            assert rs_.o <= NW - 2560
            rm_ = Bump(R0, NW)
            o_S = rm_(4096)
            Sv2 = [vw(o_S, 4096), vw(o_OTa, 4096)]
            o_wrk = rm_(4096); wrk = vw(o_wrk, 4096)
            o_M = rm_(2048); Mv = vw(o_M, 2048, BF16)
            assert rm_.o <= NW - 2560
            ra_ = Bump(NW - 2560, NW)
            PT = [vw(ra_(256), 256, BF16) for _ in range(4)]
            o_rd = ra_(512); rd = vw(o_rd, 512)
            o_bcs = ra_(512); bcs = vw(o_bcs, 512)
            o_osb = ra_(512); osb = vw(o_osb, 512)
            S0 = ("S", 0)
            p.alias = {("hb", 0): [S0], ("hb", 1): [S0], ("hb", 2): [S0], ("hb", 3): [S0], "xs2": [S0, "wrk"],
                       "Wqb": ["wrk"], "Wqi": ["M"], "Qb": ["M", "junk"], "junk": ["M", "Qb"], "sq": [S0], "Qs": ["wrk"],
                       ("rt", 0): [S0], ("rt", 1): [S0], ("rt", 2): [S0], ("rt", 3): [S0]}
            assert o_hb + 2048 <= o_S + 4096 and o_xs2 + 4096 <= o_wrk + 4096 and o_Wqb >= o_wrk and o_Wqb + 2048 <= o_wrk + 4096
            assert o_Wqi >= o_M and o_Qb + 1024 <= NW - 2560

            for i, ((c0, c1), o0) in enumerate([(QB, 0), (QI, 512), (WI, 768)]):
                load_w(WQb[:, :, o0:o0 + (c1 - c0)], w_in_v[:, :, c0:c1], "wqb%d" % i, ("WQb", i))
            WQK = [("WQb", i) for i in range(3)]

            def start2(j):
                QTb = QTb2[BUF2[j]]
                make_hT(xo[j * 512:(j + 1) * 512, :], xs2, "xs2", hb, hT, junk, "xs2")
                for t in range(4):
                    for kc in range(8):
                        MM(PS[4][:], hT[:, kc, t * 128:(t + 1) * 128], WQb[:, kc, 0:512], WQK + [("hT", kc)], [psk(4)],
                           start=(kc == 0), stop=(kc == 7))
                    for kc in range(8):
                        MM(PS[5][:, 0:264], hT[:, kc, t * 128:(t + 1) * 128], WQb[:, kc, 512:776], WQK + [("hT", kc)], [psk(5)],
                           start=(kc == 0), stop=(kc == 7))
                    E("act", "activation", [psk(4)], ["Wqb"], out=Wqbt[:, t, :], in_=PS[4][:], func=AF.Copy)
                    E("dve", "tensor_copy", [psk(5)], ["Wqi"], out=Wqi[:, t, :], in_=PS[5][:, 0:264])
                head_prep(Wqb3, Wqb4, 32, 64, "Wqb", sqq3, ssn, rsn, gqd, "gqd", (0, 16),
                          cosO[:, j * 4:(j + 1) * 4, 16:24], sinO[:, j * 4:(j + 1) * 4, 16:24], ROPE_O, 4, 8, rtq)
                E("act", "activation", ["Wqb"], ["Qb"], out=Qb5[:, :, :, 0, :], in_=Wqb4[:, :, 0:4, :], func=AF.Copy)
                E("pool", "tensor_copy", ["Wqb"], ["Qb"], out=Qb5[:, :, :, 1, :], in_=Wqb4[:, :, 4:8, :])
                for a in range(4):
                    b = a % 2
                    for t in range(4):
                        MM(PS[b][:, t * 128:(t + 1) * 128], Qb[:, t, a * 128:(a + 1) * 128], ident, ["Qb", "ident"], [psk(b)])
                    if a % 2 == 0:
                        E("act", "activation", [psk(b)], [("QTb", BUF2[j], a)], out=QTb[:, a, :], in_=PS[b][:], func=AF.Copy)
                    else:
                        E("dve", "tensor_copy", [psk(b)], [("QTb", BUF2[j], a)], out=QTb[:, a, :], in_=PS[b][:])
                sg3 = sgn.rearrange("p (t h) -> p t h", t=4)
                ws3 = wsc.rearrange("p (t h) -> p t h", t=4)
                E("act", "activation", ["Wqi"], ["sgn"], out=sg3, in_=Wqi[:, :, 256:264], func=AF.Sign)
                E("dve", "scalar_tensor_tensor", ["Wqi", "sgn"], ["wsc"], out=ws3, in0=Wqi[:, :, 256:264], scalar=1.0 / 16.0, in1=sg3,
                  op0=ALU.mult, op1=ALU.mult)
                Wqi4 = Wqi[:, :, 0:256].rearrange("p t (h d) -> p t h d", h=8)
                head_prep(None, Wqi4, 32, 32, "Wqi", None, None, None, None, None, (0, 8),
                          cosO[:, j * 4:(j + 1) * 4, 24:28], sinO[:, j * 4:(j + 1) * 4, 24:28], ROPE_O, 4, 8, rtq)
                E("dve", "tensor_tensor", ["Wqi", "wsc"], ["Wqi"], out=Wqi4, in0=Wqi4,
                  in1=ws3.unsqueeze(3).to_broadcast([128, 4, 8, 32]), op=ALU.mult)
                Qs6 = vw(o_xs2 + 2048, 1536, BF16).rearrange("p (t h a d) -> p t h a d", t=4, h=8, a=3)
                E("dve", "tensor_copy", ["Wqi", "sq"], ["Qs"], out=Qs6[:, :, :, 0, :], in_=Wqi4)
                E("dve", "tensor_tensor", ["Wqi", "Qs"], ["Qs"], out=Qs6[:, :, :, 2, :], in0=Wqi4, in1=Qs6[:, :, :, 0, :],
                  op=ALU.subtract)
                E("pool", "tensor_copy", ["Qs"], ["Qs"], out=Qs6[:, :, :, 1, :], in_=Qs6[:, :, :, 0, :])
                for hh in range(8):
                    b = hh % 2
                    for t in range(4):
                        MM(PS[b][0:96, t * 128:(t + 1) * 128], Qs[:, t, hh * 96:(hh + 1) * 96], ident, ["Qs", "ident"], [psk(b)])
                    if hh % 2 == 0:
                        E("act", "activation", [psk(b)], [("QTi", hh)], out=QTi[0:96, hh, :], in_=PS[b][0:96, :], func=AF.Copy)
                    else:
                        E("dve", "tensor_copy", [psk(b)], [("QTi", hh)], out=QTi[0:96, hh, :], in_=PS[b][0:96, :])

            PAIRS = [1, 2]
            ntr = [0]
            npair = [0]
            ntile = [0]

            def idx_gen(j):
                maskT = mTp[BUF2[j]]
                pend = []
                for i in range(4):
                    nkt = 8 * j + 2 * i + 2
                    n = nkt * 128
                    sbi = ntile[0] % 2
                    ntile[0] += 1
                    Sv = Sv2[sbi]
                    SK = ("S", sbi)
                    ks = 0
                    while ks < n:
                        wd = min(512, n - ks)
                        for hp in range(4):
                            pp = PAIRS[npair[0] % len(PAIRS)]
                            npair[0] += 1
                            pkey = ("psp", pp)
                            for e_ in range(2):
                                hh = 2 * hp + e_
                                MM(PSP[pp][:, e_ * 512:e_ * 512 + wd], QTi[0:96, hh, i * 128:(i + 1) * 128], KTi[0:96, ks:ks + wd],
                                   [("QTi", hh)] + KTiK, [pkey])
                            pv_ = PSP[pp][:, :].rearrange("p (e c) -> p e c", e=2)[:, :, 0:wd]
                            E("act", "activation", [pkey], [pkey], out=pv_, in_=pv_, func=AF.Relu)
                            for e_ in range(2):
                                hh = 2 * hp + e_
                                sc = sgn[:, i * 8 + hh:i * 8 + hh + 1]
                                rr = PSP[pp][:, e_ * 512:e_ * 512 + wd]
                                if hh == 0:
                                    E("act", "activation", [pkey, "sgn"], [SK], out=Sv[:, ks:ks + wd], in_=rr, func=AF.Identity, scale=sc)
                                else:
                                    E("dve", "scalar_tensor_tensor", [pkey, "sgn", SK], [SK], out=Sv[:, ks:ks + wd],
                                      in0=rr, scalar=sc, in1=Sv[:, ks:ks + wd], op0=ALU.mult, op1=ALU.add)
                            yield 1.7
                        ks += wd
                        while pend:
                            pend.pop(0)()
                    E("dve", "tensor_tensor", [SK, "cbias"], [SK], out=Sv[:, n - 256:n], in0=Sv[:, n - 256:n], in1=cbias, op=ALU.add)
                    if n <= 256:
                        E("dve", "tensor_scalar", [SK], ["M"], out=Mv[:, 0:n], in0=Sv[:, 0:n], scalar1=-0.5 * BIG, scalar2=None,
                          op0=ALU.is_ge)
                    elif n <= 512:
                        for r in range(32):
                            src = Sv if r == 0 else wrk
                            E("dve", "max", [SK if r == 0 else "wrk"], ["m8"], out=m8, in_=src[:, 0:n])
                            if r < 31:
                                E("dve", "match_replace", ["m8", SK if r == 0 else "wrk"], ["wrk"], out=wrk[:, 0:n], in_to_replace=m8,
                                  in_values=src[:, 0:n], imm_value=-BIG)
                            if r % 4 == 3:
                                yield 8 * (n / 960.0 + 0.3)
                        E("dve", "tensor_scalar", ["m8"], ["thr"], out=thr, in0=m8[:, 7:8], scalar1=-0.5 * BIG, scalar2=None, op0=ALU.max)
                        E("dve", "tensor_scalar", [SK, "thr"], ["M"], out=Mv[:, 0:n], in0=Sv[:, 0:n], scalar1=thr[:, 0:1], scalar2=None,
                          op0=ALU.is_ge)
                    else:
                        m_ = n // 16
                        sub = wrk[:, 0:m_]
                        tmp2 = wrk[:, 256:256 + m_]
                        E("dve", "tensor_copy", [SK], ["wrk"], out=sub,
                          in_=Sv[:, 0:n].rearrange("p (a s) -> p a s", s=16)[:, :, 0])
                        E("dve", "tensor_scalar", ["wrk"], ["wrk2"], out=tmp2, in0=sub, scalar1=-0.5 * BIG, scalar2=2.0 * BIG,
                          op0=ALU.is_lt, op1=ALU.mult)
                        E("dve", "tensor_tensor", ["wrk", "wrk2"], ["wrk2"], out=tmp2, in0=tmp2, in1=sub, op=ALU.add)
                        E("dve", "tensor_reduce", ["wrk2"], ["fb"], out=FB[:, 0:1], in_=tmp2, axis=AX.X, op=ALU.min)
                        E("dve", "tensor_scalar", ["qrel2"], ["fb"], out=FB[:, 1:2], in0=qrel2, scalar1=float((8 * j + 2 * i) * 128 + 1),
                          scalar2=None, op0=ALU.add)
                        E("dve", "tensor_copy", ["fb"], ["nvv"], out=nvv, in_=FB[:, 1:2])
                        for r in range(6):
                            E("dve", "max", ["wrk"], ["u8"], out=u8[:, r * 8:(r + 1) * 8], in_=sub)
                            if r < 5:
                                E("dve", "match_replace", ["u8", "wrk"], ["wrk"], out=sub, in_to_replace=u8[:, r * 8:(r + 1) * 8],
                                  in_values=sub, imm_value=-BIG)
                        E("dve", "tensor_tensor", ["u8", "fb"], ["LO"], out=LO[:, 0:1], in0=u8[:, 31:32], in1=FB[:, 0:1], op=ALU.max)
                        E("dve", "tensor_tensor", ["u8", "fb"], ["fb"], out=FB[:, 0:1], in0=u8[:, 47:48], in1=FB[:, 0:1], op=ALU.max)
                        E("dve", "tensor_scalar", ["fb"], ["fb"], out=FB[:, 1:2], in0=FB[:, 1:2], scalar1=768.0, scalar2=None, op0=ALU.min)
                        E("dve", "tensor_scalar", [SK, "LO"], ["M", "LO"], out=Mv[:, 0:n], in0=Sv[:, 0:n], scalar1=LO[:, 0:1], scalar2=None,
                          op0=ALU.is_ge, op1=ALU.add, accum_out=LO[:, 1:2])
                        E("dve", "tensor_copy", ["u8"], ["HI"], out=HI[:, 0:1], in_=u8[:, 3:4])
                        E("dve", "memset", [], ["HI"], HI[:, 1:2], 64.0)
                        E("dve", "tensor_scalar", ["LO"], ["g2"], out=g2, in0=LO[:, 1:2].to_broadcast([128, 2]), scalar1=256.0, scalar2=None,
                          op0=ALU.is_lt)
                        E("dve", "copy_predicated", ["g2", "LO", "HI"], ["HI"], out=HI, mask=g2, data=LO)
                        E("dve", "copy_predicated", ["g2", "fb", "LO", "HI"], ["LO"], out=LO, mask=g2, data=FB)
                        yield n / 960.0 + 6.0
                        for it in range(7):
                            E("dve", "tensor_tensor", ["LO", "HI"], ["ta"], out=ta, in0=LO[:, 1:2], in1=HI[:, 1:2], op=ALU.subtract)
                            E("dve", "reciprocal", ["ta"], ["ta"], out=ta, in_=ta)
                            E("dve", "scalar_tensor_tensor", ["LO", "ta"], ["tf"], out=tf, in0=LO[:, 1:2], scalar=-259.5, in1=ta,
                              op0=ALU.add, op1=ALU.mult)
                            E("dve", "tensor_scalar", ["tf"], ["tf"], out=tf, in0=tf, scalar1=0.03, scalar2=0.97, op0=ALU.max, op1=ALU.min)
                            E("dve", "tensor_tensor", ["LO", "HI"], ["td"], out=td, in0=HI[:, 0:1], in1=LO[:, 0:1], op=ALU.subtract)
                            E("dve", "scalar_tensor_tensor", ["td", "tf", "LO"], ["TC"], out=TC[:, 0:1], in0=td, scalar=tf[:, 0:1], in1=LO[:, 0:1],
                              op0=ALU.mult, op1=ALU.add)
                            E("dve", "tensor_scalar", [SK, "TC"], ["M", "TC"], out=Mv[:, 0:n], in0=Sv[:, 0:n], scalar1=TC[:, 0:1], scalar2=None,
                              op0=ALU.is_ge, op1=ALU.add, accum_out=TC[:, 1:2])
                            E("dve", "tensor_scalar", ["TC"], ["g2"], out=g2, in0=TC[:, 1:2].to_broadcast([128, 2]), scalar1=256.0,
                              scalar2=None, op0=ALU.is_ge)
                            E("dve", "tensor_scalar", ["TC"], ["ng2"], out=ng2, in0=TC[:, 1:2].to_broadcast([128, 2]), scalar1=256.0,
                              scalar2=None, op0=ALU.is_lt)
                            E("dve", "copy_predicated", ["g2", "TC", "LO"], ["LO"], out=LO, mask=g2, data=TC)
                            E("dve", "copy_predicated", ["ng2", "TC", "HI"], ["HI"], out=HI, mask=ng2, data=TC)
                            yield n / 960.0 + 2.2
                        E("dve", "tensor_scalar", [SK, "LO"], ["M"], out=Mv[:, 0:n], in0=Sv[:, 0:n], scalar1=LO[:, 0:1], scalar2=-BIG,
                          op0=ALU.is_lt, op1=ALU.mult)
                        nh_ = (n // 2) // 64 * 64
                        E("pool", "tensor_tensor", ["M", SK], [("wrkh", 1)], out=wrk[:, nh_:n], in0=Mv[:, nh_:n], in1=Sv[:, nh_:n], op=ALU.subtract)
                        E("dve", "tensor_tensor", ["M", SK], [("wrkh", 0)], out=wrk[:, 0:nh_], in0=Mv[:, 0:nh_], in1=Sv[:, 0:nh_], op=ALU.subtract)
                        yield 2.0 * n / 960.0 + 1.0
                        E("dve", "max", ["wrk", ("wrkh", 0), ("wrkh", 1)], ["m8", "wrk"], out=m8, in_=wrk[:, 0:n])
                        E("dve", "tensor_scalar", ["LO"], ["te"], out=te, in0=LO[:, 1:2], scalar1=-256.0, scalar2=7.0, op0=ALU.add, op1=ALU.min)
                        E("dve", "tensor_scalar", ["te", "io8"], ["oh8"], out=oh8, in0=io8, scalar1=te[:, 0:1], scalar2=None, op0=ALU.is_equal)
                        E("dve", "tensor_tensor", ["oh8", "m8"], ["oh8"], out=oh8, in0=oh8, in1=m8, op=ALU.mult)
                        E("dve", "tensor_reduce", ["oh8"], ["thr"], out=thr, in_=oh8, axis=AX.X, op=ALU.add)
                        E("dve", "tensor_scalar", ["thr"], ["thr"], out=thr, in0=thr, scalar1=-1.0, scalar2=None, op0=ALU.mult)
                        E("dve", "tensor_tensor", ["LO", "HI"], ["ta"], out=ta, in0=LO[:, 1:2], in1=HI[:, 1:2], op=ALU.add)
                        E("dve", "tensor_scalar", ["ta", "g2"], ["g2"], out=g2[:, 0:1], in0=ta, scalar1=519.0, scalar2=None, op0=ALU.is_gt)
                        E("dve", "copy_predicated", ["g2", "thr", "HI"], ["thr"], out=thr, mask=g2[:, 0:1], data=HI[:, 0:1])
                        E("dve", "tensor_scalar", ["nvv", "g2"], ["g2"], out=g2[:, 0:1], in0=nvv, scalar1=256.5, scalar2=None, op0=ALU.is_lt)
                        E("dve", "copy_predicated", ["g2", "thr", "negbig"], ["thr"], out=thr, mask=g2[:, 0:1], data=negbig)
                        E("dve", "tensor_scalar", [SK, "thr"], ["M"], out=Mv[:, 0:n], in0=Sv[:, 0:n], scalar1=thr[:, 0:1], scalar2=None,
                          op0=ALU.is_ge)
                    yield 2.0 * n / 960.0 + 2.0

                    def mtrans(i=i, nkt=nkt):
                        g0 = 0
                        while g0 < nkt:
                            cnt = min(4, nkt - g0)
                            tbk = 0
                            ntr[0] += 1
                            for a in range(cnt):
                                MM(PS[tbk][:, a * 128:(a + 1) * 128], Mv[:, (g0 + a) * 128:(g0 + a + 1) * 128], ident, ["M", "ident"], [psk(tbk)])
                            E("act", "activation", [psk(tbk)], [("mT", BUF2[j], i)], out=maskT[:, g0:g0 + cnt, i * 128:(i + 1) * 128],
                              in_=PS[tbk][:, 0:cnt * 128].rearrange("p (a q) -> p a q", a=cnt), func=AF.Copy)
                            g0 += cnt

                    pend.append(mtrans)
                for f_ in pend:
                    f_()
                yield 1.0

            def attn_gen(j, sb, ob, bcb, mengs=("pool",)):
                maskT = mTp[BUF2[j]]
                QTb = QTb2[BUF2[j]]
                MTK = [("mT", BUF2[j], i) for i in range(4)]
                nsb = len(sb)
                la = nsb - 1
                for g in range(8):
                    r0 = (g // 4) * 64
                    a = g % 4
                    kvh = g // 4
                    steps = [(kt, col_lo(max(kt - 8 * j, 0))) for kt in range(8 * j + 8)]

                    def Sstep(s):
                        kt, cl = steps[s]
                        bk = sb[s % nsb]
                        MM(PS[bk][:, cl:512], KTz[kvh][:, kt * 128:(kt + 1) * 128], QTb[:, a, cl:512],
                           KTbK + [("QTb", BUF2[j], a)], [psk(bk)])

                    for s0 in range(min(la, len(steps))):
                        Sstep(s0)
                    for s, (kt, cl) in enumerate(steps):
                        if s + la < len(steps):
                            Sstep(s + la)
                        bk = sb[s % nsb]
                        pt = PT[s % 4]
                        pk = ("PT", s % 4)
                        E("act", "activation", [psk(bk)], [pk], out=pt[:, cl:512], in_=PS[bk][:, cl:512], func=AF.Exp, scale=0.125)
                        E(mengs[s % len(mengs)], "tensor_tensor", [pk] + MTK, [pk], out=pt[:, cl:512], in0=pt[:, cl:512],
                          in1=maskT[:, kt, cl:512], op=ALU.mult)

                        def PVstep(s2):
                            kt2, cl2 = steps[s2]
                            MM(PS[ob][0:65, cl2:512], Vb[:, kt2, kvh * 65:(kvh + 1) * 65], PT[s2 % 4][:, cl2:512], [("PT", s2 % 4), "Vb"],
                               [psk(ob)], start=(s2 == 0), stop=(s2 == len(steps) - 1))

                        if s >= 2:
                            PVstep(s - 2)
                        yield 1.0
                    for s2 in range(max(0, len(steps) - 2), len(steps)):
                        PVstep(s2)
                    E("act", "activation", [psk(ob)], ["rd"], out=rd[64:65, :], in_=PS[ob][64:65, :], func=AF.Ln)
                    E("act", "activation", [psk(ob)], ["osb"], out=osb[0:64, :], in_=PS[ob][0:64, :], func=AF.Copy)
                    MM(PS[bcb][0:64, :], ones_f[64:65, 0:64], rd[64:65, :], ["ones_f", "rd"], [psk(bcb)])
                    E("act", "activation", [psk(bcb)], ["bcs"], out=bcs[0:64, :], in_=PS[bcb][0:64, :], func=AF.Exp, scale=-1.0)
                    E("pool", "tensor_tensor", ["osb", "bcs"], [("OTb", j, g)],
                      out=OTb[(g % 2) * 64:(g % 2) * 64 + 64, g // 2, j * 512:(j + 1) * 512], in0=osb[0:64, :], in1=bcs[0:64, :], op=ALU.mult)
                    yield 3.0

            def run_gen(gen):
                for _ in gen:
                    pass

            def interleave(ga, gb):
                ta = tb = 0.0
                ea = eb = False
                while not (ea and eb):
                    if not ea and (eb or ta <= tb):
                        try:
                            ta += next(ga) or 1.0
                        except StopIteration:
                            ea = True
                    else:
                        try:
                            tb += next(gb) or 1.0
                        except StopIteration:
                            eb = True

            def n_idx(j):
                tot = 0
                for i in range(4):
                    n = (8 * j + 2 * i + 2) * 128
                    tot += 4 * ((n + 511) // 512)
                    tot += 0 if n <= 256 else (8 if n <= 512 else 9)
                    tot += 2
                return tot

            start2(ORDER2[0])
            run_gen(idx_gen(ORDER2[0]))
            for pos in range(4):
                j = ORDER2[pos]
                if pos + 1 < 4:
                    jn = ORDER2[pos + 1]
                    start2(jn)
                    interleave(attn_gen(j, [6, 1], 7, 6), idx_gen(jn))
                else:
                    run_gen(attn_gen(j, [0, 1, 2], 6, 7, mengs=("dve", "pool")))
            p.barrier()
            p.alias = {}
            tap("OTb", vw(o_OTb, 4096, BF16), [128, 8192], [("OTb", j, g) for j in range(4) for g in range(8)], BF16)

        if stop_after >= 3:
            ar = Bump(ARENA, NW)
            o_WQa = ar(1024); WQa = vw(o_WQa, 1024, BF16).rearrange("p (k n) -> p k n", k=8)
            o_wuq = ar(768); wuq = vw(o_wuq, 768, BF16).rearrange("p (k n) -> p k n", k=2)
            o_wukv = ar(512); wukv = vw(o_wukv, 512, BF16)
            o_cm = ar(2048); cmask = vw(o_cm, 2048, BF16).rearrange("p (m q) -> p m q", m=8)
            o_KTa = ar(8192); KTa = vw(o_KTa, 8192, BF16).rearrange("p (h s) -> p h s", h=4)
            o_Va = ar(4160); Va = vw(o_Va, 4160, BF16).rearrange("p (t c) -> p t c", c=260)
            Va4 = vw(o_Va, 4160, BF16).rearrange("p (t h c) -> p t h c", h=4, c=65)
            o_hT = ar(2048); hT = vw(o_hT, 2048, BF16).rearrange("p (k n) -> p k n", k=8)
            o_QTa = ar(1024); QTa = vw(o_QTa, 1024, BF16).rearrange("p (h q) -> p h q", h=4)
            o_junk = ar(512); junk = vw(o_junk, 512, BF16)
            o_cqT = ar(512); cqT = vw(o_cqT, 512, BF16).rearrange("p (f q) -> p f q", f=2)
            o_sqb2 = ar(512); sqb2 = vw(o_sqb2, 512, BF16).rearrange("p (f q) -> p f q", f=2)
            PT = [vw(ar(256), 256, BF16) for _ in range(4)]
            o_rd = ar(512); rd = vw(o_rd, 512)
            o_bcs = ar(512); bcs = vw(o_bcs, 512)
            R0 = ar.o
            rs_ = Bump(R0, NW)
            o_hb = rs_(2048); hb = vw(o_hb, 2048, BF16).rearrange("p (t d) -> p t d", t=4)
            o_xs3 = rs_(4096); xs3 = vw(o_xs3, 4096).rearrange("p (t d) -> p t d", t=4)
            sqa3 = vw(o_xs3, 1536).rearrange("p (n d) -> p n d", d=96)
            Qa = vw(o_xs3 + 1536, 768, BF16).rearrange("p (t n) -> p t n", t=4)
            o_Wqa = rs_(1536); Wqa3 = vw(o_Wqa, 1536).rearrange("p (n d) -> p n d", d=96)
            Wqa4 = vw(o_Wqa, 1536).rearrange("p (t h d) -> p t h d", t=4, h=4)
            Wqat = vw(o_Wqa, 1536).rearrange("p (t n) -> p t n", t=4)
            rta = [vw(o_xs3 + 256 * k_, 256) for k_ in range(4)]
            o_rta1 = rs_(1024)
            rk_ = Bump(R0, NW)
            KSET = []
            KSQ = []
            for par_ in range(2):
                o_w_ = rk_(1536); o_s_ = rk_(1536); o_k_ = rk_(768)
                KSQ.append(o_s_)
                KSET.append((vw(o_w_, 1536).rearrange("p (n d) -> p n d", d=96),
                             vw(o_w_, 1536).rearrange("p (t h d) -> p t h d", t=4, h=4),
                             vw(o_s_, 1536).rearrange("p (n d) -> p n d", d=96),
                             vw(o_k_, 768, BF16).rearrange("p (t n) -> p t n", t=4),
                             [vw(o_s_ + 256 * k_, 256) for k_ in range(4)],
                             o_w_))

            load_w(WQa, w_in_v[:, :, CQ[0]:CQ[1]], "wqa", "WQa")
            load_w(wuq, w_uq.rearrange("(k p) n -> p k n", p=128), "wuq", "wuq")
            load_w(wukv, w_ukv, "wukv", "wukv")
            kr = vw(o_small + 130, 8)
            kri = vw(o_small + 140, 8, I32)
            E("pool", "iota", [], ["kri"], kri, pattern=[[128, 8]], base=0, channel_multiplier=1)
            E("dve", "tensor_copy", ["kri"], ["kr"], out=kr, in_=kri)
            for m in range(8):
                E("dve", "tensor_scalar", ["qrel", "kr"], ["cmask"], out=cmask[:, m, :], in0=qrel, scalar1=kr[:, m:m + 1], scalar2=None,
                  op0=ALU.is_ge)

            QTa2 = [QTa, vw(o_rta1, 1024, BF16).rearrange("p (h q) -> p h q", h=4)]

            def kexp_gen(hg, par):
                W3_, W4_, sq_, Kb_, rt_, o_w = KSET[par]
                bA = (4, 5) if par == 0 else (6, 7)
                bT = (0, 1) if par == 0 else (2, 3)
                sfx = "k%d" % par
                Wkey = "Wk%d" % par
                for c in range(par, 8, 2):
                    for t in range(4):
                        T_ = 4 * c + t
                        b = bA[t % 2]
                        MM(PS[b][:], ckvT[:, T_ * 128:(T_ + 1) * 128], wukv[:, hg * 512:(hg + 1) * 512], [("ckvT", c), "wukv"], [psk(b)])
                        pv = PS[b][:].rearrange("p (h d) -> p h d", h=4)
                        E("act", "activation", [psk(b), ("rkv", c)], [Wkey], out=W4_[:, t, :, 0:64], in_=pv[:, :, 0:64], func=AF.Identity,
                          scale=rkv[:, T_:T_ + 1])
                        E("dve", "tensor_scalar", [psk(b), ("rkv", c)], ["Va"], out=Va4[:, T_, :, 0:64], in0=pv[:, :, 64:128],
                          scalar1=rkv[:, T_:T_ + 1], scalar2=None, op0=ALU.mult)
                    E("pool", "tensor_copy", [("kpe", c), Wkey], [Wkey], out=W4_[:, :, :, 64:96],
                      in_=kpe[:, c * 4:(c + 1) * 4, :].unsqueeze(2).to_broadcast([128, 4, 4, 32]))
                    yield 5.0
                    ssv_ = ssn[:, par * 16:par * 16 + 16]
                    rsv_ = rsn[:, par * 16:par * 16 + 16]
                    sq64 = vw(KSQ[par], 1024).rearrange("p (n d) -> p n d", d=64)
                    E("act", "activation", [Wkey], ["sq" + sfx], out=sq64, in_=W3_[:, :, 0:64], func=AF.Square)
                    E("dve", "tensor_reduce", ["sq" + sfx], ["ssn" + sfx], out=ssv_, in_=sq64, axis=AX.X, op=ALU.add)
                    E("dve", "tensor_tensor", ["ssn" + sfx, ("kss", c)], ["ssn" + sfx], out=ssv_.rearrange("p (t h) -> p t h", t=4),
                      in0=ssv_.rearrange("p (t h) -> p t h", t=4), in1=kss[:, c * 4:(c + 1) * 4].unsqueeze(2).to_broadcast([128, 4, 4]),
                      op=ALU.add)
                    E("act", "activation", ["ssn" + sfx], ["rsn" + sfx], out=rsv_, in_=ssv_, func=AF.Ln, scale=1.0 / 96, bias=EPS)
                    E("act", "activation", ["rsn" + sfx], ["rsn" + sfx], out=rsv_, in_=rsv_, func=AF.Exp, scale=-0.5)
                    E("dve", "tensor_tensor", [Wkey, "rsn" + sfx], [Wkey], out=W3_, in0=W3_, in1=rsv_.unsqueeze(2).to_broadcast([128, 16, 96]),
                      op=ALU.mult)
                    E("dve", "tensor_tensor", [Wkey, "gk"], [Wkey], out=W3_[:, :, 0:64], in0=W3_[:, :, 0:64],
                      in1=gk[:, 0:64].unsqueeze(1).to_broadcast([128, 16, 64]), op=ALU.mult)
                    yield 10.0
                    E("act", "activation", [Wkey], ["Kb16" + sfx], out=Kb_, in_=vw(o_w, 1536).rearrange("p (t n) -> p t n", t=4), func=AF.Copy)
                    for hl in range(4):
                        b = bT[hl % 2]
                        for t in range(4):
                            MM(PS[b][0:96, t * 128:(t + 1) * 128], Kb_[:, t, hl * 96:(hl + 1) * 96], ident, ["Kb16" + sfx, "ident"], [psk(b)])
                        if hl % 2 == 0:
                            E("act", "activation", [psk(b)], [("KTa", c)], out=KTa[0:96, hl, c * 512:(c + 1) * 512], in_=PS[b][0:96, :],
                              func=AF.Copy)
                        else:
                            E("dve", "tensor_copy", [psk(b)], [("KTa", c)], out=KTa[0:96, hl, c * 512:(c + 1) * 512], in_=PS[b][0:96, :])
                    yield 5.0

            XS3_ALIAS = ["sq", "Qa"] + [("rt", k) for k in range(4)]

            def start3_gen(hg, j):
                QTb_ = QTa2[j % 2]
                make_hT(xo[j * 512:(j + 1) * 512, :], xs3, "xs3", hb, hT, junk, "xs3", tb=(4, 5), alias_keys=XS3_ALIAS, do_load=False)
                yield 14.0
                for f in range(2):
                    for kc in range(8):
                        MM(PS[6 + f][:], WQa[:, kc, f * 128:(f + 1) * 128], hT[:, kc, :], ["WQa", ("hT", kc)], [psk(6 + f)],
                           start=(kc == 0), stop=(kc == 7))
                    E("dve", "tensor_scalar", [psk(6 + f), "gqn"], [("cqT", f)], out=cqT[:, f, :], in0=PS[6 + f][:],
                      scalar1=gqn[:, f:f + 1], scalar2=None, op0=ALU.mult)
                    E("act", "activation", [psk(6 + f)], [("sqb2", f)], out=sqb2[:, f, :], in_=PS[6 + f][:], func=AF.Square)
                yield 6.0
                for t in range(4):
                    for f in range(2):
                        MM(PS[6][:, 2 * t:2 * t + 2], sqb2[:, f, t * 128:(t + 1) * 128], ones_b[:, 0:2], [("sqb2", f), "ones_b"], [psk(6)],
                           start=(f == 0), stop=(f == 1))
                E("act", "activation", [psk(6)], ["rq4"], out=rq4, in_=PS[6][:, 0:8].rearrange("p (t two) -> p t two", two=2)[:, :, 0],
                  func=AF.Ln, scale=1.0 / 256, bias=EPS)
                E("act", "activation", ["rq4"], ["rq4"], out=rq4, in_=rq4, func=AF.Exp, scale=-0.5)
                for t in range(4):
                    b = 6 + (t % 2)
                    for f in range(2):
                        MM(PS[b][:, 0:384], cqT[:, f, t * 128:(t + 1) * 128], wuq[:, f, hg * 384:(hg + 1) * 384],
                           [("cqT", f), "wuq"], [psk(b)], start=(f == 0), stop=(f == 1))
                    E("act", "activation", [psk(b), "rq4"], ["Wqa"], out=Wqat[:, t, :], in_=PS[b][:, 0:384], func=AF.Identity,
                      scale=rq4[:, t:t + 1])
                yield 6.0
                head_prep(Wqa3, Wqa4, 16, 96, "Wqa", sqa3, ssn[:, 0:16], rsn[:, 0:16], gq, "gq", (64, 32),
                          cosO[:, j * 4:(j + 1) * 4, 0:16], sinO[:, j * 4:(j + 1) * 4, 0:16], ROPE_O, 4, 4, rta)
                yield 16.0
                E("act", "activation", ["Wqa", "sq"], ["Qa"], out=Qa, in_=Wqat, func=AF.Copy)
                for hl in range(4):
                    b = 4 + hl % 2
                    for t in range(4):
                        MM(PS[b][0:96, t * 128:(t + 1) * 128], Qa[:, t, hl * 96:(hl + 1) * 96], ident, ["Qa", "ident"], [psk(b)])
                    if hl % 2 == 0:
                        E("act", "activation", [psk(b)], [("QTa", j % 2, hl)], out=QTb_[0:96, hl, :], in_=PS[b][0:96, :], func=AF.Copy)
                    else:
                        E("dve", "tensor_copy", [psk(b)], [("QTa", j % 2, hl)], out=QTb_[0:96, hl, :], in_=PS[b][0:96, :])
                if j + 1 < 4:
                    load_x(xo[(j + 1) * 512:(j + 2) * 512, :], xs3, "xs3", "xs3", XS3_ALIAS)
                yield 6.0

            def attn3_gen(hg, j, sb):
                QTb_ = QTa2[j % 2]
                nsb = len(sb)
                la = nsb - 1
                for hl in range(4):
                    g = hg * 4 + hl
                    steps = [(kt, col_lo(max(kt - 8 * j, 0))) for kt in range(8 * j + 8)]
                    ob = 2 if nsb == 2 else 4

                    def Sstep(s):
                        kt, cl = steps[s]
                        bk = sb[s % nsb]
                        MM(PS[bk][:, cl:512], KTa[0:96, hl, kt * 128:(kt + 1) * 128], QTb_[0:96, hl, cl:512],
                           KTaK + [("QTa", j % 2, hl)], [psk(bk)])

                    def PVstep(s2):
                        kt2, cl2 = steps[s2]
                        MM(PS[ob][0:65, cl2:512], Va[:, kt2, hl * 65:(hl + 1) * 65], PT[s2 % 4][:, cl2:512], [("PT", s2 % 4), "Va"],
                           [psk(ob)], start=(s2 == 0), stop=(s2 == len(steps) - 1))

                    for s0 in range(min(la, len(steps))):
                        Sstep(s0)
                    for s, (kt, cl) in enumerate(steps):
                        if s + la < len(steps):
                            Sstep(s + la)
                        bk = sb[s % nsb]
                        pt = PT[s % 4]
                        pk = ("PT", s % 4)
                        E("act", "activation", [psk(bk)], [pk], out=pt[:, cl:512], in_=PS[bk][:, cl:512], func=AF.Exp,
                          scale=float(96.0 ** -0.5))
                        m = kt - 8 * j
                        if m >= 0:
                            E("dve" if s % 2 == 0 else "pool", "tensor_tensor", [pk, "cmask"], [pk], out=pt[:, cl:512],
                              in0=pt[:, cl:512], in1=cmask[:, m, cl:512], op=ALU.mult)
                        if s >= 2:
                            PVstep(s - 2)
                        yield 0.9
                    for s2 in range(max(0, len(steps) - 2), len(steps)):
                        PVstep(s2)
                    finalize(ob, OTa[(g % 2) * 64:(g % 2) * 64 + 64, g // 2, j * 512:(j + 1) * 512], ("OTa", j, g), rd, bcs, 3 if nsb == 2 else 6)
                    yield 3.0

            def run_gen3(gen):
                for _ in gen:
                    pass

            def interleave3(ga, gb):
                ta = tb = 0.0
                ea = eb = False
                while not (ea and eb):
                    if not ea and (eb or ta <= tb):
                        try:
                            ta += next(ga) or 1.0
                        except StopIteration:
                            ea = True
                    else:
                        try:
                            tb += next(gb) or 1.0
                        except StopIteration:
                            eb = True

            WGp = vw(o_cosA, 4096, BF16).rearrange("p (k n) -> p k n", k=8)
            assert o_ckvT == o_cosA + 2688 and o_kpe >= o_cosA + 4096
            KTaK = [("KTa", c) for c in range(8)]
            for hg in range(2):
                interleave3(kexp_gen(hg, 0), kexp_gen(hg, 1))
                if hg == 0:
                    E("pool", "memset", [], ["Va"], Va4[:, :, :, 64:65], 1.0)
                    tap("KTa", vw(o_KTa, 8192, BF16)[0:96, 0:4096], [96, 4096], KTaK, BF16)
                    tap("Va", vw(o_Va, 4160, BF16), [128, 8320], ["Va"], BF16)
                p.barrier()
                load_x(xo[0:512, :], xs3, "xs3", "xs3", XS3_ALIAS)
                run_gen3(start3_gen(hg, 0))
                for j in range(4):
                    if j + 1 < 4:
                        interleave3(attn3_gen(hg, j, [0, 1]), start3_gen(hg, j + 1))
                    else:
                        run_gen3(attn3_gen(hg, j, [0, 1, 2]))
                if hg == 1 and stop_after >= 4:
                    dead_keys = ["tabO"] + [("ckvT", c_) for c_ in range(8)]
                    DMA("pool", WGp[:, :, 0:512], w_in_v[:, :, GA[0]:GA[1]], "wg0", [], [("WG", 0)] + dead_keys)
                    DMA("pool", WGp[:, :, 512:1024], w_in_v[:, :, GB[0]:GB[1]], "wg1", [], [("WG", 1)] + dead_keys)
                p.barrier()
            tap("OTa", vw(o_OTa, 4096, BF16), [128, 8192], [("OTa", j, g) for j in range(4) for g in range(8)], BF16)

        if stop_after >= 4:
            ar = Bump(ARENA, NW)
            WG = vw(o_cosA, 4096, BF16).rearrange("p (k n) -> p k n", k=8)
            o_WM = ar(8192); WM = vw(o_WM, 8192, BF16).rearrange("p (k n) -> p k n", k=8)
            o_WBA = ar(2048); WBA = vw(o_WBA, 2048, BF16).rearrange("p (k n) -> p k n", k=4)
            o_WBD = ar(2048); WBD = vw(o_WBD, 2048, BF16).rearrange("p (k n) -> p k n", k=4)
            o_WO = ar(4096); WO = vw(o_WO, 4096, BF16).rearrange("p (k n) -> p k n", k=8)
            o_xs4 = ar(4096); xs4 = vw(o_xs4, 4096).rearrange("p (t d) -> p t d", t=4)
            o_hb = ar(2048); hb = vw(o_hb, 2048, BF16).rearrange("p (t d) -> p t d", t=4)
            o_junk = ar(512); junk = vw(o_junk, 512, BF16)
            sgt = [vw(ar(256), 256, BF16) for _ in range(2)]
            o_og = ar(2048); og = vw(o_og, 2048, BF16).rearrange("p (c q) -> p c q", c=8)
            o_sig = ar(512); sig = vw(o_sig, 512, BF16).rearrange("p (a q) -> p a q", a=2)
            tmpf = [vw(ar(512), 512) for _ in range(2)]
            yo = [vw(ar(1024), 1024) for _ in range(2)]
            o_hT = ar(2048); hT = vw(o_hT, 2048, BF16).rearrange("p (k n) -> p k n", k=8)
            o_mg = ar(2048); mg = vw(o_mg, 2048, BF16).rearrange("p (f q) -> p f q", f=8)

            load_w(WM[:, :, 0:1024], w_in_v[:, :, MA[0]:MA[1]], "wm0", ("WM", 0))
            load_w(WM[:, :, 1024:2048], w_in_v[:, :, MB[0]:MB[1]], "wm1", ("WM", 1))
            load_w(WBA, wba_d.rearrange("(k p) n -> p k n", p=128), "wba", "WBA")
            load_w(WBD, wbd_d.rearrange("(k p) n -> p k n", p=128), "wbd", "WBD")
            load_w(WO, wo_d.rearrange("(k p) n -> p k n", p=128), "wo", "WO")
            nst = 0
            for j in range(4):
                make_hT(xo[j * 512:(j + 1) * 512, :], xs4, "xs4", hb, hT, junk, "xs4")
                idx = 0
                for br in range(2):
                    OT = OTa if br == 0 else OTb
                    otk = [("OTa" if br == 0 else "OTb", j, g) for g in range(8)]
                    for f in range(4):
                        b = 4 + (idx % 2)
                        for kc in range(8):
                            MM(PS[b][:], WG[:, kc, br * 512 + f * 128:br * 512 + (f + 1) * 128], hT[:, kc, :],
                               [("WG", br), ("hT", kc)], [psk(b)], start=(kc == 0), stop=(kc == 7))
                        E("act", "activation", [psk(b)], [("sg", idx % 2)], out=sgt[idx % 2], in_=PS[b][:], func=AF.Silu)
                        E("dve" if idx % 2 == 0 else "pool", "tensor_tensor", [("sg", idx % 2)] + otk, [("og", br * 4 + f)],
                          out=og[:, br * 4 + f, :], in0=OT[:, f, j * 512:(j + 1) * 512], in1=sgt[idx % 2], op=ALU.mult)
                        idx += 1
                for f in range(8):
                    for a in range(2):
                        for kc in range(8):
                            MM(PS[4 + a][:], WM[:, kc, a * 1024 + f * 128:a * 1024 + (f + 1) * 128], hT[:, kc, :],
                               [("WM", a), ("hT", kc)], [psk(4 + a)], start=(kc == 0), stop=(kc == 7))
                        E("act", "activation", [psk(4 + a), "bm"], [("sig", a)], out=sig[:, a, :], in_=PS[4 + a][:], func=AF.Sigmoid,
                          bias=bm[:, a * 8 + f:a * 8 + f + 1])
                    for c4 in range(4):
                        MM(PS[6][:], WBA[:, c4, f * 128:(f + 1) * 128], og[:, c4, :], ["WBA", ("og", c4)], [psk(6)],
                           start=(c4 == 0), stop=(c4 == 3))
                    for c4 in range(4):
                        MM(PS[7][:], WBD[:, c4, f * 128:(f + 1) * 128], og[:, 4 + c4, :], ["WBD", ("og", 4 + c4)], [psk(7)],
                           start=(c4 == 0), stop=(c4 == 3))
                    E("dve", "tensor_tensor", [psk(6), ("sig", 0)], [("tmpf", 0)], out=tmpf[0], in0=PS[6][:], in1=sig[:, 0, :], op=ALU.mult)
                    E("dve", "tensor_tensor", [psk(7), ("sig", 1)], [("tmpf", 1)], out=tmpf[1], in0=PS[7][:], in1=sig[:, 1, :], op=ALU.mult)
                    E("pool", "tensor_tensor", [("tmpf", 0), ("tmpf", 1)], [("mg", f)], out=mg[:, f, :], in0=tmpf[0], in1=tmpf[1], op=ALU.add)
                for t in range(4):
                    sl = nst % 2
                    for hf in range(2):
                        b = hf
                        for f in range(8):
                            MM(PS[b][:], mg[:, f, t * 128:(t + 1) * 128], WO[:, f, hf * 512:(hf + 1) * 512], [("mg", f), "WO"], [psk(b)],
                               start=(f == 0), stop=(f == 7))
                        E("dve", "tensor_tensor", [psk(b), "xs4"], [("yo", sl)], out=yo[sl][:, hf * 512:(hf + 1) * 512], in0=PS[b][:],
                          in1=xs4[:, t, hf * 512:(hf + 1) * 512], op=ALU.add)
                    u = 4 * j + t
                    DMA("sp", y[u * 128:(u + 1) * 128, :], yo[sl], "st%d" % sl, [("yo", sl)], [])
                    nst += 1

        p.barrier()
        p.op("sp", None)
        p.emit(st)
    return nc, tap_out


def _prep_inputs(inputs):
    f32 = np.float32
    x = np.ascontiguousarray(inputs["x"], dtype=f32)
    pos = np.ascontiguousarray(inputs["positions"]).astype(np.int32)

    def pl(v, k):
        return np.ascontiguousarray(np.asarray(v, dtype=f32).reshape(k, 128).T)

    def rep(v):
        v = np.asarray(v, dtype=f32).reshape(1, -1)
        return np.ascontiguousarray(np.repeat(v, 128, axis=0))

    shared = {
        "ng": pl(inputs["norm_gain"][0], 8),
        "w_in": np.ascontiguousarray(inputs["w_in"][0], dtype=f32),
        "bm": np.ascontiguousarray(np.concatenate([pl(inputs["b_merge"][0, 0], 8), pl(inputs["b_merge"][0, 1], 8)], axis=1)),
        "gqn": pl(inputs["mla_q_norm"][0], 2),
        "w_uq": np.ascontiguousarray(inputs["mla_w_uq"][0], dtype=f32),
        "gkvn": pl(inputs["mla_kv_norm"][0], 1),
        "w_ukv": np.ascontiguousarray(inputs["mla_w_ukv"][0], dtype=f32),
        "gq": rep(inputs["mla_q_gain"][0]), "gk": rep(inputs["mla_k_gain"][0]),
        "gqd": rep(inputs["dsa_q_gain"][0]), "gkd": rep(inputs["dsa_k_gain"][0]),
        "wba": np.ascontiguousarray(inputs["w_branch_mla"][0], dtype=f32),
        "wbd": np.ascontiguousarray(inputs["w_branch_dsa"][0], dtype=f32),
        "wo": np.ascontiguousarray(inputs["w_out"][0], dtype=f32),
    }
    in_maps = []
    own_tiles = []
    for core in range(8):
        b, h = core // 2, core % 2
        tiles = [8 * j + 2 * i + h for j in range(4) for i in range(4)]
        own_tiles.append(tiles)
        xb = x[b]
        xo = np.ascontiguousarray(np.concatenate([xb[t * 128:(t + 1) * 128] for t in tiles], axis=0))
        pb = pos[b]
        posa = np.ascontiguousarray(pb.reshape(32, 128).T)
        poso = np.ascontiguousarray(np.stack([pb[t * 128:(t + 1) * 128] for t in tiles], axis=1))
        qrel = np.concatenate([(2 * i + h) * 128 + np.arange(128) for i in range(4)]).astype(f32)
        m = dict(shared)
        m.update({
            "xa": np.ascontiguousarray(xb), "xo": xo, "posa": posa.astype(np.int32), "poso": poso.astype(np.int32),
            "qrel": rep(qrel), "qrel2": (h * 128 + np.arange(128, dtype=f32)).reshape(128, 1).astype(f32),
        })
        in_maps.append(m)
    return in_maps, own_tiles


_CACHE = {}


def kernel(**inputs):
    in_maps, own_tiles = _prep_inputs(inputs)
    if "nc" not in _CACHE:
        _CACHE["nc"] = build_program()[0]
    nc = _CACHE["nc"]
    res = run_bass_kernel_spmd(nc, in_maps, core_ids=list(range(8)))
    out = np.empty((4, 4096, 1024), dtype=np.float32)
    for core in range(8):
        b = core // 2
        yv = np.asarray(res.results[core]["y"], dtype=np.float32)
        for u, t in enumerate(own_tiles[core]):
            out[b, t * 128:(t + 1) * 128, :] = yv[u * 128:(u + 1) * 128, :]
    return out
```

```python
import os
from contextlib import ExitStack

import numpy as np
import concourse.bass as bass
import concourse.mybir as mybir
from concourse.bass_utils import run_bass_kernel_spmd

F32 = mybir.dt.float32
BF16 = mybir.dt.bfloat16
I32 = mybir.dt.int32
ALU = mybir.AluOpType
AF = mybir.ActivationFunctionType
AX = mybir.AxisListType

THETA = 500000.0
EPS = 1e-6
BIG = 1.0e30
MAGIC = 12582912.0
TWO_PI = 6.283185307179586

CQ = (0, 256); CKV = (256, 384); KPE = (384, 416); GA = (416, 928); QB = (928, 1440)
KB = (1440, 1568); VB = (1568, 1696); GB = (1696, 2208); QI = (2208, 2464); KI = (2464, 2496)
WI = (2496, 2504); MA = (2504, 3528); MB = (3528, 4552)


class Op:
    __slots__ = ("eng", "fn", "deps", "odeps", "idx", "dma", "need", "ms", "dur", "region", "seq", "dval", "succ", "npred", "prio", "fin")

    def __init__(self, eng, fn, dma, dur, region, seq):
        self.eng = eng
        self.fn = fn
        self.deps = set()
        self.odeps = set()
        self.idx = 0
        self.dma = dma
        self.need = False
        self.ms = 0
        self.dur = dur
        self.region = region
        self.seq = seq
        self.dval = 0


class Prog:
    ENGS = ["pe", "act", "dve", "pool", "sp"]

    def __init__(self, nc, schedule=True):
        self.nc = nc
        self.ops = []
        self.lastw = {}
        self.readers = {}
        self.region = 0
        self.last_dma = {}
        self.schedule = schedule
        self.alias = {}

    def _expand(self, keys):
        out = []
        for k in keys:
            if isinstance(k, tuple) and k[0] == "psp":
                out.append(("ps", 2 * k[1]))
                out.append(("ps", 2 * k[1] + 1))
            else:
                out.append(k)
                for a_ in self.alias.get(k, ()):
                    out.append(a_)
        return out

    def op(self, eng, fn, reads=(), writes=(), dma=None, dur=0.3):
        reads = self._expand(reads)
        writes = self._expand(writes)
        o = Op(eng, fn, dma, dur, self.region, len(self.ops))
        deps = set()
        for k in reads:
            t = self.lastw.get(k)
            if t is not None:
                deps.add(t)
            if isinstance(k, tuple) and k[0] == "ps":
                for r in self.readers.get(k, ()):
                    deps.add(r)
        for k in writes:
            t = self.lastw.get(k)
            if t is not None:
                deps.add(t)
            for r in self.readers.get(k, ()):
                deps.add(r)
        for d in deps:
            if d.region != o.region:
                continue
            if d.eng == "pe" and eng == "pe" and d.dma is None:
                o.odeps.add(d)
            else:
                o.deps.add(d)
        if dma is not None:
            p_ = self.last_dma.get((eng, dma))
            if p_ is not None and p_.region == o.region:
                o.odeps.add(p_)
            self.last_dma[(eng, dma)] = o
        self.ops.append(o)
        for k in reads:
            self.readers.setdefault(k, []).append(o)
        for k in writes:
            self.lastw[k] = o
            self.readers[k] = []
        return o

    def barrier(self):
        self.region += 1

    def _schedule_region(self, ops):
        if not self.schedule or len(ops) < 3:
            return list(ops)
        LAT_X, LAT_S = 0.35, 0.12
        for o in ops:
            o.succ = []
            o.npred = 0
        for o in ops:
            for d in list(o.deps) + list(o.odeps):
                d.succ.append(o)
                o.npred += 1
        for o in reversed(ops):
            m = 0.0
            for s_ in o.succ:
                if s_.prio > m:
                    m = s_.prio
            o.prio = m + o.dur + LAT_X
        free = {e: 0.0 for e in self.ENGS}
        ready = [o for o in ops if o.npred == 0]
        out = []
        rt = {}
        for o in ready:
            rt[o] = 0.0
        while ready:
            best = None
            bs = None
            for o in ready:
                st = rt[o]
                if free[o.eng] > st:
                    st = free[o.eng]
                key = (st, -o.prio, o.seq)
                if bs is None or key < bs:
                    bs = key
                    best = o
            ready.remove(best)
            st = bs[0]
            extra = 0.0
            if best.dma is not None:
                extra = 2.0
                fin_issue = st + 0.1
                free[best.eng] = fin_issue
                best.fin = fin_issue + best.dur + extra
            else:
                best.fin = st + best.dur
                free[best.eng] = best.fin
            out.append(best)
            for s_ in best.succ:
                s_.npred -= 1
                lat = LAT_S if (s_.eng == best.eng and best in s_.odeps) else LAT_X
                t_ = (best.fin + lat) if best not in s_.odeps else (st + 0.01)
                if t_ > rt.get(s_, 0.0):
                    rt[s_] = t_
                if s_.npred == 0:
                    ready.append(s_)
        assert len(out) == len(ops), (len(out), len(ops))
        return out

    def emit(self, stack):
        nc = self.nc
        nreg = self.region + 1
        regions = [[] for _ in range(nreg)]
        for o in self.ops:
            regions[o.region].append(o)
        self.q = {e: [] for e in self.ENGS}
        prev_last = {}
        prev_dmas = []
        for r in range(nreg):
            order = self._schedule_region(regions[r])
            first = {}
            for o in order:
                if o.eng not in first:
                    first[o.eng] = o
            if r > 0:
                for e, o in first.items():
                    for e2, l2 in prev_last.items():
                        if e2 == e == "pe" and l2.dma is None:
                            continue
                        o.deps.add(l2)
                    for d in prev_dmas:
                        o.deps.add(d)
            last = {}
            for o in order:
                last[o.eng] = o
                self.q[o.eng].append(o)
            for e, l2 in prev_last.items():
                if e not in last:
                    last[e] = l2
            prev_last = last
            prev_dmas = prev_dmas + [o for o in order if o.dma is not None]
        sems = {}
        for e in self.ENGS:
            sems[("e", e)] = stack.enter_context(nc.semaphore("s_" + e))
        dnames = sorted({o.dma for o in self.ops if o.dma is not None}, key=str)
        for d in dnames:
            sems[("d", d)] = stack.enter_context(nc.semaphore("d_" + str(d)))
        dcnt = {}
        for e in self.ENGS:
            for i, o in enumerate(self.q[e]):
                o.idx = i
                if o.dma is not None:
                    dcnt[o.dma] = dcnt.get(o.dma, 0) + 16
                    o.dval = dcnt[o.dma]
        for o in self.ops:
            for d in o.deps:
                if d.dma is None and d.fn is not None:
                    d.need = True
        for e in self.ENGS:
            c = 0
            for o in self.q[e]:
                if o.need:
                    c += 1
                o.ms = c
        block = stack.enter_context(nc.Block())
        engobj = {"pe": "tensor", "act": "scalar", "dve": "vector", "pool": "gpsimd", "sp": "sync"}

        def run(e):
            def body(eng):
                waited = {}
                for o in self.q[e]:
                    ws = {}
                    for d in o.deps:
                        if d.dma is None:
                            if d.eng == e and d.idx > o.idx:
                                raise RuntimeError("same-engine dependency scheduled out of order")
                            key = ("e", d.eng)
                            val = d.ms
                        else:
                            key = ("d", d.dma)
                            val = d.dval
                        if val > ws.get(key, 0):
                            ws[key] = val
                    for key, val in ws.items():
                        if val > waited.get(key, 0):
                            eng.wait_ge(sems[key], val)
                            waited[key] = val
                    if o.fn is None:
                        continue
                    ins = o.fn(eng)
                    if o.dma is not None:
                        ins.then_inc(sems[("d", o.dma)], 16)
                    elif o.need:
                        ins.then_inc(sems[("e", e)], 1)

            getattr(block, engobj[e])(body)

        for e in self.ENGS:
            if self.q[e]:
                run(e)


class Bump:
    def __init__(self, base, limit):
        self.o = base
        self.limit = limit

    def __call__(self, words):
        o = self.o
        self.o += (int(words) + 15) // 16 * 16
        assert self.o <= self.limit, (self.o, self.limit)
        return o


def col_lo(m):
    if m <= 1:
        return 0
    return 128 * ((m - 1 + 1) // 2)


def build_program(stop_after=99, taps=(), P1C=8, P1S=99, SCHED=True):
    nc = bass.Bass("TRN2", target_bir_lowering=False)

    def din(name, shape, dt=F32):
        return nc.dram_tensor(name, shape, dt, kind="ExternalInput").ap()

    xa = din("xa", [4096, 1024]); xo = din("xo", [2048, 1024])
    posa_d = din("posa", [128, 32], I32); poso_d = din("poso", [128, 16], I32)
    qrel_d = din("qrel", [128, 512]); qrel2_d = din("qrel2", [128, 1])
    ng_d = din("ng", [128, 8]); w_in = din("w_in", [1024, 4552]); bm_d = din("bm", [128, 16])
    gqn_d = din("gqn", [128, 2]); w_uq = din("w_uq", [256, 768]); gkvn_d = din("gkvn", [128, 1])
    w_ukv = din("w_ukv", [128, 1024])
    gq_d = din("gq", [128, 96]); gk_d = din("gk", [128, 96]); gqd_d = din("gqd", [128, 64]); gkd_d = din("gkd", [128, 64])
    wba_d = din("wba", [512, 1024]); wbd_d = din("wbd", [512, 1024]); wo_d = din("wo", [1024, 1024])
    y = nc.dram_tensor("y", [2048, 1024], F32, kind="ExternalOutput").ap()
    tap_out = {}
    w_in_v = w_in.rearrange("(kc p) n -> p kc n", p=128)

    with ExitStack() as st:
        NW = 52480
        A = st.enter_context(nc.sbuf_tensor("A", [128, NW], F32))
        PSP = [st.enter_context(nc.psum_tensor("psp%d" % i, [128, 1024], F32)) for i in range(4)]
        PS = [PSP[i // 2][:, (i % 2) * 512:(i % 2 + 1) * 512] for i in range(8)]
        p = Prog(nc, schedule=SCHED)

        def vw(off, words, dt=F32):
            a = A[:, off:off + int(words)]
            return a if dt == F32 else a.bitcast(dt)

        def _fsz(ap):
            n = 1
            for d_ in ap.shape[1:]:
                n *= int(d_)
            return n

        def E(eng, meth, reads, writes, *a, **kw):
            ap = kw.get("out", kw.get("in_", a[0] if a else None))
            n = _fsz(ap) if ap is not None else 64
            if meth in ("max", "match_replace"):
                n = _fsz(kw.get("in_", kw.get("in_values")))
            if eng == "act":
                dur = 0.22 + n / 1400.0
            elif eng == "pool":
                dur = 0.35 + n / 480.0
            else:
                dur = 0.14 + n / 960.0
            return p.op(eng, lambda e: getattr(e, meth)(*a, **kw), reads, writes, dur=dur)

        def DMA(eng, out, in_, sem, reads, writes, **kw):
            n = _fsz(out)
            return p.op(eng, lambda e: e.dma_start(out=out, in_=in_, **kw), reads, writes, dma=sem, dur=1.0 + n * 4 / 1500.0)

        def MM(out, lhsT, rhs, reads, writes, start=True, stop=True):
            n = _fsz(rhs)
            return p.op("pe", lambda e: e.matmul(out, lhsT=lhsT, rhs=rhs, start=start, stop=stop), reads, writes,
                        dur=0.06 + n / 1500.0)

        ntap = [0]

        def tap(name, ap, shape, key, dt=F32):
            if name not in taps:
                return
            t = nc.dram_tensor("tap_" + name, list(shape), dt, kind="ExternalOutput").ap()
            tap_out[name] = t
            ntap[0] += 1
            DMA("sp", t, ap, "tap%d" % ntap[0], [key] if not isinstance(key, list) else key, [])

        bank_rr = [0]

        def psk(i):
            return ("ps", i)

        P = Bump(0, NW)
        o_ident = P(64); ident = vw(o_ident, 64, BF16)
        o_onesf = P(64); ones_f = vw(o_onesf, 64)
        o_onesb = P(1); ones_b = vw(o_onesb, 1, BF16)
        o_ng = P(8); ng = vw(o_ng, 8)
        o_gqn = P(2); gqn = vw(o_gqn, 2)
        o_gkvn = P(1); gkvn = vw(o_gkvn, 1)
        o_bm = P(16); bm = vw(o_bm, 16)
        o_gq = P(96); gq = vw(o_gq, 96)
        o_gk = P(96); gk = vw(o_gk, 96)
        o_gqd = P(64); gqd = vw(o_gqd, 64)
        o_gkd = P(64); gkd = vw(o_gkd, 64)
        o_invf = P(28); invf = vw(o_invf, 28)
        o_qrel = P(512); qrel = vw(o_qrel, 512)
        o_qrel2 = P(1); qrel2 = vw(o_qrel2, 1)
        o_cbias = P(256); cbias = vw(o_cbias, 256)
        o_rkv = P(32); rkv = vw(o_rkv, 32)
        o_small = P(256)
        o_OTa = P(4096); OTa = vw(o_OTa, 4096, BF16).rearrange("p (c t) -> p c t", c=4)
        o_OTb = P(4096); OTb = vw(o_OTb, 4096, BF16).rearrange("p (c t) -> p c t", c=4)
        o_dead4 = P.o
        o_cosA = P(896); cosA = vw(o_cosA, 896).rearrange("p (t f) -> p t f", f=28)
        o_sinA = P(896); sinA = vw(o_sinA, 896).rearrange("p (t f) -> p t f", f=28)
        o_cosO = P(448); cosO = vw(o_cosO, 448).rearrange("p (t f) -> p t f", f=28)
        o_sinO = P(448); sinO = vw(o_sinO, 448).rearrange("p (t f) -> p t f", f=28)
        o_ckvT = P(2048); ckvT = vw(o_ckvT, 2048, BF16)
        o_kpe = P(1024); kpe = vw(o_kpe, 1024).rearrange("p (t f) -> p t f", f=32)
        o_kss = P(32); kss = vw(o_kss, 32)
        o_dead4_end = P.o
        ARENA = P.o
        assert ARENA % 16 == 0

        ss4 = vw(o_small, 4); rs4 = vw(o_small + 4, 4)
        ssn = vw(o_small + 8, 32); rsn = vw(o_small + 40, 32)
        sgn = vw(o_small + 72, 32)
        rq4 = vw(o_small + 104, 4)
        m8 = vw(o_small + 112, 8)
        thr = vw(o_small + 120, 1)
        ssk = vw(o_small + 124, 4)
        LO = vw(o_small + 150, 2); HI = vw(o_small + 152, 2); TC = vw(o_small + 154, 2); FB = vw(o_small + 156, 2)
        ta = vw(o_small + 158, 1); tf = vw(o_small + 159, 1); td = vw(o_small + 160, 1); te = vw(o_small + 161, 1)
        negbig = vw(o_small + 162, 1)
        ss4b = vw(o_small + 224, 4); rs4b = vw(o_small + 228, 4); sskb = vw(o_small + 232, 4)
        g2 = vw(o_small + 164, 2, I32); ng2 = vw(o_small + 166, 2, I32)
        u8 = vw(o_small + 168, 48); nvv = vw(o_small + 236, 1)
        io8 = vw(o_small + 238, 8); oh8 = vw(o_small + 246, 8); io8i = vw(o_small + 216, 8, I32)

        tmpc = Bump(ARENA, NW)
        o_t0 = tmpc(1024); o_t1 = tmpc(1024); o_t2 = tmpc(1024); o_t3 = tmpc(1024)
        idi = vw(o_t0, 128, I32); idf = vw(o_t1, 128)
        E("pool", "iota", [], ["idi"], idi, pattern=[[1, 128]], base=0, channel_multiplier=-1)
        E("dve", "tensor_copy", ["idi"], ["idf"], out=idf, in_=idi)
        E("dve", "tensor_scalar", ["idf"], ["ident"], out=ident, in0=idf, scalar1=0.0, scalar2=None, op0=ALU.is_equal)
        E("dve", "memset", [], ["ones_f"], ones_f, 1.0)
        E("dve", "memset", [], ["ones_b"], ones_b, 1.0)
        fr = []
        for rot in (32, 16, 8):
            for j in range(rot // 2):
                fr.append(float(np.float32(THETA) ** np.float32(-(2.0 * j) / rot)))
        for j, f in enumerate(fr):
            E("dve" if j % 2 == 0 else "pool", "memset", [], [("invf", j)], invf[:, j:j + 1], f)
        INVF = [("invf", j) for j in range(len(fr))]
        for nm, dst, src in (("ng", ng, ng_d), ("gqn", gqn, gqn_d), ("gkvn", gkvn, gkvn_d), ("bm", bm, bm_d),
                             ("gq", gq, gq_d), ("gk", gk, gk_d), ("gqd", gqd, gqd_d), ("gkd", gkd, gkd_d),
                             ("qrel", qrel, qrel_d), ("qrel2", qrel2, qrel2_d)):
            DMA("sp", dst, src, "c_" + nm, [], [nm])
        E("pool", "iota", [], ["io8i"], io8i, pattern=[[1, 8]], base=0, channel_multiplier=0)
        E("dve", "tensor_copy", ["io8i"], ["io8"], out=io8, in_=io8i)
        E("dve", "memset", [], ["negbig"], negbig, -0.5 * BIG)
        kii = vw(o_t0 + 128, 256, I32)
        kio = vw(o_t1 + 128, 256)
        E("pool", "iota", [], ["kii"], kii, pattern=[[1, 256]], base=0, channel_multiplier=0)
        E("dve", "tensor_copy", ["kii"], ["kio"], out=kio, in_=kii)
        E("dve", "tensor_scalar", ["kio", "qrel2"], ["cbias"], out=cbias, in0=kio, scalar1=qrel2[:, 0:1], scalar2=-BIG,
          op0=ALU.is_gt, op1=ALU.mult)

        o_t4 = tmpc(1024); o_t5 = tmpc(1024)
        o_rsc = {"A": (o_t2, o_t3, o_t4, o_t5), "O": tuple(tmpc(1024) for _ in range(4))}

        def rope_tables(pos_d, ntile, cosT, sinT, nm):
            q2, q3, q4, q5 = o_rsc[nm]
            pi_ = vw(q2, ntile, I32)
            pf = vw(q2 + 64, ntile)
            ang = vw(q3, ntile * 28).rearrange("p (t f) -> p t f", f=28)
            uu = vw(q4, ntile * 28).rearrange("p (t f) -> p t f", f=28)
            kk = vw(q5, ntile * 28).rearrange("p (t f) -> p t f", f=28)
            DMA("sp", pi_, pos_d, "c_pos" + nm, [], ["pi" + nm])
            E("dve", "tensor_copy", ["pi" + nm], ["pf" + nm], out=pf, in_=pi_)
            E("dve", "tensor_tensor", ["pf" + nm] + INVF, ["ang" + nm], out=ang,
              in0=pf.unsqueeze(2).to_broadcast([128, ntile, 28]),
              in1=invf.unsqueeze(1).to_broadcast([128, ntile, 28]), op=ALU.mult)
            E("dve", "tensor_scalar", ["ang" + nm], ["ang" + nm], out=ang, in0=ang, scalar1=1.0 / TWO_PI, scalar2=None,
              op0=ALU.mult)
            for dst, shift in ((sinT, 0.0), (cosT, 0.25)):
                E("dve", "tensor_scalar", ["ang" + nm], ["uu" + nm], out=uu, in0=ang, scalar1=shift, scalar2=None, op0=ALU.add)
                E("dve", "tensor_scalar", ["uu" + nm], ["rk" + nm], out=kk, in0=uu, scalar1=MAGIC, scalar2=MAGIC,
                  op0=ALU.add, op1=ALU.subtract)
                E("dve", "tensor_tensor", ["uu" + nm, "rk" + nm], ["rk" + nm], out=kk, in0=uu, in1=kk, op=ALU.subtract)
                E("act", "activation", ["rk" + nm], ["tab" + nm], out=dst, in_=kk, func=AF.Sin, scale=TWO_PI * (1.0 - 1e-6))

        rope_tables(posa_d, 32, cosA, sinA, "A")
        rope_tables(poso_d, 16, cosO, sinO, "O")
        ROPE_A = ["tabA"]
        ROPE_O = ["tabO"]

        def load_x(src_rows, xs, xskey, semname, alias_keys=()):
            DMA("sp", xs, src_rows.rearrange("(t p) d -> p t d", p=128), semname, [], [xskey] + list(alias_keys))

        def make_hT(src_rows, xs, xskey, hb, hT, junk, semname, tb=(0, 1, 2, 3), hkey="hT", alias_keys=(), do_load=True,
                    sfx="", st4=None):
            ss4_, rs4_ = (ss4, rs4) if st4 is None else st4
            kj, kr = "junk" + sfx, "rs4" + sfx
            if do_load:
                load_x(src_rows, xs, xskey, semname, alias_keys)
            for t in range(4):
                E("act", "activation", [xskey], [kj, ("ss4" + sfx, t)], out=junk, in_=xs[:, t, :], func=AF.Square,
                  accum_out=ss4_[:, t:t + 1])
            E("act", "activation", [("ss4" + sfx, t) for t in range(4)], [kr], out=rs4_, in_=ss4_, func=AF.Ln,
              scale=1.0 / 1024, bias=EPS)
            E("act", "activation", [kr], [kr], out=rs4_, in_=rs4_, func=AF.Exp, scale=-0.5)
            for t in range(4):
                if t % 2 == 0:
                    E("dve", "tensor_scalar", [xskey, kr], [("hb" + sfx, t)], out=hb[:, t, :], in0=xs[:, t, :],
                      scalar1=rs4_[:, t:t + 1], scalar2=None, op0=ALU.mult)
                else:
                    E("pool", "tensor_scalar", [xskey, kr], [("hb" + sfx, t)], out=hb[:, t, :], in0=xs[:, t, :],
                      scalar1=rs4_[:, t:t + 1], scalar2=0.0, op0=ALU.mult, op1=ALU.add)
            for kc in range(8):
                b = tb[kc % len(tb)]
                for t in range(4):
                    MM(PS[b][:, t * 128:(t + 1) * 128], hb[:, t, kc * 128:(kc + 1) * 128], ident,
                       [("hb" + sfx, t), "ident"], [psk(b)])
                if kc % 2 == 0:
                    E("act", "activation", [psk(b), "ng"], [(hkey, kc)], out=hT[:, kc, :], in_=PS[b][:], func=AF.Identity,
                      scale=ng[:, kc:kc + 1])
                else:
                    E("dve", "tensor_scalar", [psk(b), "ng"], [(hkey, kc)], out=hT[:, kc, :], in0=PS[b][:],
                      scalar1=ng[:, kc:kc + 1], scalar2=None, op0=ALU.mult)

        HT_KEYS = [("hT", kc) for kc in range(8)]

        def head_prep(W3, W4, n, D, Wk, sq, ssv, rsv, gain, gkey, rope, cosv, sinv, tkey, T, H, rt, sfx=""):
            ksq, kss, krs = "sq" + sfx, "ssn" + sfx, "rsn" + sfx
            if W3 is not None:
                E("act", "activation", [Wk], [ksq], out=sq, in_=W3, func=AF.Square)
                E("dve", "tensor_reduce", [ksq], [kss], out=ssv, in_=sq, axis=AX.X, op=ALU.add)
                E("act", "activation", [kss], [krs], out=rsv, in_=ssv, func=AF.Ln, scale=1.0 / D, bias=EPS)
                E("act", "activation", [krs], [krs], out=rsv, in_=rsv, func=AF.Exp, scale=-0.5)
                E("dve", "tensor_tensor", [Wk, krs], [Wk], out=W3, in0=W3, in1=rsv.unsqueeze(2).to_broadcast([128, n, D]),
                  op=ALU.mult)
                E("dve", "tensor_tensor", [Wk, gkey], [Wk], out=W3, in0=W3, in1=gain.unsqueeze(1).to_broadcast([128, n, D]),
                  op=ALU.mult)
            if rope is not None:
                ro, r = rope
                r2 = r // 2
                x1 = W4[:, :, :, ro:ro + r2]
                x2 = W4[:, :, :, ro + r2:ro + r]
                c = cosv.unsqueeze(2).to_broadcast([128, T, H, r2])
                s = sinv.unsqueeze(2).to_broadcast([128, T, H, r2])
                t1, t2, t3, t4 = [rt[k][:, 0:T * H * r2].rearrange("p (t h d) -> p t h d", t=T, h=H) for k in range(4)]
                rk = [("rt" + sfx, k) for k in range(4)]
                E("dve", "tensor_tensor", [Wk] + tkey, [rk[0]], out=t1, in0=x1, in1=c, op=ALU.mult)
                E("dve", "tensor_tensor", [Wk] + tkey, [rk[1]], out=t2, in0=x2, in1=s, op=ALU.mult)
                E("dve", "tensor_tensor", [Wk] + tkey, [rk[2]], out=t3, in0=x2, in1=c, op=ALU.mult)
                E("dve", "tensor_tensor", [Wk] + tkey, [rk[3]], out=t4, in0=x1, in1=s, op=ALU.mult)
                E("dve", "tensor_tensor", [rk[0], rk[1]], [Wk], out=x1, in0=t1, in1=t2, op=ALU.subtract)
                E("dve", "tensor_tensor", [rk[2], rk[3]], [Wk], out=x2, in0=t3, in1=t4, op=ALU.add)

        def finalize(bank, dest, dkey, rd, bcs, bcbank):
            E("act", "activation", [psk(bank)], ["rd"], out=rd[64:65, :], in_=PS[bank][64:65, :], func=AF.Ln)
            MM(PS[bcbank][0:64, :], ones_f[64:65, 0:64], rd[64:65, :], ["ones_f", "rd"], [psk(bcbank)])
            E("act", "activation", [psk(bcbank)], ["bcs"], out=bcs[0:64, :], in_=PS[bcbank][0:64, :], func=AF.Exp, scale=-1.0)
            E("dve", "tensor_tensor", [psk(bank), "bcs"], [dkey], out=dest, in0=PS[bank][0:64, :], in1=bcs[0:64, :],
              op=ALU.mult)

        def load_w(dst, src, sem, key):
            DMA("pool", dst, src, sem, [], [key])

        p.barrier()
        ar = Bump(ARENA, NW)
        o_KTb = ar(4096)
        KTz = [vw(o_KTb, 2048, BF16), vw(o_KTb + 2048, 2048, BF16)]
        KTb = KTz[0]
        o_Vb = ar(2080); Vb = vw(o_Vb, 2080, BF16).rearrange("p (t c) -> p t c", c=130)
        Vb4 = vw(o_Vb, 2080, BF16).rearrange("p (t h c) -> p t h c", h=2, c=65)
        o_KTi = ar(2048); KTi = vw(o_KTi, 2048, BF16)
        DSAK_END = ar.o
        o_WK = ar(1792); WK = vw(o_WK, 1792, BF16).rearrange("p (k n) -> p k n", k=8)
        P1 = []
        for par_ in range(2):
            d_ = {}
            d_["xs"] = vw(ar(4096), 4096).rearrange("p (t d) -> p t d", t=4)
            d_["hb"] = vw(ar(2048), 2048, BF16).rearrange("p (t d) -> p t d", t=4)
            d_["hT"] = vw(ar(2048), 2048, BF16).rearrange("p (k n) -> p k n", k=8)
            d_["junk"] = vw(ar(512), 512, BF16)
            o_ = ar(1280); d_["Wt"] = vw(o_, 1280).rearrange("p (t n) -> p t n", t=4)
            o_ = ar(512); d_["o_kbw"] = o_
            d_["kbw3"] = vw(o_, 512).rearrange("p (n d) -> p n d", d=64)
            d_["kbw4"] = vw(o_, 512).rearrange("p (t h d) -> p t h d", t=4, h=2)
            d_["sq3"] = vw(ar(512), 512).rearrange("p (n d) -> p n d", d=64)
            d_["rt"] = [vw(ar(64), 64) for _ in range(4)]
            d_["Kb"] = vw(ar(256), 256, BF16).rearrange("p (t n) -> p t n", t=4)
            o_ = ar(192); d_["Ks"] = vw(o_, 192, BF16).rearrange("p (t n) -> p t n", t=4)
            d_["Ks5"] = vw(o_, 192, BF16).rearrange("p (t a d) -> p t a d", t=4, a=3)
            d_["kiw"] = vw(ar(128), 128).rearrange("p (t n) -> p t n", t=4)
            d_["sqb"] = vw(ar(256), 256, BF16)
            d_["sqp"] = vw(ar(128), 128).rearrange("p (t f) -> p t f", f=32)
            P1.append(d_)

        cols = [(CKV, 0), (KPE, 128), (KB, 160), (VB, 288), (KI, 416)]
        for i, ((c0, c1), o0) in enumerate(cols):
            load_w(WK[:, :, o0:o0 + (c1 - c0)], w_in_v[:, :, c0:c1], "wk%d" % i, ("WK", i))
        WKK = [("WK", i) for i in range(5)]
        tap("WK", vw(o_WK, 1792, BF16), [128, 3584], WKK, BF16)
        E("pool", "memset", [], ["Vb"], Vb[:, :, 64:65], 1.0)
        E("pool", "memset", [], ["Vb"], Vb[:, :, 129:130], 1.0)
        E("pool", "memset", [], ["KTz"], KTz[0][64:128, :], 0.0)
        E("pool", "memset", [], ["KTz"], KTz[1][0:64, :], 0.0)

        def p1_gen(par):
            d = P1[par]
            sx = "p%d" % par
            bA = 4 + 2 * par
            bB = 5 + 2 * par
            tb = (0, 1) if par == 0 else (2, 3)
            st4 = (ss4, rs4) if par == 0 else (ss4b, rs4b)
            ssk_ = ssk if par == 0 else sskb
            HK = "hT" + sx
            hT = d["hT"]; Wt = d["Wt"]; sqb = d["sqb"]; Kb = d["Kb"]; Ks = d["Ks"]; Ks5 = d["Ks5"]; kiw = d["kiw"]
            for c in range(par, P1C, 2):
                make_hT(xa[c * 512:(c + 1) * 512, :], d["xs"], "xs" + sx, d["hb"], hT, d["junk"], "xs%d" % par, tb=tb, hkey=HK, sfx=sx, st4=st4)
                yield 12.0
                for kc in range(8):
                    MM(PS[bA][:], WK[:, kc, 0:128], hT[:, kc, :], WKK + [(HK, kc)], [psk(bA)], start=(kc == 0), stop=(kc == 7))
                E("dve", "tensor_scalar", [psk(bA), "gkvn"], [("ckvT", c)], out=ckvT[:, c * 512:(c + 1) * 512], in0=PS[bA][:],
                  scalar1=gkvn[:, 0:1], scalar2=None, op0=ALU.mult)
                E("act", "activation", [psk(bA)], ["sqb" + sx], out=sqb, in_=PS[bA][:], func=AF.Square)
                for t in range(4):
                    MM(PS[bB][:, 2 * t:2 * t + 2], sqb[:, t * 128:(t + 1) * 128], ones_b[:, 0:2], ["sqb" + sx, "ones_b"], [psk(bB)])
                E("act", "activation", [psk(bB)], ["ssk" + sx], out=ssk_, in_=PS[bB][:, 0:8].rearrange("p (t two) -> p t two", two=2)[:, :, 0],
                  func=AF.Ln, scale=1.0 / 128, bias=EPS)
                E("act", "activation", ["ssk" + sx], [("rkv", c)], out=rkv[:, c * 4:(c + 1) * 4], in_=ssk_, func=AF.Exp, scale=-0.5)
                yield 5.0
                for t in range(4):
                    for kc in range(8):
                        MM(PS[bB][:, 0:320], hT[:, kc, t * 128:(t + 1) * 128], WK[:, kc, 128:448], WKK + [(HK, kc)], [psk(bB)],
                           start=(kc == 0), stop=(kc == 7))
                    E("act", "activation", [psk(bB)], [("Wt" + sx, t)], out=Wt[:, t, :], in_=PS[bB][:, 0:320], func=AF.Copy)
                    yield 2.5
                WtK = [("Wt" + sx, t) for t in range(4)]
                kpc = kpe[:, c * 4:(c + 1) * 4, :]
                E("pool", "tensor_copy", WtK, [("kpe", c)], out=kpc, in_=Wt[:, :, 0:32])
                E("act", "activation", [("kpe", c)], ["sqp" + sx], out=d["sqp"], in_=kpc, func=AF.Square)
                E("dve", "tensor_reduce", ["sqp" + sx], [("kss", c)], out=kss[:, c * 4:(c + 1) * 4], in_=d["sqp"], axis=AX.X, op=ALU.add)
                E("dve", "tensor_tensor", [("kpe", c), "gk", "sqp" + sx], [("kpe", c)], out=kpc, in0=kpc,
                  in1=gk[:, 64:96].unsqueeze(1).to_broadcast([128, 4, 32]), op=ALU.mult)
                head_prep(None, kpc.unsqueeze(2), 4, 32, ("kpe", c), None, None, None, None, None, (0, 32),
                          cosA[:, c * 4:(c + 1) * 4, 0:16], sinA[:, c * 4:(c + 1) * 4, 0:16], ROPE_A, 4, 1, d["rt"], sfx=sx)
                E("pool", "tensor_copy", WtK, ["Vb"], out=Vb4[:, c * 4:(c + 1) * 4, :, 0:64],
                  in_=Wt[:, :, 160:288].rearrange("p t (h d) -> p t h d", h=2))
                E("pool", "tensor_copy", WtK, ["kbw" + sx], out=d["kbw4"], in_=Wt[:, :, 32:160].rearrange("p t (h d) -> p t h d", h=2))
                head_prep(d["kbw3"], d["kbw4"], 8, 64, "kbw" + sx, d["sq3"], ssn[:, par * 8:par * 8 + 8], rsn[:, par * 8:par * 8 + 8], gkd, "gkd", (0, 16),
                          cosA[:, c * 4:(c + 1) * 4, 16:24], sinA[:, c * 4:(c + 1) * 4, 16:24], ROPE_A, 4, 2, d["rt"], sfx=sx)
                yield 14.0
                E("act", "activation", ["kbw" + sx], ["Kb" + sx], out=Kb, in_=vw(d["o_kbw"], 512).rearrange("p (t n) -> p t n", t=4), func=AF.Copy)
                for t in range(4):
                    MM(PS[bA][:, t * 128:(t + 1) * 128], Kb[:, t, :], ident, ["Kb" + sx, "ident"], [psk(bA)])
                E("dve", "tensor_copy", [psk(bA), "KTz"], [("KTb", c)], out=KTz[0][0:64, c * 512:(c + 1) * 512], in_=PS[bA][0:64, :])
                E("act", "activation", [psk(bA), "KTz"], [("KTb", c)], out=KTz[1][64:128, c * 512:(c + 1) * 512], in_=PS[bA][64:128, :], func=AF.Copy)
                E("pool", "tensor_copy", WtK, ["kiw" + sx], out=kiw, in_=Wt[:, :, 288:320])
                head_prep(None, kiw.unsqueeze(2), 4, 32, "kiw" + sx, None, None, None, None, None, (0, 8),
                          cosA[:, c * 4:(c + 1) * 4, 24:28], sinA[:, c * 4:(c + 1) * 4, 24:28], ROPE_A, 4, 1, d["rt"], sfx=sx)
                E("dve", "tensor_copy", ["kiw" + sx], ["Ks" + sx], out=Ks5[:, :, 0, :], in_=kiw)
                E("dve", "tensor_tensor", ["kiw" + sx, "Ks" + sx], ["Ks" + sx], out=Ks5[:, :, 1, :], in0=kiw, in1=Ks5[:, :, 0, :], op=ALU.subtract)
                E("pool", "tensor_copy", ["Ks" + sx], ["Ks" + sx], out=Ks5[:, :, 2, :], in_=Ks5[:, :, 0, :])
                for t in range(4):
                    MM(PS[bB][0:96, t * 128:(t + 1) * 128], Ks[:, t, :], ident, ["Ks" + sx, "ident"], [psk(bB)])
                E("act", "activation", [psk(bB)], [("KTi", c)], out=KTi[0:96, c * 512:(c + 1) * 512], in_=PS[bB][0:96, :], func=AF.Copy)
                yield 9.0

        def _ileave(ga, gb):
            ta = tb_ = 0.0
            ea = eb = False
            while not (ea and eb):
                if not ea and (eb or ta <= tb_):
                    try:
                        ta += next(ga) or 1.0
                    except StopIteration:
                        ea = True
                else:
                    try:
                        tb_ += next(gb) or 1.0
                    except StopIteration:
                        eb = True

        def _stagger(g, t0):
            yield t0
            yield from g

        _ileave(p1_gen(0), _stagger(p1_gen(1), 25.0))

        KTbK = [("KTb", c) for c in range(8)]
        KTiK = [("KTi", c) for c in range(8)]
        tap("KTb", KTz[0], [128, 4096], KTbK, BF16)
        tap("KTb1", KTz[1], [128, 4096], KTbK, BF16)
        tap("KTi", KTi[0:96, :], [96, 4096], KTiK, BF16)
        tap("Vb", vw(o_Vb, 2080, BF16), [128, 4160], ["Vb"], BF16)
        tap("ckvT", ckvT, [128, 4096], [("ckvT", c) for c in range(8)], BF16)
        tap("rkv", rkv, [128, 32], [("rkv", c) for c in range(8)])
        tap("kpe", vw(o_kpe, 1024), [128, 1024], [("kpe", c) for c in range(8)])
        tap("cosA", vw(o_cosA, 896), [128, 896], ["tabA"])
        tap("sinA", vw(o_sinA, 896), [128, 896], ["tabA"])
        p.barrier()

        if stop_after >= 2:
            U8 = mybir.dt.uint8
            ar = Bump(DSAK_END, NW)
            o_WQb = ar(3104); WQb = vw(o_WQb, 3104, BF16).rearrange("p (k n) -> p k n", k=8)
            o_mT = ar(7168)
            ORDER2 = [0, 1, 3, 2]
            BUF2 = {jj: pos % 2 for pos, jj in enumerate(ORDER2)}
            mTp = [vw(o_mT, 4096, U8).rearrange("p (k q) -> p k q", q=512),
                   vw(o_mT + 4096, 3072, U8).rearrange("p (k q) -> p k q", q=512)]
            o_QTb = ar(1024)
            QTb2 = [vw(o_QTb, 1024, BF16).rearrange("p (a q) -> p a q", a=4),
                    vw(o_cosA, 1024, BF16).rearrange("p (a q) -> p a q", a=4)]
            o_hT = ar(2048); hT = vw(o_hT, 2048, BF16).rearrange("p (k n) -> p k n", k=8)
            o_QTi = ar(2048); QTi = vw(o_QTi, 2048, BF16).rearrange("p (h q) -> p h q", h=8)
            R0 = ar.o
            rs_ = Bump(R0, NW)
            o_hb = rs_(2048); hb = vw(o_hb, 2048, BF16).rearrange("p (t d) -> p t d", t=4)
            o_xs2 = rs_(4096); xs2 = vw(o_xs2, 4096).rearrange("p (t d) -> p t d", t=4)
            sqq3 = vw(o_xs2, 2048).rearrange("p (n d) -> p n d", d=64)
            Qs = vw(o_xs2 + 2048, 1536, BF16).rearrange("p (t n) -> p t n", t=4)
            o_Wqb = rs_(2048); Wqb3 = vw(o_Wqb, 2048).rearrange("p (n d) -> p n d", d=64)
            Wqb4 = vw(o_Wqb, 2048).rearrange("p (t h d) -> p t h d", t=4, h=8)
            Wqbt = vw(o_Wqb, 2048).rearrange("p (t n) -> p t n", t=4)
            o_Wqi = rs_(1056); Wqi = vw(o_Wqi, 1056).rearrange("p (t n) -> p t n", t=4)
            o_Qb = rs_(1024); Qb = vw(o_Qb, 1024, BF16).rearrange("p (t n) -> p t n", t=4)
            Qb5 = vw(o_Qb, 1024, BF16).rearrange("p (t a g d) -> p t a g d", t=4, a=4, g=2)
            rtq = [vw(o_xs2 + 256 * k_, 256) for k_ in range(4)]
            o_wsc = rs_(32); wsc = vw(o_wsc, 32)
            junk = vw(o_Qb, 512, BF16)
            assert rs_.o <= NW - 2560
            rm_ = Bump(R0, NW)
            o_S = rm_(4096)
            Sv2 = [vw(o_S, 4096), vw(o_OTa, 4096)]
            o_wrk = rm_(4096); wrk = vw(o_wrk, 4096)
            o_M = rm_(2048); Mv = vw(o_M, 2048, BF16)
            assert rm_.o <= NW - 2560
            ra_ = Bump(NW - 2560, NW)
            PT = [vw(ra_(256), 256, BF16) for _ in range(4)]
            o_rd = ra_(512); rd = vw(o_rd, 512)
            o_bcs = ra_(512); bcs = vw(o_bcs, 512)
            o_osb = ra_(512); osb = vw(o_osb, 512)
            S0 = ("S", 0)
            p.alias = {("hb", 0): [S0], ("hb", 1): [S0], ("hb", 2): [S0], ("hb", 3): [S0], "xs2": [S0, "wrk"],
                       "Wqb": ["wrk"], "Wqi": ["M"], "Qb": ["M", "junk"], "junk": ["M", "Qb"], "sq": [S0], "Qs": ["wrk"],
                       ("rt", 0): [S0], ("rt", 1): [S0], ("rt", 2): [S0], ("rt", 3): [S0]}
            assert o_hb + 2048 <= o_S + 4096 and o_xs2 + 4096 <= o_wrk + 4096 and o_Wqb >= o_wrk and o_Wqb + 2048 <= o_wrk + 4096
            assert o_Wqi >= o_M and o_Qb + 1024 <= NW - 2560

            for i, ((c0, c1), o0) in enumerate([(QB, 0), (QI, 512), (WI, 768)]):
                load_w(WQb[:, :, o0:o0 + (c1 - c0)], w_in_v[:, :, c0:c1], "wqb%d" % i, ("WQb", i))
            WQK = [("WQb", i) for i in range(3)]

            def start2(j):
                QTb = QTb2[BUF2[j]]
                make_hT(xo[j * 512:(j + 1) * 512, :], xs2, "xs2", hb, hT, junk, "xs2")
                for t in range(4):
                    for kc in range(8):
                        MM(PS[4][:], hT[:, kc, t * 128:(t + 1) * 128], WQb[:, kc, 0:512], WQK + [("hT", kc)], [psk(4)],
                           start=(kc == 0), stop=(kc == 7))
                    for kc in range(8):
                        MM(PS[5][:, 0:264], hT[:, kc, t * 128:(t + 1) * 128], WQb[:, kc, 512:776], WQK + [("hT", kc)], [psk(5)],
                           start=(kc == 0), stop=(kc == 7))
                    E("act", "activation", [psk(4)], ["Wqb"], out=Wqbt[:, t, :], in_=PS[4][:], func=AF.Copy)
                    E("dve", "tensor_copy", [psk(5)], ["Wqi"], out=Wqi[:, t, :], in_=PS[5][:, 0:264])
                head_prep(Wqb3, Wqb4, 32, 64, "Wqb", sqq3, ssn, rsn, gqd, "gqd", (0, 16),
                          cosO[:, j * 4:(j + 1) * 4, 16:24], sinO[:, j * 4:(j + 1) * 4, 16:24], ROPE_O, 4, 8, rtq)
                E("act", "activation", ["Wqb"], ["Qb"], out=Qb5[:, :, :, 0, :], in_=Wqb4[:, :, 0:4, :], func=AF.Copy)
                E("pool", "tensor_copy", ["Wqb"], ["Qb"], out=Qb5[:, :, :, 1, :], in_=Wqb4[:, :, 4:8, :])
                for a in range(4):
                    b = a % 2
                    for t in range(4):
                        MM(PS[b][:, t * 128:(t + 1) * 128], Qb[:, t, a * 128:(a + 1) * 128], ident, ["Qb", "ident"], [psk(b)])
                    if a % 2 == 0:
                        E("act", "activation", [psk(b)], [("QTb", BUF2[j], a)], out=QTb[:, a, :], in_=PS[b][:], func=AF.Copy)
                    else:
                        E("dve", "tensor_copy", [psk(b)], [("QTb", BUF2[j], a)], out=QTb[:, a, :], in_=PS[b][:])
                sg3 = sgn.rearrange("p (t h) -> p t h", t=4)
                ws3 = wsc.rearrange("p (t h) -> p t h", t=4)
                E("act", "activation", ["Wqi"], ["sgn"], out=sg3, in_=Wqi[:, :, 256:264], func=AF.Sign)
                E("dve", "scalar_tensor_tensor", ["Wqi", "sgn"], ["wsc"], out=ws3, in0=Wqi[:, :, 256:264], scalar=1.0 / 16.0, in1=sg3,
                  op0=ALU.mult, op1=ALU.mult)
                Wqi4 = Wqi[:, :, 0:256].rearrange("p t (h d) -> p t h d", h=8)
                head_prep(None, Wqi4, 32, 32, "Wqi", None, None, None, None, None, (0, 8),
                          cosO[:, j * 4:(j + 1) * 4, 24:28], sinO[:, j * 4:(j + 1) * 4, 24:28], ROPE_O, 4, 8, rtq)
                E("dve", "tensor_tensor", ["Wqi", "wsc"], ["Wqi"], out=Wqi4, in0=Wqi4,
                  in1=ws3.unsqueeze(3).to_broadcast([128, 4, 8, 32]), op=ALU.mult)
                Qs6 = vw(o_xs2 + 2048, 1536, BF16).rearrange("p (t h a d) -> p t h a d", t=4, h=8, a=3)
                E("dve", "tensor_copy", ["Wqi", "sq"], ["Qs"], out=Qs6[:, :, :, 0, :], in_=Wqi4)
                E("dve", "tensor_tensor", ["Wqi", "Qs"], ["Qs"], out=Qs6[:, :, :, 2, :], in0=Wqi4, in1=Qs6[:, :, :, 0, :],
                  op=ALU.subtract)
                E("pool", "tensor_copy", ["Qs"], ["Qs"], out=Qs6[:, :, :, 1, :], in_=Qs6[:, :, :, 0, :])
                for hh in range(8):
                    b = hh % 2
                    for t in range(4):
                        MM(PS[b][0:96, t * 128:(t + 1) * 128], Qs[:, t, hh * 96:(hh + 1) * 96], ident, ["Qs", "ident"], [psk(b)])
                    if hh % 2 == 0:
                        E("act", "activation", [psk(b)], [("QTi", hh)], out=QTi[0:96, hh, :], in_=PS[b][0:96, :], func=AF.Copy)
                    else:
                        E("dve", "tensor_copy", [psk(b)], [("QTi", hh)], out=QTi[0:96, hh, :], in_=PS[b][0:96, :])

            PAIRS = [1, 2]
            ntr = [0]
            npair = [0]
            ntile = [0]

            def idx_gen(j):
                maskT = mTp[BUF2[j]]
                pend = []
                for i in range(4):
                    nkt = 8 * j + 2 * i + 2
                    n = nkt * 128
                    sbi = ntile[0] % 2
                    ntile[0] += 1
                    Sv = Sv2[sbi]
                    SK = ("S", sbi)
                    ks = 0
                    while ks < n:
                        wd = min(512, n - ks)
                        for hp in range(4):
                            pp = PAIRS[npair[0] % len(PAIRS)]
                            npair[0] += 1
                            pkey = ("psp", pp)
                            for e_ in range(2):
                                hh = 2 * hp + e_
                                MM(PSP[pp][:, e_ * 512:e_ * 512 + wd], QTi[0:96, hh, i * 128:(i + 1) * 128], KTi[0:96, ks:ks + wd],
                                   [("QTi", hh)] + KTiK, [pkey])
                            pv_ = PSP[pp][:, :].rearrange("p (e c) -> p e c", e=2)[:, :, 0:wd]
                            E("act", "activation", [pkey], [pkey], out=pv_, in_=pv_, func=AF.Relu)
                            for e_ in range(2):
                                hh = 2 * hp + e_
                                sc = sgn[:, i * 8 + hh:i * 8 + hh + 1]
                                rr = PSP[pp][:, e_ * 512:e_ * 512 + wd]
                                if hh == 0:
                                    E("act", "activation", [pkey, "sgn"], [SK], out=Sv[:, ks:ks + wd], in_=rr, func=AF.Identity, scale=sc)
                                else:
                                    E("dve", "scalar_tensor_tensor", [pkey, "sgn", SK], [SK], out=Sv[:, ks:ks + wd],
                                      in0=rr, scalar=sc, in1=Sv[:, ks:ks + wd], op0=ALU.mult, op1=ALU.add)
                            yield 1.7
                        ks += wd
                        while pend:
                            pend.pop(0)()
                    E("dve", "tensor_tensor", [SK, "cbias"], [SK], out=Sv[:, n - 256:n], in0=Sv[:, n - 256:n], in1=cbias, op=ALU.add)
                    if n <= 256:
                        E("dve", "tensor_scalar", [SK], ["M"], out=Mv[:, 0:n], in0=Sv[:, 0:n], scalar1=-0.5 * BIG, scalar2=None,
                          op0=ALU.is_ge)
                    elif n <= 512:
                        for r in range(32):
                            src = Sv if r == 0 else wrk
                            E("dve", "max", [SK if r == 0 else "wrk"], ["m8"], out=m8, in_=src[:, 0:n])
                            if r < 31:
                                E("dve", "match_replace", ["m8", SK if r == 0 else "wrk"], ["wrk"], out=wrk[:, 0:n], in_to_replace=m8,
                                  in_values=src[:, 0:n], imm_value=-BIG)
                            if r % 4 == 3:
                                yield 8 * (n / 960.0 + 0.3)
                        E("dve", "tensor_scalar", ["m8"], ["thr"], out=thr, in0=m8[:, 7:8], scalar1=-0.5 * BIG, scalar2=None, op0=ALU.max)
                        E("dve", "tensor_scalar", [SK, "thr"], ["M"], out=Mv[:, 0:n], in0=Sv[:, 0:n], scalar1=thr[:, 0:1], scalar2=None,
                          op0=ALU.is_ge)
                    else:
                        m_ = n // 16
                        sub = wrk[:, 0:m_]
                        tmp2 = wrk[:, 256:256 + m_]
                        E("dve", "tensor_copy", [SK], ["wrk"], out=sub,
                          in_=Sv[:, 0:n].rearrange("p (a s) -> p a s", s=16)[:, :, 0])
                        E("dve", "tensor_scalar", ["wrk"], ["wrk2"], out=tmp2, in0=sub, scalar1=-0.5 * BIG, scalar2=2.0 * BIG,
                          op0=ALU.is_lt, op1=ALU.mult)
                        E("dve", "tensor_tensor", ["wrk", "wrk2"], ["wrk2"], out=tmp2, in0=tmp2, in1=sub, op=ALU.add)
                        E("dve", "tensor_reduce", ["wrk2"], ["fb"], out=FB[:, 0:1], in_=tmp2, axis=AX.X, op=ALU.min)
                        E("dve", "tensor_scalar", ["qrel2"], ["fb"], out=FB[:, 1:2], in0=qrel2, scalar1=float((8 * j + 2 * i) * 128 + 1),
                          scalar2=None, op0=ALU.add)
                        E("dve", "tensor_copy", ["fb"], ["nvv"], out=nvv, in_=FB[:, 1:2])
                        for r in range(6):
                            E("dve", "max", ["wrk"], ["u8"], out=u8[:, r * 8:(r + 1) * 8], in_=sub)
                            if r < 5:
                                E("dve", "match_replace", ["u8", "wrk"], ["wrk"], out=sub, in_to_replace=u8[:, r * 8:(r + 1) * 8],
                                  in_values=sub, imm_value=-BIG)
                        E("dve", "tensor_tensor", ["u8", "fb"], ["LO"], out=LO[:, 0:1], in0=u8[:, 31:32], in1=FB[:, 0:1], op=ALU.max)
                        E("dve", "tensor_tensor", ["u8", "fb"], ["fb"], out=FB[:, 0:1], in0=u8[:, 47:48], in1=FB[:, 0:1], op=ALU.max)
                        E("dve", "tensor_scalar", ["fb"], ["fb"], out=FB[:, 1:2], in0=FB[:, 1:2], scalar1=768.0, scalar2=None, op0=ALU.min)
                        E("dve", "tensor_scalar", [SK, "LO"], ["M", "LO"], out=Mv[:, 0:n], in0=Sv[:, 0:n], scalar1=LO[:, 0:1], scalar2=None,
                          op0=ALU.is_ge, op1=ALU.add, accum_out=LO[:, 1:2])
                        E("dve", "tensor_copy", ["u8"], ["HI"], out=HI[:, 0:1], in_=u8[:, 3:4])
                        E("dve", "memset", [], ["HI"], HI[:, 1:2], 64.0)
                        E("dve", "tensor_scalar", ["LO"], ["g2"], out=g2, in0=LO[:, 1:2].to_broadcast([128, 2]), scalar1=256.0, scalar2=None,
                          op0=ALU.is_lt)
                        E("dve", "copy_predicated", ["g2", "LO", "HI"], ["HI"], out=HI, mask=g2, data=LO)
                        E("dve", "copy_predicated", ["g2", "fb", "LO", "HI"], ["LO"], out=LO, mask=g2, data=FB)
                        yield n / 960.0 + 6.0
                        for it in range(7):
                            E("dve", "tensor_tensor", ["LO", "HI"], ["ta"], out=ta, in0=LO[:, 1:2], in1=HI[:, 1:2], op=ALU.subtract)
                            E("dve", "reciprocal", ["ta"], ["ta"], out=ta, in_=ta)
                            E("dve", "scalar_tensor_tensor", ["LO", "ta"], ["tf"], out=tf, in0=LO[:, 1:2], scalar=-259.5, in1=ta,
                              op0=ALU.add, op1=ALU.mult)
                            E("dve", "tensor_scalar", ["tf"], ["tf"], out=tf, in0=tf, scalar1=0.03, scalar2=0.97, op0=ALU.max, op1=ALU.min)
                            E("dve", "tensor_tensor", ["LO", "HI"], ["td"], out=td, in0=HI[:, 0:1], in1=LO[:, 0:1], op=ALU.subtract)
                            E("dve", "scalar_tensor_tensor", ["td", "tf", "LO"], ["TC"], out=TC[:, 0:1], in0=td, scalar=tf[:, 0:1], in1=LO[:, 0:1],
                              op0=ALU.mult, op1=ALU.add)
                            E("dve", "tensor_scalar", [SK, "TC"], ["M", "TC"], out=Mv[:, 0:n], in0=Sv[:, 0:n], scalar1=TC[:, 0:1], scalar2=None,
                              op0=ALU.is_ge, op1=ALU.add, accum_out=TC[:, 1:2])
                            E("dve", "tensor_scalar", ["TC"], ["g2"], out=g2, in0=TC[:, 1:2].to_broadcast([128, 2]), scalar1=256.0,
                              scalar2=None, op0=ALU.is_ge)
                            E("dve", "tensor_scalar", ["TC"], ["ng2"], out=ng2, in0=TC[:, 1:2].to_broadcast([128, 2]), scalar1=256.0,
                              scalar2=None, op0=ALU.is_lt)
                            E("dve", "copy_predicated", ["g2", "TC", "LO"], ["LO"], out=LO, mask=g2, data=TC)
                            E("dve", "copy_predicated", ["ng2", "TC", "HI"], ["HI"], out=HI, mask=ng2, data=TC)
                            yield n / 960.0 + 2.2
                        E("dve", "tensor_scalar", [SK, "LO"], ["M"], out=Mv[:, 0:n], in0=Sv[:, 0:n], scalar1=LO[:, 0:1], scalar2=-BIG,
                          op0=ALU.is_lt, op1=ALU.mult)
                        nh_ = (n // 2) // 64 * 64
                        E("pool", "tensor_tensor", ["M", SK], [("wrkh", 1)], out=wrk[:, nh_:n], in0=Mv[:, nh_:n], in1=Sv[:, nh_:n], op=ALU.subtract)
                        E("dve", "tensor_tensor", ["M", SK], [("wrkh", 0)], out=wrk[:, 0:nh_], in0=Mv[:, 0:nh_], in1=Sv[:, 0:nh_], op=ALU.subtract)
                        yield 2.0 * n / 960.0 + 1.0
                        E("dve", "max", ["wrk", ("wrkh", 0), ("wrkh", 1)], ["m8", "wrk"], out=m8, in_=wrk[:, 0:n])
                        E("dve", "tensor_scalar", ["LO"], ["te"], out=te, in0=LO[:, 1:2], scalar1=-256.0, scalar2=7.0, op0=ALU.add, op1=ALU.min)
                        E("dve", "tensor_scalar", ["te", "io8"], ["oh8"], out=oh8, in0=io8, scalar1=te[:, 0:1], scalar2=None, op0=ALU.is_equal)
                        E("dve", "tensor_tensor", ["oh8", "m8"], ["oh8"], out=oh8, in0=oh8, in1=m8, op=ALU.mult)
                        E("dve", "tensor_reduce", ["oh8"], ["thr"], out=thr, in_=oh8, axis=AX.X, op=ALU.add)
                        E("dve", "tensor_scalar", ["thr"], ["thr"], out=thr, in0=thr, scalar1=-1.0, scalar2=None, op0=ALU.mult)
                        E("dve", "tensor_tensor", ["LO", "HI"], ["ta"], out=ta, in0=LO[:, 1:2], in1=HI[:, 1:2], op=ALU.add)
                        E("dve", "tensor_scalar", ["ta", "g2"], ["g2"], out=g2[:, 0:1], in0=ta, scalar1=519.0, scalar2=None, op0=ALU.is_gt)
                        E("dve", "copy_predicated", ["g2", "thr", "HI"], ["thr"], out=thr, mask=g2[:, 0:1], data=HI[:, 0:1])
                        E("dve", "tensor_scalar", ["nvv", "g2"], ["g2"], out=g2[:, 0:1], in0=nvv, scalar1=256.5, scalar2=None, op0=ALU.is_lt)
                        E("dve", "copy_predicated", ["g2", "thr", "negbig"], ["thr"], out=thr, mask=g2[:, 0:1], data=negbig)
                        E("dve", "tensor_scalar", [SK, "thr"], ["M"], out=Mv[:, 0:n], in0=Sv[:, 0:n], scalar1=thr[:, 0:1], scalar2=None,
                          op0=ALU.is_ge)
                    yield 2.0 * n / 960.0 + 2.0

                    def mtrans(i=i, nkt=nkt):
                        g0 = 0
                        while g0 < nkt:
                            cnt = min(4, nkt - g0)
                            tbk = 0
                            ntr[0] += 1
                            for a in range(cnt):
                                MM(PS[tbk][:, a * 128:(a + 1) * 128], Mv[:, (g0 + a) * 128:(g0 + a + 1) * 128], ident, ["M", "ident"], [psk(tbk)])
                            E("act", "activation", [psk(tbk)], [("mT", BUF2[j], i)], out=maskT[:, g0:g0 + cnt, i * 128:(i + 1) * 128],
                              in_=PS[tbk][:, 0:cnt * 128].rearrange("p (a q) -> p a q", a=cnt), func=AF.Copy)
                            g0 += cnt

                    pend.append(mtrans)
                for f_ in pend:
                    f_()
                yield 1.0

            def attn_gen(j, sb, ob, bcb, mengs=("pool",)):
                maskT = mTp[BUF2[j]]
                QTb = QTb2[BUF2[j]]
                MTK = [("mT", BUF2[j], i) for i in range(4)]
                nsb = len(sb)
                la = nsb - 1
                for g in range(8):
                    r0 = (g // 4) * 64
                    a = g % 4
                    kvh = g // 4
                    steps = [(kt, col_lo(max(kt - 8 * j, 0))) for kt in range(8 * j + 8)]

                    def Sstep(s):
                        kt, cl = steps[s]
                        bk = sb[s % nsb]
                        MM(PS[bk][:, cl:512], KTz[kvh][:, kt * 128:(kt + 1) * 128], QTb[:, a, cl:512],
                           KTbK + [("QTb", BUF2[j], a)], [psk(bk)])

                    for s0 in range(min(la, len(steps))):
                        Sstep(s0)
                    for s, (kt, cl) in enumerate(steps):
                        if s + la < len(steps):
                            Sstep(s + la)
                        bk = sb[s % nsb]
                        pt = PT[s % 4]
                        pk = ("PT", s % 4)
                        E("act", "activation", [psk(bk)], [pk], out=pt[:, cl:512], in_=PS[bk][:, cl:512], func=AF.Exp, scale=0.125)
                        E(mengs[s % len(mengs)], "tensor_tensor", [pk] + MTK, [pk], out=pt[:, cl:512], in0=pt[:, cl:512],
                          in1=maskT[:, kt, cl:512], op=ALU.mult)

                        def PVstep(s2):
                            kt2, cl2 = steps[s2]
                            MM(PS[ob][0:65, cl2:512], Vb[:, kt2, kvh * 65:(kvh + 1) * 65], PT[s2 % 4][:, cl2:512], [("PT", s2 % 4), "Vb"],
                               [psk(ob)], start=(s2 == 0), stop=(s2 == len(steps) - 1))

                        if s >= 2:
                            PVstep(s - 2)
                        yield 1.0
                    for s2 in range(max(0, len(steps) - 2), len(steps)):
                        PVstep(s2)
                    E("act", "activation", [psk(ob)], ["rd"], out=rd[64:65, :], in_=PS[ob][64:65, :], func=AF.Ln)
                    E("act", "activation", [psk(ob)], ["osb"], out=osb[0:64, :], in_=PS[ob][0:64, :], func=AF.Copy)
                    MM(PS[bcb][0:64, :], ones_f[64:65, 0:64], rd[64:65, :], ["ones_f", "rd"], [psk(bcb)])
                    E("act", "activation", [psk(bcb)], ["bcs"], out=bcs[0:64, :], in_=PS[bcb][0:64, :], func=AF.Exp, scale=-1.0)
                    E("pool", "tensor_tensor", ["osb", "bcs"], [("OTb", j, g)],
                      out=OTb[(g % 2) * 64:(g % 2) * 64 + 64, g // 2, j * 512:(j + 1) * 512], in0=osb[0:64, :], in1=bcs[0:64, :], op=ALU.mult)
                    yield 3.0

            def run_gen(gen):
                for _ in gen:
                    pass

            def interleave(ga, gb):
                ta = tb = 0.0
                ea = eb = False
                while not (ea and eb):
                    if not ea and (eb or ta <= tb):
                        try:
                            ta += next(ga) or 1.0
                        except StopIteration:
                            ea = True
                    else:
                        try:
                            tb += next(gb) or 1.0
                        except StopIteration:
                            eb = True

            def n_idx(j):
                tot = 0
                for i in range(4):
                    n = (8 * j + 2 * i + 2) * 128
                    tot += 4 * ((n + 511) // 512)
                    tot += 0 if n <= 256 else (8 if n <= 512 else 9)
                    tot += 2
                return tot

            start2(ORDER2[0])
            run_gen(idx_gen(ORDER2[0]))
            for pos in range(4):
                j = ORDER2[pos]
                if pos + 1 < 4:
                    jn = ORDER2[pos + 1]
                    start2(jn)
                    interleave(attn_gen(j, [6, 1], 7, 6), idx_gen(jn))
                else:
                    run_gen(attn_gen(j, [0, 1, 2], 6, 7, mengs=("dve", "pool")))
            p.barrier()
            p.alias = {}
            tap("OTb", vw(o_OTb, 4096, BF16), [128, 8192], [("OTb", j, g) for j in range(4) for g in range(8)], BF16)

        if stop_after >= 3:
            ar = Bump(ARENA, NW)
            o_WQa = ar(1024); WQa = vw(o_WQa, 1024, BF16).rearrange("p (k n) -> p k n", k=8)
            o_wuq = ar(768); wuq = vw(o_wuq, 768, BF16).rearrange("p (k n) -> p k n", k=2)
            o_wukv = ar(512); wukv = vw(o_wukv, 512, BF16)
            o_cm = ar(2048); cmask = vw(o_cm, 2048, BF16).rearrange("p (m q) -> p m q", m=8)
            o_KTa = ar(8192); KTa = vw(o_KTa, 8192, BF16).rearrange("p (h s) -> p h s", h=4)
            o_Va = ar(4160); Va = vw(o_Va, 4160, BF16).rearrange("p (t c) -> p t c", c=260)
            Va4 = vw(o_Va, 4160, BF16).rearrange("p (t h c) -> p t h c", h=4, c=65)
            o_hT = ar(2048); hT = vw(o_hT, 2048, BF16).rearrange("p (k n) -> p k n", k=8)
            o_QTa = ar(1024); QTa = vw(o_QTa, 1024, BF16).rearrange("p (h q) -> p h q", h=4)
            o_junk = ar(512); junk = vw(o_junk, 512, BF16)
            o_cqT = ar(512); cqT = vw(o_cqT, 512, BF16).rearrange("p (f q) -> p f q", f=2)
            o_sqb2 = ar(512); sqb2 = vw(o_sqb2, 512, BF16).rearrange("p (f q) -> p f q", f=2)
            PT = [vw(ar(256), 256, BF16) for _ in range(4)]
            o_rd = ar(512); rd = vw(o_rd, 512)
            o_bcs = ar(512); bcs = vw(o_bcs, 512)
            R0 = ar.o
            rs_ = Bump(R0, NW)
            o_hb = rs_(2048); hb = vw(o_hb, 2048, BF16).rearrange("p (t d) -> p t d", t=4)
            o_xs3 = rs_(4096); xs3 = vw(o_xs3, 4096).rearrange("p (t d) -> p t d", t=4)
            sqa3 = vw(o_xs3, 1536).rearrange("p (n d) -> p n d", d=96)
            Qa = vw(o_xs3 + 1536, 768, BF16).rearrange("p (t n) -> p t n", t=4)
            o_Wqa = rs_(1536); Wqa3 = vw(o_Wqa, 1536).rearrange("p (n d) -> p n d", d=96)
            Wqa4 = vw(o_Wqa, 1536).rearrange("p (t h d) -> p t h d", t=4, h=4)
            Wqat = vw(o_Wqa, 1536).rearrange("p (t n) -> p t n", t=4)
            rta = [vw(o_xs3 + 256 * k_, 256) for k_ in range(4)]
            o_rta1 = rs_(1024)
            rk_ = Bump(R0, NW)
            KSET = []
            KSQ = []
            for par_ in range(2):
                o_w_ = rk_(1536); o_s_ = rk_(1536); o_k_ = rk_(768)
                KSQ.append(o_s_)
                KSET.append((vw(o_w_, 1536).rearrange("p (n d) -> p n d", d=96),
                             vw(o_w_, 1536).rearrange("p (t h d) -> p t h d", t=4, h=4),
                             vw(o_s_, 1536).rearrange("p (n d) -> p n d", d=96),
                             vw(o_k_, 768, BF16).rearrange("p (t n) -> p t n", t=4),
                             [vw(o_s_ + 256 * k_, 256) for k_ in range(4)],
                             o_w_))

            load_w(WQa, w_in_v[:, :, CQ[0]:CQ[1]], "wqa", "WQa")
            load_w(wuq, w_uq.rearrange("(k p) n -> p k n", p=128), "wuq", "wuq")
            load_w(wukv, w_ukv, "wukv", "wukv")
            kr = vw(o_small + 130, 8)
            kri = vw(o_small + 140, 8, I32)
            E("pool", "iota", [], ["kri"], kri, pattern=[[128, 8]], base=0, channel_multiplier=1)
            E("dve", "tensor_copy", ["kri"], ["kr"], out=kr, in_=kri)
            for m in range(8):
                E("dve", "tensor_scalar", ["qrel", "kr"], ["cmask"], out=cmask[:, m, :], in0=qrel, scalar1=kr[:, m:m + 1], scalar2=None,
                  op0=ALU.is_ge)

            QTa2 = [QTa, vw(o_rta1, 1024, BF16).rearrange("p (h q) -> p h q", h=4)]

            def kexp_gen(hg, par):
                W3_, W4_, sq_, Kb_, rt_, o_w = KSET[par]
                bA = (4, 5) if par == 0 else (6, 7)
                bT = (0, 1) if par == 0 else (2, 3)
                sfx = "k%d" % par
                Wkey = "Wk%d" % par
                for c in range(par, 8, 2):
                    for t in range(4):
                        T_ = 4 * c + t
                        b = bA[t % 2]
                        MM(PS[b][:], ckvT[:, T_ * 128:(T_ + 1) * 128], wukv[:, hg * 512:(hg + 1) * 512], [("ckvT", c), "wukv"], [psk(b)])
                        pv = PS[b][:].rearrange("p (h d) -> p h d", h=4)
                        E("act", "activation", [psk(b), ("rkv", c)], [Wkey], out=W4_[:, t, :, 0:64], in_=pv[:, :, 0:64], func=AF.Identity,
                          scale=rkv[:, T_:T_ + 1])
                        E("dve", "tensor_scalar", [psk(b), ("rkv", c)], ["Va"], out=Va4[:, T_, :, 0:64], in0=pv[:, :, 64:128],
                          scalar1=rkv[:, T_:T_ + 1], scalar2=None, op0=ALU.mult)
                    E("pool", "tensor_copy", [("kpe", c), Wkey], [Wkey], out=W4_[:, :, :, 64:96],
                      in_=kpe[:, c * 4:(c + 1) * 4, :].unsqueeze(2).to_broadcast([128, 4, 4, 32]))
                    yield 5.0
                    ssv_ = ssn[:, par * 16:par * 16 + 16]
                    rsv_ = rsn[:, par * 16:par * 16 + 16]
                    sq64 = vw(KSQ[par], 1024).rearrange("p (n d) -> p n d", d=64)
                    E("act", "activation", [Wkey], ["sq" + sfx], out=sq64, in_=W3_[:, :, 0:64], func=AF.Square)
                    E("dve", "tensor_reduce", ["sq" + sfx], ["ssn" + sfx], out=ssv_, in_=sq64, axis=AX.X, op=ALU.add)
                    E("dve", "tensor_tensor", ["ssn" + sfx, ("kss", c)], ["ssn" + sfx], out=ssv_.rearrange("p (t h) -> p t h", t=4),
                      in0=ssv_.rearrange("p (t h) -> p t h", t=4), in1=kss[:, c * 4:(c + 1) * 4].unsqueeze(2).to_broadcast([128, 4, 4]),
                      op=ALU.add)
                    E("act", "activation", ["ssn" + sfx], ["rsn" + sfx], out=rsv_, in_=ssv_, func=AF.Ln, scale=1.0 / 96, bias=EPS)
                    E("act", "activation", ["rsn" + sfx], ["rsn" + sfx], out=rsv_, in_=rsv_, func=AF.Exp, scale=-0.5)
                    E("dve", "tensor_tensor", [Wkey, "rsn" + sfx], [Wkey], out=W3_, in0=W3_, in1=rsv_.unsqueeze(2).to_broadcast([128, 16, 96]),
                      op=ALU.mult)
                    E("dve", "tensor_tensor", [Wkey, "gk"], [Wkey], out=W3_[:, :, 0:64], in0=W3_[:, :, 0:64],
                      in1=gk[:, 0:64].unsqueeze(1).to_broadcast([128, 16, 64]), op=ALU.mult)
                    yield 10.0
                    E("act", "activation", [Wkey], ["Kb16" + sfx], out=Kb_, in_=vw(o_w, 1536).rearrange("p (t n) -> p t n", t=4), func=AF.Copy)
                    for hl in range(4):
                        b = bT[hl % 2]
                        for t in range(4):
                            MM(PS[b][0:96, t * 128:(t + 1) * 128], Kb_[:, t, hl * 96:(hl + 1) * 96], ident, ["Kb16" + sfx, "ident"], [psk(b)])
                        if hl % 2 == 0:
                            E("act", "activation", [psk(b)], [("KTa", c)], out=KTa[0:96, hl, c * 512:(c + 1) * 512], in_=PS[b][0:96, :],
                              func=AF.Copy)
                        else:
                            E("dve", "tensor_copy", [psk(b)], [("KTa", c)], out=KTa[0:96, hl, c * 512:(c + 1) * 512], in_=PS[b][0:96, :])
                    yield 5.0

            XS3_ALIAS = ["sq", "Qa"] + [("rt", k) for k in range(4)]

            def start3_gen(hg, j):
                QTb_ = QTa2[j % 2]
                make_hT(xo[j * 512:(j + 1) * 512, :], xs3, "xs3", hb, hT, junk, "xs3", tb=(4, 5), alias_keys=XS3_ALIAS, do_load=False)
                yield 14.0
                for f in range(2):
                    for kc in range(8):
                        MM(PS[6 + f][:], WQa[:, kc, f * 128:(f + 1) * 128], hT[:, kc, :], ["WQa", ("hT", kc)], [psk(6 + f)],
                           start=(kc == 0), stop=(kc == 7))
                    E("dve", "tensor_scalar", [psk(6 + f), "gqn"], [("cqT", f)], out=cqT[:, f, :], in0=PS[6 + f][:],
                      scalar1=gqn[:, f:f + 1], scalar2=None, op0=ALU.mult)
                    E("act", "activation", [psk(6 + f)], [("sqb2", f)], out=sqb2[:, f, :], in_=PS[6 + f][:], func=AF.Square)
                yield 6.0
                for t in range(4):
                    for f in range(2):
                        MM(PS[6][:, 2 * t:2 * t + 2], sqb2[:, f, t * 128:(t + 1) * 128], ones_b[:, 0:2], [("sqb2", f), "ones_b"], [psk(6)],
                           start=(f == 0), stop=(f == 1))
                E("act", "activation", [psk(6)], ["rq4"], out=rq4, in_=PS[6][:, 0:8].rearrange("p (t two) -> p t two", two=2)[:, :, 0],
                  func=AF.Ln, scale=1.0 / 256, bias=EPS)
                E("act", "activation", ["rq4"], ["rq4"], out=rq4, in_=rq4, func=AF.Exp, scale=-0.5)
                for t in range(4):
                    b = 6 + (t % 2)
                    for f in range(2):
                        MM(PS[b][:, 0:384], cqT[:, f, t * 128:(t + 1) * 128], wuq[:, f, hg * 384:(hg + 1) * 384],
                           [("cqT", f), "wuq"], [psk(b)], start=(f == 0), stop=(f == 1))
                    E("act", "activation", [psk(b), "rq4"], ["Wqa"], out=Wqat[:, t, :], in_=PS[b][:, 0:384], func=AF.Identity,
                      scale=rq4[:, t:t + 1])
                yield 6.0
                head_prep(Wqa3, Wqa4, 16, 96, "Wqa", sqa3, ssn[:, 0:16], rsn[:, 0:16], gq, "gq", (64, 32),
                          cosO[:, j * 4:(j + 1) * 4, 0:16], sinO[:, j * 4:(j + 1) * 4, 0:16], ROPE_O, 4, 4, rta)
                yield 16.0
                E("act", "activation", ["Wqa", "sq"], ["Qa"], out=Qa, in_=Wqat, func=AF.Copy)
                for hl in range(4):
                    b = 4 + hl % 2
                    for t in range(4):
                        MM(PS[b][0:96, t * 128:(t + 1) * 128], Qa[:, t, hl * 96:(hl + 1) * 96], ident, ["Qa", "ident"], [psk(b)])
                    if hl % 2 == 0:
                        E("act", "activation", [psk(b)], [("QTa", j % 2, hl)], out=QTb_[0:96, hl, :], in_=PS[b][0:96, :], func=AF.Copy)
                    else:
                        E("dve", "tensor_copy", [psk(b)], [("QTa", j % 2, hl)], out=QTb_[0:96, hl, :], in_=PS[b][0:96, :])
                if j + 1 < 4:
                    load_x(xo[(j + 1) * 512:(j + 2) * 512, :], xs3, "xs3", "xs3", XS3_ALIAS)
                yield 6.0

            def attn3_gen(hg, j, sb):
                QTb_ = QTa2[j % 2]
                nsb = len(sb)
                la = nsb - 1
                for hl in range(4):
                    g = hg * 4 + hl
                    steps = [(kt, col_lo(max(kt - 8 * j, 0))) for kt in range(8 * j + 8)]
                    ob = 2 if nsb == 2 else 4

                    def Sstep(s):
                        kt, cl = steps[s]
                        bk = sb[s % nsb]
                        MM(PS[bk][:, cl:512], KTa[0:96, hl, kt * 128:(kt + 1) * 128], QTb_[0:96, hl, cl:512],
                           KTaK + [("QTa", j % 2, hl)], [psk(bk)])

                    def PVstep(s2):
                        kt2, cl2 = steps[s2]
                        MM(PS[ob][0:65, cl2:512], Va[:, kt2, hl * 65:(hl + 1) * 65], PT[s2 % 4][:, cl2:512], [("PT", s2 % 4), "Va"],
                           [psk(ob)], start=(s2 == 0), stop=(s2 == len(steps) - 1))

                    for s0 in range(min(la, len(steps))):
                        Sstep(s0)
                    for s, (kt, cl) in enumerate(steps):
                        if s + la < len(steps):
                            Sstep(s + la)
                        bk = sb[s % nsb]
                        pt = PT[s % 4]
                        pk = ("PT", s % 4)
                        E("act", "activation", [psk(bk)], [pk], out=pt[:, cl:512], in_=PS[bk][:, cl:512], func=AF.Exp,
                          scale=float(96.0 ** -0.5))
                        m = kt - 8 * j
                        if m >= 0:
                            E("dve" if s % 3 != 2 else "pool", "tensor_tensor", [pk, "cmask"], [pk], out=pt[:, cl:512],
                              in0=pt[:, cl:512], in1=cmask[:, m, cl:512], op=ALU.mult)
                        if s >= 2:
                            PVstep(s - 2)
                        yield 0.9
                    for s2 in range(max(0, len(steps) - 2), len(steps)):
                        PVstep(s2)
                    finalize(ob, OTa[(g % 2) * 64:(g % 2) * 64 + 64, g // 2, j * 512:(j + 1) * 512], ("OTa", j, g), rd, bcs, 3 if nsb == 2 else 6)
                    yield 3.0

            def run_gen3(gen):
                for _ in gen:
                    pass

            def interleave3(ga, gb):
                ta = tb = 0.0
                ea = eb = False
                while not (ea and eb):
                    if not ea and (eb or ta <= tb):
                        try:
                            ta += next(ga) or 1.0
                        except StopIteration:
                            ea = True
                    else:
                        try:
                            tb += next(gb) or 1.0
                        except StopIteration:
                            eb = True

            WGp = vw(o_cosA, 4096, BF16).rearrange("p (k n) -> p k n", k=8)
            assert o_ckvT == o_cosA + 2688 and o_kpe >= o_cosA + 4096
            KTaK = [("KTa", c) for c in range(8)]
            for hg in range(2):
                interleave3(kexp_gen(hg, 0), kexp_gen(hg, 1))
                if hg == 0:
                    E("pool", "memset", [], ["Va"], Va4[:, :, :, 64:65], 1.0)
                    tap("KTa", vw(o_KTa, 8192, BF16)[0:96, 0:4096], [96, 4096], KTaK, BF16)
                    tap("Va", vw(o_Va, 4160, BF16), [128, 8320], ["Va"], BF16)
                p.barrier()
                load_x(xo[0:512, :], xs3, "xs3", "xs3", XS3_ALIAS)
                run_gen3(start3_gen(hg, 0))
                for j in range(4):
                    if j + 1 < 4:
                        interleave3(attn3_gen(hg, j, [0, 1]), start3_gen(hg, j + 1))
                    else:
                        run_gen3(attn3_gen(hg, j, [0, 1, 2]))
                if hg == 1 and stop_after >= 4:
                    dead_keys = ["tabO"] + [("ckvT", c_) for c_ in range(8)]
                    DMA("pool", WGp[:, :, 0:512], w_in_v[:, :, GA[0]:GA[1]], "wg0", [], [("WG", 0)] + dead_keys)
                    DMA("pool", WGp[:, :, 512:1024], w_in_v[:, :, GB[0]:GB[1]], "wg1", [], [("WG", 1)] + dead_keys)
                p.barrier()
            tap("OTa", vw(o_OTa, 4096, BF16), [128, 8192], [("OTa", j, g) for j in range(4) for g in range(8)], BF16)

        if stop_after >= 4:
            ar = Bump(ARENA, NW)
            WG = vw(o_cosA, 4096, BF16).rearrange("p (k n) -> p k n", k=8)
            o_WM = ar(8192); WM = vw(o_WM, 8192, BF16).rearrange("p (k n) -> p k n", k=8)
            o_WBA = ar(2048); WBA = vw(o_WBA, 2048, BF16).rearrange("p (k n) -> p k n", k=4)
            o_WBD = ar(2048); WBD = vw(o_WBD, 2048, BF16).rearrange("p (k n) -> p k n", k=4)
            o_WO = ar(4096); WO = vw(o_WO, 4096, BF16).rearrange("p (k n) -> p k n", k=8)
            o_xs4 = ar(4096); xs4 = vw(o_xs4, 4096).rearrange("p (t d) -> p t d", t=4)
            o_hb = ar(2048); hb = vw(o_hb, 2048, BF16).rearrange("p (t d) -> p t d", t=4)
            o_junk = ar(512); junk = vw(o_junk, 512, BF16)
            sgt = [vw(ar(256), 256, BF16) for _ in range(2)]
            o_og = ar(2048); og = vw(o_og, 2048, BF16).rearrange("p (c q) -> p c q", c=8)
            o_sig = ar(512); sig = vw(o_sig, 512, BF16).rearrange("p (a q) -> p a q", a=2)
            tmpf = [vw(ar(512), 512) for _ in range(2)]
            yo = [vw(ar(1024), 1024) for _ in range(2)]
            o_hT = ar(2048); hT = vw(o_hT, 2048, BF16).rearrange("p (k n) -> p k n", k=8)
            o_mg = ar(2048); mg = vw(o_mg, 2048, BF16).rearrange("p (f q) -> p f q", f=8)

            load_w(WM[:, :, 0:1024], w_in_v[:, :, MA[0]:MA[1]], "wm0", ("WM", 0))
            load_w(WM[:, :, 1024:2048], w_in_v[:, :, MB[0]:MB[1]], "wm1", ("WM", 1))
            load_w(WBA, wba_d.rearrange("(k p) n -> p k n", p=128), "wba", "WBA")
            load_w(WBD, wbd_d.rearrange("(k p) n -> p k n", p=128), "wbd", "WBD")
            load_w(WO, wo_d.rearrange("(k p) n -> p k n", p=128), "wo", "WO")
            nst = 0
            for j in range(4):
                make_hT(xo[j * 512:(j + 1) * 512, :], xs4, "xs4", hb, hT, junk, "xs4")
                idx = 0
                for br in range(2):
                    OT = OTa if br == 0 else OTb
                    otk = [("OTa" if br == 0 else "OTb", j, g) for g in range(8)]
                    for f in range(4):
                        b = 4 + (idx % 2)
                        for kc in range(8):
                            MM(PS[b][:], WG[:, kc, br * 512 + f * 128:br * 512 + (f + 1) * 128], hT[:, kc, :],
                               [("WG", br), ("hT", kc)], [psk(b)], start=(kc == 0), stop=(kc == 7))
                        E("act", "activation", [psk(b)], [("sg", idx % 2)], out=sgt[idx % 2], in_=PS[b][:], func=AF.Silu)
                        E("dve" if idx % 2 == 0 else "pool", "tensor_tensor", [("sg", idx % 2)] + otk, [("og", br * 4 + f)],
                          out=og[:, br * 4 + f, :], in0=OT[:, f, j * 512:(j + 1) * 512], in1=sgt[idx % 2], op=ALU.mult)
                        idx += 1
                for f in range(8):
                    for a in range(2):
                        for kc in range(8):
                            MM(PS[4 + a][:], WM[:, kc, a * 1024 + f * 128:a * 1024 + (f + 1) * 128], hT[:, kc, :],
                               [("WM", a), ("hT", kc)], [psk(4 + a)], start=(kc == 0), stop=(kc == 7))
                        E("act", "activation", [psk(4 + a), "bm"], [("sig", a)], out=sig[:, a, :], in_=PS[4 + a][:], func=AF.Sigmoid,
                          bias=bm[:, a * 8 + f:a * 8 + f + 1])
                    for c4 in range(4):
                        MM(PS[6][:], WBA[:, c4, f * 128:(f + 1) * 128], og[:, c4, :], ["WBA", ("og", c4)], [psk(6)],
                           start=(c4 == 0), stop=(c4 == 3))
                    for c4 in range(4):
                        MM(PS[7][:], WBD[:, c4, f * 128:(f + 1) * 128], og[:, 4 + c4, :], ["WBD", ("og", 4 + c4)], [psk(7)],
                           start=(c4 == 0), stop=(c4 == 3))
                    E("dve", "tensor_tensor", [psk(6), ("sig", 0)], [("tmpf", 0)], out=tmpf[0], in0=PS[6][:], in1=sig[:, 0, :], op=ALU.mult)
                    E("dve", "tensor_tensor", [psk(7), ("sig", 1)], [("tmpf", 1)], out=tmpf[1], in0=PS[7][:], in1=sig[:, 1, :], op=ALU.mult)
                    E("pool", "tensor_tensor", [("tmpf", 0), ("tmpf", 1)], [("mg", f)], out=mg[:, f, :], in0=tmpf[0], in1=tmpf[1], op=ALU.add)
                for t in range(4):
                    sl = nst % 2
                    for hf in range(2):
                        b = hf
                        for f in range(8):
                            MM(PS[b][:], mg[:, f, t * 128:(t + 1) * 128], WO[:, f, hf * 512:(hf + 1) * 512], [("mg", f), "WO"], [psk(b)],
                               start=(f == 0), stop=(f == 7))
                        E("dve", "tensor_tensor", [psk(b), "xs4"], [("yo", sl)], out=yo[sl][:, hf * 512:(hf + 1) * 512], in0=PS[b][:],
                          in1=xs4[:, t, hf * 512:(hf + 1) * 512], op=ALU.add)
                    u = 4 * j + t
                    DMA("sp", y[u * 128:(u + 1) * 128, :], yo[sl], "st%d" % sl, [("yo", sl)], [])
                    nst += 1

        p.barrier()
        p.op("sp", None)
        p.emit(st)
    return nc, tap_out


def _prep_inputs(inputs):
    f32 = np.float32
    x = np.ascontiguousarray(inputs["x"], dtype=f32)
    pos = np.ascontiguousarray(inputs["positions"]).astype(np.int32)

    def pl(v, k):
        return np.ascontiguousarray(np.asarray(v, dtype=f32).reshape(k, 128).T)

    def rep(v):
        v = np.asarray(v, dtype=f32).reshape(1, -1)
        return np.ascontiguousarray(np.repeat(v, 128, axis=0))

    shared = {
        "ng": pl(inputs["norm_gain"][0], 8),
        "w_in": np.ascontiguousarray(inputs["w_in"][0], dtype=f32),
        "bm": np.ascontiguousarray(np.concatenate([pl(inputs["b_merge"][0, 0], 8), pl(inputs["b_merge"][0, 1], 8)], axis=1)),
        "gqn": pl(inputs["mla_q_norm"][0], 2),
        "w_uq": np.ascontiguousarray(inputs["mla_w_uq"][0], dtype=f32),
        "gkvn": pl(inputs["mla_kv_norm"][0], 1),
        "w_ukv": np.ascontiguousarray(inputs["mla_w_ukv"][0], dtype=f32),
        "gq": rep(inputs["mla_q_gain"][0]), "gk": rep(inputs["mla_k_gain"][0]),
        "gqd": rep(inputs["dsa_q_gain"][0]), "gkd": rep(inputs["dsa_k_gain"][0]),
        "wba": np.ascontiguousarray(inputs["w_branch_mla"][0], dtype=f32),
        "wbd": np.ascontiguousarray(inputs["w_branch_dsa"][0], dtype=f32),
        "wo": np.ascontiguousarray(inputs["w_out"][0], dtype=f32),
    }
    in_maps = []
    own_tiles = []
    for core in range(8):
        b, h = core // 2, core % 2
        tiles = [8 * j + 2 * i + h for j in range(4) for i in range(4)]
        own_tiles.append(tiles)
        xb = x[b]
        xo = np.ascontiguousarray(np.concatenate([xb[t * 128:(t + 1) * 128] for t in tiles], axis=0))
        pb = pos[b]
        posa = np.ascontiguousarray(pb.reshape(32, 128).T)
        poso = np.ascontiguousarray(np.stack([pb[t * 128:(t + 1) * 128] for t in tiles], axis=1))
        qrel = np.concatenate([(2 * i + h) * 128 + np.arange(128) for i in range(4)]).astype(f32)
        m = dict(shared)
        m.update({
            "xa": np.ascontiguousarray(xb), "xo": xo, "posa": posa.astype(np.int32), "poso": poso.astype(np.int32),
            "qrel": rep(qrel), "qrel2": (h * 128 + np.arange(128, dtype=f32)).reshape(128, 1).astype(f32),
        })
        in_maps.append(m)
    return in_maps, own_tiles


_CACHE = {}


def kernel(**inputs):
    in_maps, own_tiles = _prep_inputs(inputs)
    if "nc" not in _CACHE:
        _CACHE["nc"] = build_program()[0]
    nc = _CACHE["nc"]
    res = run_bass_kernel_spmd(nc, in_maps, core_ids=list(range(8)))
    out = np.empty((4, 4096, 1024), dtype=np.float32)
    for core in range(8):
        b = core // 2
        yv = np.asarray(res.results[core]["y"], dtype=np.float32)
        for u, t in enumerate(own_tiles[core]):
            out[b, t * 128:(t + 1) * 128, :] = yv[u * 128:(u + 1) * 128, :]
    return out
```
